# Optimizing a Trainium2 kernel written in Bass

```python
import jax
import jax.numpy as jnp
from jax import lax
import numpy as np

D_MODEL = 2048
BATCH = 4
SEQ = 2048
DEPTH = 2
DEC_BATCH = 128
DEC_SEQ = 1
PAST_LEN = 2048
PAGE_SIZE = 128

D_FF = 5504
LRU_W = D_MODEL // 2
LRU_BLOCKS = 16
LRU_BS = LRU_W // LRU_BLOCKS
CONV_W = 4
LRU_C = 8.0
RWKV_W = D_MODEL // 2
RWKV_HD = 64
RWKV_H = RWKV_W // RWKV_HD
W_LORA = 64
A_LORA = 64
G_LORA = 160
SHIFT_W = 3 * RWKV_W + W_LORA + A_LORA + G_LORA
AB_COLS = 2 * LRU_W + SHIFT_W
NSA_H = 16
NSA_G = 4
NSA_HPG = NSA_H // NSA_G
NSA_HD = 64
NSA_W = NSA_H * NSA_HD
ROPE_DIMS = NSA_HD // 4
ROPE_THETA = 500000.0
CMP_BLOCK = 32
CMP_STRIDE = 16
CMP_R = CMP_BLOCK // CMP_STRIDE
CMP_HID = 256
SEL_BLOCK = 64
SEL_TOP = 16
SEL_Q_BLOCK = 64
WINDOW = 512
WIN_BLOCK = 128
FORCE_SCORE = 1e4
KV_SLOTS = 4
RET_H = 8
RET_DK = 64
RET_DV = 128
RET_W = RET_H * RET_DV
RET_CHUNK = 128
RET_THETA = 10000.0
CD_COLS = NSA_W + 6 * NSA_G * NSA_HD + 3 * NSA_H + 2 * RET_H * RET_DK + 2 * RET_W

kernel_name = 'hybrid_lru_rwkv7_nsa_retention_step'


def rms_norm(x, g, eps=1e-6):
    xf = x.astype(jnp.float32)
    y = xf * lax.rsqrt(jnp.mean(xf * xf, axis=-1, keepdims=True) + eps)
    return (y * g.astype(jnp.float32)).astype(x.dtype)


def head_group_norm(y, g, b, eps):
    yf = y.astype(jnp.float32)
    mu = jnp.mean(yf, axis=-1, keepdims=True)
    var = jnp.mean(jnp.square(yf - mu), axis=-1, keepdims=True)
    yn = ((yf - mu) * lax.rsqrt(var + eps)).reshape(y.shape[:-2] + (-1,))
    return (yn * g.astype(jnp.float32) + b.astype(jnp.float32)).astype(y.dtype)


def swiglu(x, w_in, w_out):
    g, u = jnp.split(x @ w_in, 2, axis=-1)
    return (jax.nn.silu(g) * u) @ w_out


def masked_softmax(s, mask):
    s = jnp.where(mask, s.astype(jnp.float32), -jnp.inf)
    m = jnp.max(s, axis=-1, keepdims=True)
    e = jnp.exp(s - jnp.where(jnp.isfinite(m), m, 0.0))
    den = jnp.sum(e, axis=-1, keepdims=True)
    return e / jnp.where(den > 0, den, 1.0)


def rope(x, pos, n_rot, theta):
    half = n_rot // 2
    inv = jnp.exp(-jnp.log(jnp.float32(theta)) * jnp.arange(half, dtype=jnp.float32) / half)
    ang = pos.astype(jnp.float32)[:, None] * inv[None, :]
    cos = jnp.cos(ang)[None, :, None, :]
    sin = jnp.sin(ang)[None, :, None, :]
    xf = x.astype(jnp.float32)
    x1, x2 = xf[..., :half], xf[..., half:n_rot]
    out = jnp.concatenate([x1 * cos - x2 * sin, x2 * cos + x1 * sin, xf[..., n_rot:]], axis=-1)
    return out.astype(x.dtype)


def linear_scan(a, b, h0):
    b = b.at[:, 0].add(a[:, 0] * h0)

    def combine(left, right):
        return left[0] * right[0], right[0] * left[1] + right[1]

    return lax.associative_scan(combine, (a, b), axis=1)[1]


def wkv7_scan(r, w, k, v, a, b, s0):
    xs = tuple(jnp.moveaxis(z.astype(jnp.float32), 1, 0) for z in (r, w, k, v, a, b))

    def step(S, inp):
        r_t, w_t, k_t, v_t, a_t, b_t = inp
        sa = jnp.einsum('bhij,bhj->bhi', S, a_t)
        S = S * w_t[:, :, None, :] + sa[..., None] * b_t[:, :, None, :] + v_t[..., None] * k_t[:, :, None, :]
        return S, jnp.einsum('bhij,bhj->bhi', S, r_t)

    S, ys = lax.scan(step, s0.astype(jnp.float32), xs)
    return jnp.moveaxis(ys, 0, 1), S


def even_mixer(h, p, lru_h0, lru_conv0, shift0, wkv0):
    B, T, _ = h.shape
    f32 = jnp.float32
    proj = h @ p['w_in']
    xb, gb, rw = jnp.split(proj, [LRU_W, 2 * LRU_W], axis=-1)
    xcat = jnp.concatenate([lru_conv0.astype(h.dtype), xb], axis=1)
    xc = p['conv_b'] + sum(p['conv_w'][j] * xcat[:, j:j + T] for j in range(CONV_W))
    xbd = xc.reshape(B, T, LRU_BLOCKS, LRU_BS)
    gate_r = jax.nn.sigmoid(jnp.einsum('btnc,ncd->btnd', xbd, p['wa']).reshape(B, T, LRU_W) + p['ba'])
    gate_i = jax.nn.sigmoid(jnp.einsum('btnc,ncd->btnd', xbd, p['wx']).reshape(B, T, LRU_W) + p['bx'])
    log_a = -LRU_C * gate_r.astype(f32) * jax.nn.softplus(-p['lam'].astype(f32))
    u = jnp.sqrt(-jnp.expm1(2.0 * log_a)) * (gate_i * xc).astype(f32)
    hs = linear_scan(jnp.exp(log_a), u, lru_h0.astype(f32))
    y_lru = hs.astype(h.dtype) * jax.nn.gelu(gb)
    prev = jnp.concatenate([shift0.astype(h.dtype)[:, None], rw[:, :-1]], axis=1)
    rs = rw + p['mu'] * (prev - rw)
    r, k, v, xw, xa, xg = jnp.split(
        rs, [RWKV_W, 2 * RWKV_W, 3 * RWKV_W, 3 * RWKV_W + W_LORA, 3 * RWKV_W + W_LORA + A_LORA], axis=-1)
    w_log = -jax.nn.softplus(-(p['w0'] + jnp.tanh(xw) @ p['w2']).astype(f32)) - 0.5
    decay = jnp.exp(-jnp.exp(w_log))
    a_icl = jax.nn.sigmoid(p['a0'] + xa @ p['a2'])
    g = jax.nn.sigmoid(xg) @ p['g2']
    heads = (B, T, RWKV_H, RWKV_HD)
    kk = (k * p['k_k']).reshape(heads).astype(f32)
    kk = kk / jnp.maximum(jnp.sqrt(jnp.sum(kk * kk, axis=-1, keepdims=True)), 1e-12)
    k = k * (1.0 + (a_icl - 1.0) * p['k_a'])
    rh, kh, vh, ah = (z.reshape(heads) for z in (r, k, v, a_icl))
    y, wkv = wkv7_scan(rh, decay.reshape(heads), kh, vh, -kk, kk * ah.astype(f32), wkv0)
    y = head_group_norm(y, p['ln_g'], p['ln_b'], 64e-5).astype(h.dtype)
    bonus = (jnp.sum(rh * kh * p['r_k'], axis=-1, keepdims=True) * vh).reshape(B, T, RWKV_W)
    y_rwkv = (y + bonus) * g
    out = jnp.concatenate([y_lru, y_rwkv], axis=-1) @ p['w_out']
    return out, hs[:, -1], xcat[:, T:], rw[:, -1], wkv


def odd_project(h, p, pos):
    B, T, _ = h.shape
    sizes = [NSA_W] + [NSA_G * NSA_HD] * 6 + [3 * NSA_H, RET_H * RET_DK, RET_H * RET_DK, RET_W, RET_W]
    q, kc, vc, ks, vs, kw, vw, gt, rq, rk, rv, rg = jnp.split(
        h @ p['w_in'], np.cumsum(sizes)[:-1].tolist(), axis=-1)
    kvs = (B, T, NSA_G, NSA_HD)
    q_n = rms_norm(q.reshape(B, T, NSA_H, NSA_HD), p['q_norm'])
    return {
        'q_n': q_n,
        'q_r': rope(q_n, pos, ROPE_DIMS, ROPE_THETA),
        'kc': kc.reshape(kvs), 'vc': vc.reshape(kvs),
        'ks': rope(rms_norm(ks.reshape(kvs), p['k_norm'][1]), pos, ROPE_DIMS, ROPE_THETA),
        'vs': vs.reshape(kvs),
        'kw': rope(rms_norm(kw.reshape(kvs), p['k_norm'][2]), pos, ROPE_DIMS, ROPE_THETA),
        'vw': vw.reshape(kvs),
        'gates': jax.nn.sigmoid(gt).reshape(B, T, NSA_H, 3),
        'rq': rope(rq.reshape(B, T, RET_H, RET_DK), pos, RET_DK, RET_THETA),
        'rk': rope(rk.reshape(B, T, RET_H, RET_DK), pos, RET_DK, RET_THETA) * (RET_DK ** -0.5),
        'rv': rv.reshape(B, T, RET_H, RET_DV),
        'rg': rg,
    }


def to_groups_q(q):
    B, T = q.shape[:2]
    return jnp.moveaxis(q.reshape(B, T, NSA_G, NSA_HPG, NSA_HD), 1, 3)


def to_groups_k(k):
    return jnp.moveaxis(k, 1, 2)


def nsa_compress(x, w1, b1, w2, b2):
    B, L = x.shape[:2]
    n_chunk = L // CMP_STRIDE
    n_cmp = n_chunk - CMP_R + 1
    ch = x[:, :n_chunk * CMP_STRIDE].reshape(B, n_chunk, CMP_STRIDE, NSA_G, NSA_HD)
    ch = jnp.moveaxis(ch, 3, 2).reshape(B, n_chunk, NSA_G, CMP_STRIDE * NSA_HD)
    part = jnp.einsum('bngc,rch->bngrh', ch, w1)
    pre = b1 + sum(part[:, m:m + n_cmp, :, m] for m in range(CMP_R))
    return jax.nn.gelu(pre) @ w2 + b2


def nsa_compressed_branch(qn, kc_raw, vc_raw, p, q_pos):
    kc = to_groups_k(rms_norm(nsa_compress(kc_raw, *p['ck']), p['k_norm'][0]))
    vc = to_groups_k(nsa_compress(vc_raw, *p['cv']))
    s = jnp.einsum('bghqd,bgcd->bghqc', qn, kc) * NSA_HD ** -0.5
    ends = jnp.arange(kc.shape[2]) * CMP_STRIDE + CMP_BLOCK - 1
    prob = masked_softmax(s, ends[None, :] <= q_pos[:, None])
    return jnp.einsum('bghqc,bgcd->bghqd', prob.astype(vc.dtype), vc), prob


def cmp_sel_overlap(n_cmp, n_sel):
    cs = np.arange(n_cmp) * CMP_STRIDE
    ss = np.arange(n_sel) * SEL_BLOCK
    ov = np.minimum(cs[None] + CMP_BLOCK, ss[:, None] + SEL_BLOCK) - np.maximum(cs[None], ss[:, None])
    return jnp.asarray(np.clip(ov, 0, None) / CMP_BLOCK, dtype=jnp.float32)


def nsa_select(p_cmp, q_pos, n_sel):
    imp = jnp.einsum('bgqc,sc->bgqs', p_cmp.sum(axis=2), cmp_sel_overlap(p_cmp.shape[-1], n_sel))
    j = jnp.arange(n_sel)[None, :]
    qb = (q_pos // SEL_BLOCK)[:, None]
    valid = j <= qb
    forced = (j == 0) | (j == qb) | (j == qb - 1)
    score = jnp.where(valid, jnp.where(forced, FORCE_SCORE, imp), -jnp.inf)
    _, idx = lax.top_k(score, min(SEL_TOP, n_sel))
    sel_ok = jnp.take_along_axis(jnp.broadcast_to(valid, score.shape), idx, axis=-1)
    return idx, sel_ok


def sel_blocks(x, n_sel):
    B, L = x.shape[:2]
    x = jnp.pad(x, ((0, 0), (0, n_sel * SEL_BLOCK - L), (0, 0), (0, 0)))
    return jnp.moveaxis(x.reshape(B, n_sel, SEL_BLOCK, NSA_G, NSA_HD), 3, 1)


def nsa_slc_attend(q, kb, vb, idx, sel_ok, q_pos):
    B, G = kb.shape[:2]
    bi = jnp.arange(B)[:, None, None, None]
    gi = jnp.arange(G)[None, :, None, None]
    kg = kb[bi, gi, idx]
    vg = vb[bi, gi, idx]
    s = jnp.einsum('bghqd,bgqnld->bghqnl', q, kg) * NSA_HD ** -0.5
    kpos = idx[..., None] * SEL_BLOCK + jnp.arange(SEL_BLOCK)
    mask = (kpos <= q_pos[None, None, :, None, None]) & sel_ok[..., None]
    sh = s.shape
    prob = masked_softmax(s.reshape(sh[:4] + (-1,)), mask.reshape(B, G, 1, sh[3], -1))
    return jnp.einsum('bghqnl,bgqnld->bghqd', prob.reshape(sh).astype(vg.dtype), vg)


def window_attend_banded(q, k, v):
    B, G, HPG, T, HD = q.shape
    nb = T // WIN_BLOCK
    npv = WINDOW // WIN_BLOCK
    pad = ((0, 0), (0, 0), (npv * WIN_BLOCK, 0), (0, 0))

    def band(z):
        zb = jnp.pad(z, pad).reshape(B, G, nb + npv, WIN_BLOCK, HD)
        return jnp.concatenate([zb[:, :, j:j + nb] for j in range(npv + 1)], axis=3)

    kb, vb = band(k), band(v)
    qb = q.reshape(B, G, HPG, nb, WIN_BLOCK, HD)
    s = jnp.einsum('bghiqd,bgikd->bghiqk', qb, kb) * NSA_HD ** -0.5
    blk = jnp.arange(nb)[:, None]
    q_pos = blk * WIN_BLOCK + jnp.arange(WIN_BLOCK)[None]
    k_pos = (blk - npv) * WIN_BLOCK + jnp.arange((npv + 1) * WIN_BLOCK)[None]
    diff = q_pos[:, :, None] - k_pos[:, None, :]
    mask = (diff >= 0) & (diff < WINDOW) & (k_pos[:, None, :] >= 0)
    prob = masked_softmax(s, mask)
    return jnp.einsum('bghiqk,bgikd->bghiqd', prob.astype(v.dtype), vb).reshape(B, G, HPG, T, HD)


def window_attend_cached(q, k, v, q_pos, k_pos):
    s = jnp.einsum('bghqd,blgd->bghql', q, k) * NSA_HD ** -0.5
    diff = q_pos[:, None] - k_pos[None, :]
    prob = masked_softmax(s, (diff >= 0) & (diff < WINDOW))
    return jnp.einsum('bghql,blgd->bghqd', prob.astype(v.dtype), v)


def retention_chunk(S, q, k, v):
    f32 = jnp.float32
    C = q.shape[1]
    lg = jnp.log1p(-jnp.exp2(-5.0 - jnp.arange(RET_H, dtype=f32)))
    i = jnp.arange(C, dtype=f32)
    diff = i[:, None] - i[None, :]
    causal = diff >= 0
    dmask = jnp.where(causal, jnp.exp(jnp.where(causal, diff, 0.0)[None] * lg[:, None, None]), 0.0)
    qf, kf, vf = q.astype(f32), k.astype(f32), v.astype(f32)
    s = jnp.einsum('bihd,bjhd->bhij', qf, kf) * dmask
    o = jnp.einsum('bhij,bjhe->bihe', s, vf)
    o = o + jnp.einsum('bihd,bhde->bihe', qf, S) * jnp.exp((i[:, None] + 1.0) * lg[None, :])[None, :, :, None]
    k_dec = kf * jnp.exp((C - 1.0 - i)[:, None] * lg[None, :])[None, :, :, None]
    S = S * jnp.exp(C * lg)[None, :, None, None] + jnp.einsum('bjhd,bjhe->bhde', k_dec, vf)
    return S, o


def retention_prompt(q, k, v):
    B, T = q.shape[:2]
    n = T // RET_CHUNK
    xs = tuple(jnp.moveaxis(z.reshape((B, n, RET_CHUNK) + z.shape[2:]), 1, 0) for z in (q, k, v))
    s0 = jnp.zeros((B, RET_H, RET_DK, RET_DV), jnp.float32)
    S, o = lax.scan(lambda S, c: retention_chunk(S, c[0], c[1], c[2]), s0, xs)
    return S, jnp.moveaxis(o, 0, 1).reshape(B, T, RET_H, RET_DV)


def odd_output(o_cmp, o_slc, o_win, o_ret, pr, p):
    gates = pr['gates']
    B, T = gates.shape[:2]
    gg = jnp.moveaxis(gates.reshape(B, T, NSA_G, NSA_HPG, 3), 1, 3)[..., None]
    o = gg[..., 0, :] * o_cmp + gg[..., 1, :] * o_slc + gg[..., 2, :] * o_win
    o_nsa = jnp.moveaxis(o, 3, 1).reshape(B, T, NSA_W)
    y_ret = head_group_norm(o_ret, p['gn_g'], p['gn_b'], 1e-5).astype(o_nsa.dtype) * jax.nn.silu(pr['rg'])
    return jnp.concatenate([o_nsa, y_ret], axis=-1) @ p['w_out']


def odd_mixer_prompt(h, p):
    B, T, _ = h.shape
    pos = jnp.arange(T)
    pr = odd_project(h, p, pos)
    qn, qr = to_groups_q(pr['q_n']), to_groups_q(pr['q_r'])
    o_cmp, p_cmp = nsa_compressed_branch(qn, pr['kc'], pr['vc'], p, pos)
    n_sel = -(-T // SEL_BLOCK)
    idx, sel_ok = nsa_select(p_cmp, pos, n_sel)
    kb, vb = sel_blocks(pr['ks'], n_sel), sel_blocks(pr['vs'], n_sel)
    nqb = T // SEL_Q_BLOCK
    k_top = idx.shape[-1]
    q_blk = jnp.moveaxis(qr.reshape(B, NSA_G, NSA_HPG, nqb, SEL_Q_BLOCK, NSA_HD), 3, 0)
    i_blk = jnp.moveaxis(idx.reshape(B, NSA_G, nqb, SEL_Q_BLOCK, k_top), 2, 0)
    m_blk = jnp.moveaxis(sel_ok.reshape(B, NSA_G, nqb, SEL_Q_BLOCK, k_top), 2, 0)
    p_blk = pos.reshape(nqb, SEL_Q_BLOCK)
    o_slc = lax.map(lambda a: nsa_slc_attend(a[0], kb, vb, a[1], a[2], a[3]), (q_blk, i_blk, m_blk, p_blk))
    o_slc = jnp.moveaxis(o_slc, 0, 3).reshape(B, NSA_G, NSA_HPG, T, NSA_HD)
    o_win = window_attend_banded(qr, to_groups_k(pr['kw']), to_groups_k(pr['vw']))
    S, o_ret = retention_prompt(pr['rq'], pr['rk'], pr['rv'])
    out = odd_output(o_cmp, o_slc, o_win, o_ret, pr, p)
    kv_rows = jnp.stack([pr['kc'], pr['vc'], pr['ks'], pr['vs']], axis=2)
    win = jnp.stack([pr['kw'], pr['vw']], axis=2)[:, T - min(WINDOW, T):]
    return out, kv_rows, win, S


def odd_mixer_sample(h, p, past_kv, win_buf, ret_s0):
    B, T, _ = h.shape
    pos = PAST_LEN + jnp.arange(T)
    pr = odd_project(h, p, pos)
    qn, qr = to_groups_q(pr['q_n']), to_groups_q(pr['q_r'])
    rows = jnp.stack([pr['kc'], pr['vc'], pr['ks'], pr['vs']], axis=2).astype(past_kv.dtype)
    full = jnp.concatenate([past_kv, rows], axis=1)
    o_cmp, p_cmp = nsa_compressed_branch(qn, full[:, :, 0], full[:, :, 1], p, pos)
    n_sel = -(-full.shape[1] // SEL_BLOCK)
    idx, sel_ok = nsa_select(p_cmp, pos, n_sel)
    o_slc = nsa_slc_attend(qr, sel_blocks(full[:, :, 2], n_sel), sel_blocks(full[:, :, 3], n_sel),
                           idx, sel_ok, pos)
    nbuf = win_buf.shape[1]
    wfull = jnp.concatenate([win_buf, jnp.stack([pr['kw'], pr['vw']], axis=2).astype(win_buf.dtype)], axis=1)
    k_pos = PAST_LEN - nbuf + jnp.arange(nbuf + T)
    o_win = window_attend_cached(qr, wfull[:, :, 0], wfull[:, :, 1], pos, k_pos)
    S, o_ret = retention_chunk(ret_s0.astype(jnp.float32), pr['rq'], pr['rk'], pr['rv'])
    out = odd_output(o_cmp, o_slc, o_win, o_ret, pr, p)
    return out, rows, wfull[:, T:], S


def _stack(xs, dt):
    return jnp.stack(xs).astype(dt)


def setup_inputs(seed: int = 0) -> dict:
    key = jax.random.key(seed)
    keys = iter(jax.random.split(key, 80))
    f32 = jnp.float32
    ne, no = (DEPTH + 1) // 2, DEPTH // 2
    n_pages = PAST_LEN // PAGE_SIZE
    n_pool = (DEC_BATCH * n_pages * 5) // 4
    win_buf = min(WINDOW, PAST_LEN)

    def nrm(shape, scale=1.0):
        return scale * jax.random.normal(next(keys), shape, f32)

    def gain(shape):
        return 1.0 + nrm(shape, 0.05)

    def unif(shape, lo, hi):
        return jax.random.uniform(next(keys), shape, f32, lo, hi)

    lam_a = unif((ne, LRU_W), 0.9, 0.999)
    perm = jax.random.permutation(next(keys), n_pool)
    return {
        'x_prompt': nrm((BATCH, SEQ, D_MODEL)),
        'x_sample': nrm((DEC_BATCH, DEC_SEQ, D_MODEL)),
        'state_lru_h': nrm((ne, DEC_BATCH, LRU_W), 0.5),
        'state_lru_conv': nrm((ne, DEC_BATCH, CONV_W - 1, LRU_W)),
        'state_rwkv_shift': nrm((ne, DEC_BATCH, SHIFT_W)),
        'state_rwkv_wkv': nrm((ne, DEC_BATCH, RWKV_H, RWKV_HD, RWKV_HD), 0.3),
        'cache_nsa_kv': nrm((no, n_pool, PAGE_SIZE, KV_SLOTS, NSA_G, NSA_HD)),
        'cache_nsa_win': nrm((no, DEC_BATCH, win_buf, 2, NSA_G, NSA_HD)),
        'state_ret': nrm((no, DEC_BATCH, RET_H, RET_DK, RET_DV)),
        'page_table': perm[:DEC_BATCH * n_pages].reshape(DEC_BATCH, n_pages).astype(jnp.int32),
        'norm_ffn1': gain((DEPTH, D_MODEL)),
        'ffn1_w_in': nrm((DEPTH, D_MODEL, 2 * D_FF), D_MODEL ** -0.5),
        'ffn1_w_out': nrm((DEPTH, D_FF, D_MODEL), D_FF ** -0.5),
        'norm_mix': gain((DEPTH, D_MODEL)),
        'norm_ffn2': gain((DEPTH, D_MODEL)),
        'ffn2_w_in': nrm((DEPTH, D_MODEL, 2 * D_FF), D_MODEL ** -0.5),
        'ffn2_w_out': nrm((DEPTH, D_FF, D_MODEL), D_FF ** -0.5),
        'ab_w_in': nrm((ne, D_MODEL, AB_COLS), D_MODEL ** -0.5),
        'lru_conv_w': nrm((ne, CONV_W, LRU_W), CONV_W ** -0.5),
        'lru_conv_b': nrm((ne, LRU_W), 0.01),
        'lru_wa': nrm((ne, LRU_BLOCKS, LRU_BS, LRU_BS), LRU_BS ** -0.5),
        'lru_ba': nrm((ne, LRU_W), 0.01),
        'lru_wx': nrm((ne, LRU_BLOCKS, LRU_BS, LRU_BS), LRU_BS ** -0.5),
        'lru_bx': nrm((ne, LRU_W), 0.01),
        'lru_lambda': jnp.log(lam_a) - jnp.log1p(-lam_a),
        'rwkv_mu': unif((ne, SHIFT_W), 0.0, 1.0),
        'rwkv_w0': unif((ne, RWKV_W), -6.0, 1.0),
        'rwkv_w2': nrm((ne, W_LORA, RWKV_W), 0.1 * W_LORA ** -0.5),
        'rwkv_a0': nrm((ne, RWKV_W), 0.1),
        'rwkv_a2': nrm((ne, A_LORA, RWKV_W), 0.1 * A_LORA ** -0.5),
        'rwkv_g2': nrm((ne, G_LORA, RWKV_W), G_LORA ** -0.5),
        'rwkv_k_k': 0.85 + nrm((ne, RWKV_W), 0.05),
        'rwkv_k_a': gain((ne, RWKV_W)),
        'rwkv_r_k': nrm((ne, RWKV_H, RWKV_HD), 0.1),
        'rwkv_ln_g': gain((ne, RWKV_W)),
        'rwkv_ln_b': nrm((ne, RWKV_W), 0.01),
        'ab_w_out': nrm((ne, LRU_W + RWKV_W, D_MODEL), (LRU_W + RWKV_W) ** -0.5),
        'cd_w_in': nrm((no, D_MODEL, CD_COLS), D_MODEL ** -0.5),
        'nsa_q_norm': gain((no, NSA_HD)),
        'nsa_k_norm': gain((no, 3, NSA_HD)),
        'cmp_k_w1': nrm((no, CMP_R, CMP_STRIDE * NSA_HD, CMP_HID), (CMP_BLOCK * NSA_HD) ** -0.5),
        'cmp_k_b1': nrm((no, CMP_HID), 0.01),
        'cmp_k_w2': nrm((no, CMP_HID, NSA_HD), CMP_HID ** -0.5),
        'cmp_k_b2': nrm((no, NSA_HD), 0.01),
        'cmp_v_w1': nrm((no, CMP_R, CMP_STRIDE * NSA_HD, CMP_HID), (CMP_BLOCK * NSA_HD) ** -0.5),
        'cmp_v_b1': nrm((no, CMP_HID), 0.01),
        'cmp_v_w2': nrm((no, CMP_HID, NSA_HD), CMP_HID ** -0.5),
        'cmp_v_b2': nrm((no, NSA_HD), 0.01),
        'ret_gn_g': gain((no, RET_W)),
        'ret_gn_b': nrm((no, RET_W), 0.01),
        'cd_w_out': nrm((no, NSA_W + RET_W, D_MODEL), (NSA_W + RET_W) ** -0.5),
    }


def reference(x_prompt, x_sample, state_lru_h, state_lru_conv, state_rwkv_shift, state_rwkv_wkv,
              cache_nsa_kv, cache_nsa_win, state_ret, page_table,
              norm_ffn1, ffn1_w_in, ffn1_w_out, norm_mix, norm_ffn2, ffn2_w_in, ffn2_w_out,
              ab_w_in, lru_conv_w, lru_conv_b, lru_wa, lru_ba, lru_wx, lru_bx, lru_lambda,
              rwkv_mu, rwkv_w0, rwkv_w2, rwkv_a0, rwkv_a2, rwkv_g2, rwkv_k_k, rwkv_k_a, rwkv_r_k,
              rwkv_ln_g, rwkv_ln_b, ab_w_out,
              cd_w_in, nsa_q_norm, nsa_k_norm, cmp_k_w1, cmp_k_b1, cmp_k_w2, cmp_k_b2,
              cmp_v_w1, cmp_v_b1, cmp_v_w2, cmp_v_b2, ret_gn_g, ret_gn_b, cd_w_out):
    dt = x_prompt.dtype
    B = x_prompt.shape[0]
    DB = x_sample.shape[0]
    yp, ys = x_prompt, x_sample
    lru_h_p, lru_h_s, lru_c_p, lru_c_s, sh_p, sh_s, wkv_p, wkv_s = [], [], [], [], [], [], [], []
    kv_p, kv_s, win_p, win_s, ret_p, ret_s = [], [], [], [], [], []
    for layer in range(DEPTH):
        li = layer // 2
        yp = yp + 0.5 * swiglu(rms_norm(yp, norm_ffn1[layer]), ffn1_w_in[layer], ffn1_w_out[layer])
        ys = ys + 0.5 * swiglu(rms_norm(ys, norm_ffn1[layer]), ffn1_w_in[layer], ffn1_w_out[layer])
        hp = rms_norm(yp, norm_mix[layer])
        hs = rms_norm(ys, norm_mix[layer])
        if layer % 2 == 0:
            p = {'w_in': ab_w_in[li], 'conv_w': lru_conv_w[li], 'conv_b': lru_conv_b[li],
                 'wa': lru_wa[li], 'ba': lru_ba[li], 'wx': lru_wx[li], 'bx': lru_bx[li], 'lam': lru_lambda[li],
                 'mu': rwkv_mu[li], 'w0': rwkv_w0[li], 'w2': rwkv_w2[li], 'a0': rwkv_a0[li], 'a2': rwkv_a2[li],
                 'g2': rwkv_g2[li], 'k_k': rwkv_k_k[li], 'k_a': rwkv_k_a[li], 'r_k': rwkv_r_k[li],
                 'ln_g': rwkv_ln_g[li], 'ln_b': rwkv_ln_b[li], 'w_out': ab_w_out[li]}
            mp, a0, a1, a2, a3 = even_mixer(hp, p, jnp.zeros((B, LRU_W), dt), jnp.zeros((B, CONV_W - 1, LRU_W), dt),
                                            jnp.zeros((B, SHIFT_W), dt), jnp.zeros((B, RWKV_H, RWKV_HD, RWKV_HD), dt))
            ms, b0, b1, b2, b3 = even_mixer(hs, p, state_lru_h[li], state_lru_conv[li],
                                            state_rwkv_shift[li], state_rwkv_wkv[li])
            lru_h_p.append(a0); lru_c_p.append(a1); sh_p.append(a2); wkv_p.append(a3)
            lru_h_s.append(b0); lru_c_s.append(b1); sh_s.append(b2); wkv_s.append(b3)
        else:
            p = {'w_in': cd_w_in[li], 'q_norm': nsa_q_norm[li], 'k_norm': nsa_k_norm[li],
                 'ck': (cmp_k_w1[li], cmp_k_b1[li], cmp_k_w2[li], cmp_k_b2[li]),
                 'cv': (cmp_v_w1[li], cmp_v_b1[li], cmp_v_w2[li], cmp_v_b2[li]),
                 'gn_g': ret_gn_g[li], 'gn_b': ret_gn_b[li], 'w_out': cd_w_out[li]}
            n_pages = page_table.shape[1]
            past = cache_nsa_kv[li][page_table].reshape(DB, n_pages * PAGE_SIZE, KV_SLOTS, NSA_G, NSA_HD)
            mp, a0, a1, a2 = odd_mixer_prompt(hp, p)
            ms, b0, b1, b2 = odd_mixer_sample(hs, p, past, cache_nsa_win[li], state_ret[li])
            kv_p.append(a0); win_p.append(a1); ret_p.append(a2)
            kv_s.append(b0); win_s.append(b1); ret_s.append(b2)
        yp = yp + mp
        ys = ys + ms
        yp = yp + 0.5 * swiglu(rms_norm(yp, norm_ffn2[layer]), ffn2_w_in[layer], ffn2_w_out[layer])
        ys = ys + 0.5 * swiglu(rms_norm(ys, norm_ffn2[layer]), ffn2_w_in[layer], ffn2_w_out[layer])
    return (yp, ys,
            _stack(lru_h_p, dt), _stack(lru_h_s, dt), _stack(lru_c_p, dt), _stack(lru_c_s, dt),
            _stack(sh_p, dt), _stack(sh_s, dt), _stack(wkv_p, dt), _stack(wkv_s, dt),
            _stack(kv_p, dt), _stack(kv_s, dt), _stack(win_p, dt), _stack(win_s, dt),
            _stack(ret_p, dt), _stack(ret_s, dt))
```

```python
import math
import numpy as np
from contextlib import ExitStack
import concourse.bass as bass
import concourse.mybir as mybir
from concourse.bass_utils import run_bass_kernel_spmd

F32 = mybir.dt.float32
BF16 = mybir.dt.bfloat16
I32 = mybir.dt.int32
AF = mybir.ActivationFunctionType
ALU = mybir.AluOpType
AX = mybir.AxisListType

SAME_ENGINE_SYNC = True
SEM_ROTATE = 30000
N_DMA_SLOTS = 8


class Buf:
    __slots__ = ("name", "last_w", "readers", "t")

    def __init__(self, name, t=None):
        self.name = name
        self.last_w = None
        self.readers = []
        self.t = t

    def __getitem__(self, k):
        return self.t[k]


class Eng:
    def __init__(self, name):
        self.name = name
        self.ops = []
        self.known = {}
        self.cnt = 0
        self.sem_id = None
        self.slots = []
        self.slot_next = 0
        self.pending = False


def _compact(readers):
    m = {}
    for s, v in readers:
        m[s] = max(m.get(s, 0), v)
    return list(m.items())


class FW:
    def __init__(self, nc):
        self.nc = nc
        self.es = ExitStack()
        self.engs = {n: Eng(n) for n in ("sync", "scalar", "gpsimd", "vector", "tensor")}
        self.sems = {}
        self.nsem = 0
        self.nbuf = 0
        for e in self.engs.values():
            self._new_eng_sem(e)
        for qn in ("sync", "scalar", "gpsimd"):
            q = self.engs[qn]
            for i in range(N_DMA_SLOTS):
                q.slots.append([self._alloc_sem(f"d_{qn}_{i}"), 0])

    def _alloc_sem(self, name):
        h = self.es.enter_context(self.nc.semaphore(f"{name}_{self.nsem}"))
        sid = self.nsem
        self.nsem += 1
        self.sems[sid] = h
        return sid

    def _new_eng_sem(self, e):
        e.sem_id = self._alloc_sem(f"e_{e.name}")
        e.cnt = 0

    def sb(self, name, shape, dtype=F32, es=None):
        t = (es or self.es).enter_context(self.nc.sbuf_tensor(f"{name}_{self.nbuf}", list(shape), dtype))
        self.nbuf += 1
        return Buf(name, t)

    def ps(self, name, shape, dtype=F32):
        t = self.es.enter_context(self.nc.psum_tensor(f"{name}_{self.nbuf}", list(shape), dtype))
        self.nbuf += 1
        return Buf(name, t)

    def dram(self, name, shape, dtype=F32, kind="Internal"):
        t = self.nc.dram_tensor(name, list(shape), dtype, kind=kind)
        return Buf(name, t.ap())

    def _waits(self, e, reads, writes):
        need = {}
        for b in reads:
            if b.last_w is not None:
                s, v = b.last_w
                need[s] = max(need.get(s, 0), v)
        for b in writes:
            if b.last_w is not None:
                s, v = b.last_w
                need[s] = max(need.get(s, 0), v)
            for (s, v) in b.readers:
                need[s] = max(need.get(s, 0), v)
        for s, v in need.items():
            if s == e.sem_id and (not SAME_ENGINE_SYNC or v > e.cnt):
                continue
            if e.known.get(s, 0) >= v:
                continue
            e.known[s] = v
            e.ops.append(("w", s, v))

    def _record(self, ev, reads, writes):
        for b in writes:
            b.last_w = ev
            b.readers = []
        for b in reads:
            if b not in writes:
                b.readers.append(ev)
                if len(b.readers) > 48:
                    b.readers = _compact(b.readers)

    def op(self, eng, fn, reads=(), writes=(), inc=True):
        e = self.engs[eng]
        self._waits(e, reads, writes)
        if inc:
            if e.cnt >= SEM_ROTATE and not e.pending:
                self._new_eng_sem(e)
            e.cnt += 1
            e.pending = False
            e.ops.append(("o", fn, e.sem_id, 1))
            self._record((e.sem_id, e.cnt), reads, writes)
        else:
            e.pending = True
            e.ops.append(("o", fn, None, 0))
            self._record((e.sem_id, e.cnt + 1), reads, writes)

    def dma(self, q, fn, reads=(), writes=()):
        e = self.engs[q]
        self._waits(e, reads, writes)
        slot = e.slots[e.slot_next % N_DMA_SLOTS]
        e.slot_next += 1
        sid, val = slot
        if val > 0 and e.known.get(sid, 0) < val:
            e.known[sid] = val
            e.ops.append(("w", sid, val))
        slot[1] = val + 16
        e.ops.append(("o", fn, sid, 16))
        self._record((sid, val + 16), reads, writes)

    def barrier(self):
        evs = []
        for e in self.engs.values():
            if e.cnt > 0:
                evs.append((e.sem_id, e.cnt))
            for sid, val in e.slots:
                if val > 0:
                    evs.append((sid, val))
        for e in self.engs.values():
            assert not e.pending
            for s, v in evs:
                if s == e.sem_id:
                    continue
                if e.known.get(s, 0) < v:
                    e.known[s] = v
                    e.ops.append(("w", s, v))

    def finish(self):
        self.barrier()
        nc, sems, engs = self.nc, self.sems, self.engs

        def replay(name):
            def f(h):
                for o in engs[name].ops:
                    if o[0] == "w":
                        h.wait_ge(sems[o[1]], o[2])
                    else:
                        ins = o[1](h)
                        if o[2] is not None:
                            ins.then_inc(sems[o[2]], o[3])
            return f

        with nc.allow_non_contiguous_dma(reason="small strided param loads"), nc.Block() as block:
            block.sync(replay("sync"))
            block.scalar(replay("scalar"))
            block.gpsimd(replay("gpsimd"))
            block.vector(replay("vector"))
            block.tensor(replay("tensor"))
        counts = {n: len(x.ops) for n, x in engs.items()}
        self.es.close()
        return counts


D = 2048
DFF = 5504
T = 2048
TT = 512
NTILE = T // TT
NSMP = 16
KC = 16
FC = 43
LRU_W = 1024
RW = 1024
SHIFT_W = 3360
AB_COLS = 5408
CD_COLS = 5680
NPOOL = 2560

WEIGHT_NAMES = [
    'norm_ffn1', 'ffn1_w_in', 'ffn1_w_out', 'norm_mix', 'norm_ffn2', 'ffn2_w_in', 'ffn2_w_out',
    'ab_w_in', 'lru_conv_w', 'lru_conv_b', 'lru_wa', 'lru_ba', 'lru_wx', 'lru_bx', 'lru_lambda',
    'rwkv_mu', 'rwkv_w0', 'rwkv_w2', 'rwkv_a0', 'rwkv_a2', 'rwkv_g2', 'rwkv_k_k', 'rwkv_k_a', 'rwkv_r_k',
    'rwkv_ln_g', 'rwkv_ln_b', 'ab_w_out',
    'cd_w_in', 'nsa_q_norm', 'nsa_k_norm', 'cmp_k_w1', 'cmp_k_b1', 'cmp_k_w2', 'cmp_k_b2',
    'cmp_v_w1', 'cmp_v_b1', 'cmp_v_w2', 'cmp_v_b2', 'ret_gn_g', 'ret_gn_b', 'cd_w_out']

WEIGHT_SHAPES = {
    'norm_ffn1': (2, 2048), 'ffn1_w_in': (2, 2048, 11008), 'ffn1_w_out': (2, 5504, 2048), 'norm_mix': (2, 2048),
    'norm_ffn2': (2, 2048), 'ffn2_w_in': (2, 2048, 11008), 'ffn2_w_out': (2, 5504, 2048),
    'ab_w_in': (1, 2048, 5408), 'lru_conv_w': (1, 4, 1024), 'lru_conv_b': (1, 1024), 'lru_wa': (1, 16, 64, 64),
    'lru_ba': (1, 1024), 'lru_wx': (1, 16, 64, 64), 'lru_bx': (1, 1024), 'lru_lambda': (1, 1024),
    'rwkv_mu': (1, 3360), 'rwkv_w0': (1, 1024), 'rwkv_w2': (1, 64, 1024), 'rwkv_a0': (1, 1024),
    'rwkv_a2': (1, 64, 1024), 'rwkv_g2': (1, 160, 1024), 'rwkv_k_k': (1, 1024), 'rwkv_k_a': (1, 1024),
    'rwkv_r_k': (1, 16, 64), 'rwkv_ln_g': (1, 1024), 'rwkv_ln_b': (1, 1024), 'ab_w_out': (1, 2048, 2048),
    'cd_w_in': (1, 2048, 5680), 'nsa_q_norm': (1, 64), 'nsa_k_norm': (1, 3, 64),
    'cmp_k_w1': (1, 2, 1024, 256), 'cmp_k_b1': (1, 256), 'cmp_k_w2': (1, 256, 64), 'cmp_k_b2': (1, 64),
    'cmp_v_w1': (1, 2, 1024, 256), 'cmp_v_b1': (1, 256), 'cmp_v_w2': (1, 256, 64), 'cmp_v_b2': (1, 64),
    'ret_gn_g': (1, 1024), 'ret_gn_b': (1, 1024), 'cd_w_out': (1, 2048, 2048)}

STATE_SHAPES = {
    'xp': (T, D), 'xs': (NSMP, D), 's_lru_h': (NSMP, 1024), 's_lru_conv': (NSMP * 3, 1024),
    's_shift': (NSMP, SHIFT_W), 's_wkv': (NSMP, 16, 64, 64), 'cache_kv': (NPOOL * 256, 512),
    'cache_win': (NSMP, 512, 512), 's_ret': (NSMP, 8, 64, 128)}

CONST_SHAPES = {'rope_nsa': (T + 8, 16), 'rope_ret': (T + 8, 64), 'ret_dmaskT': (8, 128, 128), 'ret_dec': (128, 16),
                'sel_tab': (17, 128, 96), 'cmp_ov': (128, 33)}


def make_consts():
    f32 = np.float32
    pos = np.arange(T + 8, dtype=f32)

    def tab(half, theta):
        inv = np.exp(-np.log(f32(theta)) * np.arange(half, dtype=f32) / f32(half)).astype(f32)
        ang = (pos[:, None] * inv[None, :]).astype(f32)
        return np.concatenate([np.cos(ang), np.sin(ang)], axis=1).astype(f32)
    lg = np.log1p(-np.exp2(-5.0 - np.arange(8, dtype=f32))).astype(f32)
    i = np.arange(128, dtype=f32)
    diff = i[:, None] - i[None, :]
    dm = np.where(diff >= 0, np.exp(np.where(diff >= 0, diff, 0.0)[None] * lg[:, None, None]), 0.0).astype(f32)
    dec = np.zeros((128, 16), f32)
    dec[:, 0:8] = np.exp((i[:, None] + 1.0) * lg[None, :])
    dec[:, 8:16] = np.exp((128 - 1.0 - i)[:, None] * lg[None, :])
    sel = np.zeros((17, 128, 96), f32)
    sel[16, :, 0:32] = 1.0
    sel[16, :, [0, 31]] = 0.0
    sel[16, :, [32, 63]] = 1e4
    sel[16, :, 64:96] = 1.0
    for t_ in range(16):
        for p_ in range(128):
            qb = (128 * t_ + p_) // 64
            for j_ in range(32):
                valid = j_ <= qb
                forced = valid and (j_ == 0 or j_ == qb or j_ == qb - 1)
                sel[t_, p_, j_] = 1.0 if (valid and not forced) else 0.0
                sel[t_, p_, 32 + j_] = 1e4 if forced else (0.0 if valid else -1.0)
                sel[t_, p_, 64 + j_] = 1.0 if valid else 0.0
    ov = np.zeros((128, 33), f32)
    ov[:, 0] = 1.0
    for c_ in range(127):
        for j_ in range(32):
            o_ = min(16 * c_ + 32, 64 * j_ + 64) - max(16 * c_, 64 * j_)
            ov[c_, 1 + j_] = max(o_, 0) / 32.0
    return {'sel_tab': sel, 'cmp_ov': ov, 'rope_nsa': tab(8, 500000.0), 'rope_ret': tab(32, 10000.0),
            'ret_dmaskT': np.ascontiguousarray(dm.transpose(0, 2, 1)), 'ret_dec': dec}


OUT_SHAPES = {
    'yp': (T, D), 'ys': (NSMP, D), 'lru_h_p': (1, 1024), 'lru_h_s': (NSMP, 1024),
    'lru_conv_p': (3, 1024), 'lru_conv_s': (NSMP * 3, 1024), 'shift_p': (1, SHIFT_W), 'shift_s': (NSMP, SHIFT_W),
    'wkv_p': (16, 64, 64), 'wkv_s': (NSMP, 16, 64, 64), 'kv_p': (T, 1024), 'kv_s': (NSMP, 1024),
    'win_p': (512, 512), 'win_s': (NSMP, 512, 512), 'ret_p': (8, 64, 128), 'ret_s': (NSMP, 8, 64, 128)}


def build(stage=99):
    nc = bass.Bass("TRN2", target_bir_lowering=False)
    fw = FW(nc)
    IN = {}
    for k, s in STATE_SHAPES.items():
        IN[k] = fw.dram(k, s, F32, kind="ExternalInput")
    IN['page_table'] = fw.dram('page_table', (NSMP, 16), I32, kind="ExternalInput")
    for k_, s_ in CONST_SHAPES.items():
        IN[k_] = fw.dram(k_, s_, F32, kind="ExternalInput")
    for k in WEIGHT_NAMES:
        IN[k] = fw.dram(k, WEIGHT_SHAPES[k], F32, kind="ExternalInput")
    OUT = {k: fw.dram(k, s, F32, kind="ExternalOutput") for k, s in OUT_SHAPES.items()}
    WIN = fw.dram('scr_win', (T, 512))
    NSA_ON = True
    NSA_SAMPLE = True
    POOL2D = IN['cache_kv']
    SCR = {'q': fw.dram('scr_q', (NSMP, 512)), 'k': fw.dram('scr_k', (NSMP, 512)), 'v': fw.dram('scr_v', (NSMP, 1024))}

    def V(fn, reads=(), writes=(), inc=True):
        fw.op("vector", fn, reads, writes, inc)

    def A(fn, reads=(), writes=(), inc=True):
        fw.op("scalar", fn, reads, writes, inc)

    def PE(fn, reads=(), writes=(), inc=True):
        fw.op("tensor", fn, reads, writes, inc)

    def G(fn, reads=(), writes=(), inc=True):
        fw.op("gpsimd", fn, reads, writes, inc)

    def DS(fn, reads=(), writes=()):
        fw.dma("sync", fn, reads, writes)

    def DG(fn, reads=(), writes=()):
        fw.dma("gpsimd", fn, reads, writes)

    xT = fw.sb("xT", [128, KC, TT], F32)
    hT = fw.sb("hT", [128, KC, TT], BF16)
    banks = [fw.ps(f"bank{i}", [128, 512], F32) for i in range(8)]
    bank_i = [0]
    bank_pool = list(banks)

    def bank():
        b = bank_pool[bank_i[0] % len(bank_pool)]
        bank_i[0] += 1
        return b

    wsm = [fw.sb(f"wsm{i}", [128, KC, 128], BF16) for i in range(3)]
    wsm_i = [0]

    def proj_fm(w_ap, col0, ncols, N, src=None):
        src = src or hT
        wt = wsm[wsm_i[0] % 3]
        wsm_i[0] += 1
        DG(lambda h: h.dma_start(out=wt[:, :, 0:ncols],
                                 in_=w_ap[:, col0:col0 + ncols].rearrange("(kc p) n -> p kc n", p=128)),
           reads=[], writes=[wt])
        bk = bank()
        for kc in range(KC):
            PE(lambda h, kc=kc: h.matmul(bk[0:ncols, :N], lhsT=wt[:, kc, 0:ncols], rhs=src[:, kc, :N],
                                         start=(kc == 0), stop=(kc == KC - 1)),
               reads=[wt, src], writes=[bk], inc=(kc == KC - 1))
        return bk

    ident = fw.sb("ident", [128, 128], F32)
    ones_f = fw.sb("ones_f", [128, 128], F32)
    eps6 = fw.sb("eps6", [128, 1], F32)
    gains = fw.sb("gains", [128, 6, KC], F32)
    rstd = fw.sb("rstd", [128, TT], F32)
    sqb = [fw.sb(f"sqb{i}", [128, TT], F32) for i in range(2)]

    G(lambda h: h.memset(ident[:], 1.0), writes=[ident])
    G(lambda h: h.affine_select(out=ident[:], in_=ident[:], pattern=[[-1, 128]], compare_op=ALU.is_equal,
                                fill=0.0, base=0, channel_multiplier=1), reads=[ident], writes=[ident])
    V(lambda h: h.memset(ones_f[:], 1.0), writes=[ones_f])
    V(lambda h: h.memset(eps6[:], 1e-6), writes=[eps6])
    with nc.allow_non_contiguous_dma(reason="small param vectors"):
        for li in range(2):
            for ni, nm in enumerate(("norm_ffn1", "norm_mix", "norm_ffn2")):
                DS(lambda h, li=li, ni=ni, nm=nm: h.dma_start(
                    out=gains[:, li * 3 + ni, :], in_=IN[nm][li, :].rearrange("(kc p) -> p kc", p=128)),
                    reads=[IN[nm]], writes=[gains])

    def load_xT(src_rows, N):
        nsub = (N + 127) // 128
        es_ = ExitStack()
        tok = [fw.sb(f"tok{i}", [128, D], F32, es=es_) for i in range(2)]
        for s in range(nsub):
            r = min(128, N - s * 128)
            tb = tok[s % 2]
            DS(lambda h, tb=tb, s=s, r=r: h.dma_start(out=tb[0:r, :], in_=src_rows[s * 128:s * 128 + r, :]),
               reads=[IN['xp'], IN['xs']], writes=[tb])
            for kc4 in range(4):
                bk = bank()
                for q in range(4):
                    kc = kc4 * 4 + q
                    PE(lambda h, bk=bk, tb=tb, kc=kc, q=q, r=r: h.transpose(
                        out=bk[:, q * 128:q * 128 + r], in_=tb[0:r, kc * 128:(kc + 1) * 128], identity=ident[0:r, 0:r]),
                        reads=[tb, ident], writes=[bk])
                V(lambda h, bk=bk, kc4=kc4, s=s, r=r: h.tensor_copy(
                    out=xT[:, kc4 * 4:kc4 * 4 + 4, s * 128:s * 128 + r],
                    in_=bk[:, :].rearrange("p (q n) -> p q n", n=128)[:, :, 0:r]), reads=[bk], writes=[xT])
        fw.barrier()
        es_.close()

    def store_xT(dst_rows, dst_buf, N):
        nsub = (N + 127) // 128
        es_ = ExitStack()
        tok = [fw.sb(f"tok{i}", [128, D], F32, es=es_) for i in range(2)]
        for s in range(nsub):
            r = min(128, N - s * 128)
            tb = tok[s % 2]
            for kc4 in range(4):
                bk = bank()
                for q in range(4):
                    kc = kc4 * 4 + q
                    PE(lambda h, bk=bk, kc=kc, q=q, r=r, s=s: h.transpose(
                        out=bk[0:r, q * 128:(q + 1) * 128], in_=xT[:, kc, s * 128:s * 128 + r], identity=ident[:]),
                        reads=[xT, ident], writes=[bk])
                A(lambda h, bk=bk, kc4=kc4, r=r, tb=tb: h.copy(out=tb[0:r, kc4 * 512:(kc4 + 1) * 512], in_=bk[0:r, :]),
                  reads=[bk], writes=[tb])
            DS(lambda h, tb=tb, s=s, r=r: h.dma_start(out=dst_rows[s * 128:s * 128 + r, :], in_=tb[0:r, :]),
               reads=[tb], writes=[dst_buf])
        fw.barrier()
        es_.close()

    def rmsnorm(N, gi):
        bk = bank()
        for kc in range(KC):
            sq = sqb[kc % 2]
            A(lambda h, sq=sq, kc=kc: h.activation(out=sq[:, :N], in_=xT[:, kc, :N], func=AF.Square),
              reads=[xT], writes=[sq])
            PE(lambda h, sq=sq, kc=kc, bk=bk: h.matmul(bk[:, :N], lhsT=ones_f[:], rhs=sq[:, :N],
                                                     start=(kc == 0), stop=(kc == KC - 1)),
               reads=[sq, ones_f], writes=[bk])
        A(lambda h, bk=bk: h.activation(out=rstd[:, :N], in_=bk[:, :N], func=AF.Sqrt, scale=1.0 / D, bias=eps6[:, 0:1]),
          reads=[bk, eps6], writes=[rstd])
        V(lambda h: h.reciprocal(out=rstd[:, :N], in_=rstd[:, :N]), reads=[rstd], writes=[rstd])
        for kc in range(KC):
            V(lambda h, kc=kc: h.scalar_tensor_tensor(out=hT[:, kc, :N], in0=xT[:, kc, :N],
                                                      scalar=gains[:, gi, kc:kc + 1], in1=rstd[:, :N],
                                                      op0=ALU.mult, op1=ALU.mult),
              reads=[xT, gains, rstd], writes=[hT])

    def ffn(N, layer, which):
        gi = layer * 3 + (0 if which == 1 else 2)
        w_in = IN[f'ffn{which}_w_in']
        w_out = IN[f'ffn{which}_w_out']
        rmsnorm(N, gi)
        with ExitStack() as es:
            actT = fw.sb("actT", [128, FC, N], BF16, es=es)
            sg = [fw.sb(f"sg{i}", [128, N], F32, es=es) for i in range(2)]
            wo = [fw.sb(f"wo{i}", [128, FC, 128], BF16, es=es) for i in range(2)]
            wbufs = []
            for i in range(2):
                t = fw.sb(f"wbuf{i}", [128, KC, 256], BF16, es=es)
                wbufs.append({"t": t, "a": Buf("wa"), "b": Buf("wb")})
            wb_i = [0]

            def wbuf():
                w = wbufs[wb_i[0] % 2]
                wb_i[0] += 1
                return w
            for fb in range(FC):
                f0 = fb * 128
                w = wbuf()
                wt = w["t"]
                DG(lambda h, wt=wt, f0=f0: h.dma_start(
                    out=wt[:, :, 0:128], in_=w_in[layer, :, f0:f0 + 128].rearrange("(kc p) n -> p kc n", p=128)),
                    writes=[w["a"]])
                DG(lambda h, wt=wt, f0=f0: h.dma_start(
                    out=wt[:, :, 128:256],
                    in_=w_in[layer, :, DFF + f0:DFF + f0 + 128].rearrange("(kc p) n -> p kc n", p=128)),
                    writes=[w["b"]])
                bg, bu = bank(), bank()
                for kc in range(KC):
                    PE(lambda h, bg=bg, wt=wt, kc=kc: h.matmul(
                        bg[:, :N], lhsT=wt[:, kc, 0:128], rhs=hT[:, kc, :N],
                        start=(kc == 0), stop=(kc == KC - 1)),
                        reads=[w["a"], hT], writes=[bg], inc=(kc == KC - 1))
                for kc in range(KC):
                    PE(lambda h, bu=bu, wt=wt, kc=kc: h.matmul(
                        bu[:, :N], lhsT=wt[:, kc, 128:256], rhs=hT[:, kc, :N],
                        start=(kc == 0), stop=(kc == KC - 1)),
                        reads=[w["b"], hT], writes=[bu], inc=(kc == KC - 1))
                s_ = sg[fb % 2]
                A(lambda h, s_=s_, bg=bg: h.activation(out=s_[:, :N], in_=bg[:, :N], func=AF.Silu),
                  reads=[bg], writes=[s_])
                V(lambda h, s_=s_, bu=bu, fb=fb: h.tensor_tensor(out=actT[:, fb, :N], in0=s_[:, :N], in1=bu[:, :N],
                                                                 op=ALU.mult),
                  reads=[s_, bu], writes=[actT])
            for db in range(KC):
                w2 = wo[db % 2]
                DG(lambda h, w2=w2, db=db: h.dma_start(
                    out=w2[:], in_=w_out[layer, :, db * 128:(db + 1) * 128].rearrange("(fc p) n -> p fc n", p=128)),
                    reads=[w_out], writes=[w2])
                bk = bank()
                for fc in range(FC):
                    PE(lambda h, bk=bk, w2=w2, fc=fc: h.matmul(bk[:, :N], lhsT=w2[:, fc, :], rhs=actT[:, fc, :N],
                                                             start=(fc == 0), stop=(fc == FC - 1)),
                       reads=[w2, actT], writes=[bk], inc=(fc == FC - 1))
                V(lambda h, bk=bk, db=db: h.scalar_tensor_tensor(out=xT[:, db, :N], in0=bk[:, :N], scalar=0.5,
                                                                 in1=xT[:, db, :N], op0=ALU.mult, op1=ALU.add),
                  reads=[bk, xT], writes=[xT])
            fw.barrier()

    CB, BA, BX_, NSP, W0, A0, KK_, KA, LNG, LNB, RK, CW0 = 0, 1, 2, 3, 4, 5, 6, 7, 8, 9, 10, 11
    cvec = fw.sb("cvec", [128, 15, 8], F32)
    muT = fw.sb("muT", [128, 27], F32)
    blk1 = fw.sb("blk1", [128, 128], F32)
    M_ar = fw.sb("M_ar", [128, 256], F32)
    M_sl = fw.sb("M_sl", [128, 128], F32)
    eps_gn = fw.sb("eps_gn", [128, 1], F32)
    ones_tt = fw.sb("ones_tt", [128, TT], F32)
    convtail = fw.sb("convtail", [128, 8, 3], F32)
    hprev = fw.sb("hprev", [128, 8], F32)
    lastcol = fw.sb("lastcol", [128, 27], F32)
    Pst = [[fw.sb(f"Pst{j}_{k}", [128, 128], F32) for k in range(2)] for j in range(8)]
    pcur = [0] * 8

    def vload(idx, ap1d):
        DS(lambda h: h.dma_start(out=cvec[:, idx, :], in_=ap1d.rearrange("(c p) -> p c", p=128)), writes=[cvec])

    vload(CB, IN['lru_conv_b'][0, :])
    vload(BA, IN['lru_ba'][0, :])
    vload(BX_, IN['lru_bx'][0, :])
    vload(NSP, IN['lru_lambda'][0, :])
    vload(W0, IN['rwkv_w0'][0, :])
    vload(A0, IN['rwkv_a0'][0, :])
    vload(KK_, IN['rwkv_k_k'][0, :])
    vload(KA, IN['rwkv_k_a'][0, :])
    vload(LNG, IN['rwkv_ln_g'][0, :])
    vload(LNB, IN['rwkv_ln_b'][0, :])
    vload(RK, IN['rwkv_r_k'][0].rearrange("h d -> (h d)"))
    for j_ in range(4):
        vload(CW0 + j_, IN['lru_conv_w'][0, j_, :])
    A(lambda h: h.activation(out=cvec[:, NSP, :], in_=cvec[:, NSP, :], func=AF.Exp, scale=-1.0), reads=[cvec], writes=[cvec])
    A(lambda h: h.activation(out=cvec[:, NSP, :], in_=cvec[:, NSP, :], func=AF.Ln, bias=ones_f[:, 0:1]), reads=[cvec, ones_f], writes=[cvec])
    V(lambda h: h.tensor_scalar(out=cvec[:, NSP, :], in0=cvec[:, NSP, :], scalar1=-8.0, scalar2=None, op0=ALU.mult), reads=[cvec], writes=[cvec])
    V(lambda h: h.memset(muT[:], 0.0), writes=[muT])
    DS(lambda h: h.dma_start(out=muT[:, 0:26], in_=IN['rwkv_mu'][0, 0:3328].rearrange("(c p) -> p c", p=128)), writes=[muT])
    DS(lambda h: h.dma_start(out=muT[0:32, 26:27], in_=IN['rwkv_mu'][0, 3328:3360].rearrange("(c p) -> p c", p=32)), writes=[muT])
    V(lambda h: h.memset(blk1[:], 0.0), writes=[blk1])
    V(lambda h: h.memset(blk1[0:64, 0:64], 1.0), writes=[blk1])
    V(lambda h: h.memset(blk1[64:128, 64:128], 1.0), writes=[blk1])
    G(lambda h: h.affine_select(out=M_ar[:, 0:128], in_=blk1[:], pattern=[[1, 128]], compare_op=ALU.is_gt, fill=0.0,
                                base=0, channel_multiplier=-1), reads=[blk1], writes=[M_ar])
    G(lambda h: h.affine_select(out=M_ar[:, 128:256], in_=blk1[:], pattern=[[1, 128]], compare_op=ALU.is_ge, fill=0.0,
                                base=0, channel_multiplier=-1), reads=[blk1], writes=[M_ar])
    G(lambda h: h.affine_select(out=M_sl[:], in_=blk1[:], pattern=[[-1, 128]], compare_op=ALU.is_gt, fill=0.0,
                                base=0, channel_multiplier=1), reads=[blk1], writes=[M_sl])
    V(lambda h: h.memset(eps_gn[:], 64e-5), writes=[eps_gn])
    V(lambda h: h.memset(ones_tt[:], 1.0), writes=[ones_tt])
    V(lambda h: h.memset(convtail[:], 0.0), writes=[convtail])
    V(lambda h: h.memset(hprev[:], 0.0), writes=[hprev])
    V(lambda h: h.memset(lastcol[:], 0.0), writes=[lastcol])
    for j_ in range(8):
        V(lambda h, j_=j_: h.memset(Pst[j_][0][:], 0.0), writes=[Pst[j_][0]])

    def tt(out, in0, in1, op, reads, writes, eng=None):
        (eng or V)(lambda h: h.tensor_tensor(out=out, in0=in0, in1=in1, op=op), reads=reads, writes=writes)

    def stt(out, in0, scalar, in1, op0, op1, reads, writes):
        V(lambda h: h.scalar_tensor_tensor(out=out, in0=in0, scalar=scalar, in1=in1, op0=op0, op1=op1), reads=reads, writes=writes)

    def ts(out, in0, s1, s2, op0, op1, reads, writes):
        if s2 is None:
            V(lambda h: h.tensor_scalar(out=out, in0=in0, scalar1=s1, scalar2=None, op0=op0), reads=reads, writes=writes)
        else:
            V(lambda h: h.tensor_scalar(out=out, in0=in0, scalar1=s1, scalar2=s2, op0=op0, op1=op1), reads=reads, writes=writes)

    def act(out, in_, func, reads, writes, bias=None, scale=None):
        kw = {}
        if bias is not None:
            kw["bias"] = bias
        if scale is not None:
            kw["scale"] = scale
        A(lambda h: h.activation(out=out, in_=in_, func=func, **kw), reads=reads, writes=writes)

    def mm(out, lhsT, rhs, reads, writes, start=True, stop=True, inc=True):
        PE(lambda h: h.matmul(out, lhsT=lhsT, rhs=rhs, start=start, stop=stop), reads=reads, writes=writes, inc=inc)

    def vcopy(out, in_, reads, writes):
        V(lambda h: h.tensor_copy(out=out, in_=in_), reads=reads, writes=writes)

    def acopy(out, in_, reads, writes):
        A(lambda h: h.copy(out=out, in_=in_), reads=reads, writes=writes)

    def even_mixer(kind, N, last):
        W = IN['ab_w_in'][0]
        prm = (kind == "p")
        n = 64 if prm else 1
        NU = N // n
        rmsnorm(N, 1)
        with ExitStack() as es:
            yTin = fw.sb("yTin", [128, KC, N], BF16, es=es)
            w2a2 = fw.sb("w2a2", [128, 1024], F32, es=es)
            g2a = fw.sb("g2a", [128, 1024], F32, es=es)
            g2b = fw.sb("g2b", [32, 1024], F32, es=es)
            WAbd = fw.sb("WAbd", [128, 8, 128], F32, es=es)
            WXbd = fw.sb("WXbd", [128, 8, 128], F32, es=es)
            DS(lambda h: h.dma_start(out=w2a2[0:64, :], in_=IN['rwkv_w2'][0]), writes=[w2a2])
            DS(lambda h: h.dma_start(out=w2a2[64:128, :], in_=IN['rwkv_a2'][0]), writes=[w2a2])
            DS(lambda h: h.dma_start(out=g2a[:], in_=IN['rwkv_g2'][0, 0:128, :]), writes=[g2a])
            DS(lambda h: h.dma_start(out=g2b[:], in_=IN['rwkv_g2'][0, 128:160, :]), writes=[g2b])
            V(lambda h: h.memset(WAbd[:], 0.0), writes=[WAbd])
            V(lambda h: h.memset(WXbd[:], 0.0), writes=[WXbd])
            for n_ in range(16):
                c_, hh_ = n_ // 2, n_ % 2
                sl_ = slice(64 * hh_, 64 * hh_ + 64)
                DS(lambda h, n_=n_, c_=c_, sl_=sl_: h.dma_start(out=WAbd[sl_, c_, sl_], in_=IN['lru_wa'][0, n_]), writes=[WAbd])
                DS(lambda h, n_=n_, c_=c_, sl_=sl_: h.dma_start(out=WXbd[sl_, c_, sl_], in_=IN['lru_wx'][0, n_]), writes=[WXbd])

            with ExitStack() as es2:
                xbpad = fw.sb("xbpad", [128, 8, N + 3], F32, es=es2)
                hs = fw.sb("hs", [128, 8, N], F32, es=es2)
                tl = [fw.sb(f"tl{i}", [128, N], F32, es=es2) for i in range(6)]
                if not prm:
                    convT = fw.sb("convT", [128, 8, 3, N], F32, es=es2)
                    h0T = fw.sb("h0T", [128, 8, N], F32, es=es2)
                    csout = fw.sb("csout", [128, 8, 3, N], F32, es=es2)
                    for c in range(8):
                        cs = slice(c * 128, (c + 1) * 128)
                        for j3 in range(3):
                            DS(lambda h, c=c, cs=cs, j3=j3: h.dma_start(out=convT[:, c, j3, :], in_=IN['s_lru_conv'][:, cs].rearrange("(b j) p -> p j b", j=3)[:, j3, :]), writes=[convT])
                        DS(lambda h, c=c, cs=cs: h.dma_start(out=h0T[:, c, :], in_=IN['s_lru_h'][:, cs].rearrange("b p -> p b")), writes=[h0T])
                for c in range(8):
                    bk = proj_fm(W, c * 128, 128, N)
                    acopy(xbpad[:, c, 3:3 + N], bk[:, :N], [bk], [xbpad])
                    if prm:
                        vcopy(xbpad[:, c, 0:3], convtail[:, c, :], [convtail], [xbpad])
                        srcs = [xbpad[:, c, j:j + N] for j in range(4)]
                        srd = [xbpad]
                    else:
                        srcs = [convT[:, c, 0, :], convT[:, c, 1, :], convT[:, c, 2, :], xbpad[:, c, 3:3 + N]]
                        srd = [xbpad, convT]
                    xc, gr, gi, a_, t1, g_ = tl
                    ts(xc[:, :N], srcs[0], cvec[:, CW0, c:c + 1], cvec[:, CB, c:c + 1], ALU.mult, ALU.add, srd + [cvec], [xc])
                    for j in range(1, 4):
                        stt(xc[:, :N], srcs[j], cvec[:, CW0 + j, c:c + 1], xc[:, :N], ALU.mult, ALU.add, srd + [cvec, xc], [xc])
                    b1, b2 = bank(), bank()
                    mm(b1[:, :N], WAbd[:, c, :], xc[:, :N], [WAbd, xc], [b1])
                    mm(b2[:, :N], WXbd[:, c, :], xc[:, :N], [WXbd, xc], [b2])
                    act(gr[:, :N], b1[:, :N], AF.Sigmoid, [b1, cvec], [gr], bias=cvec[:, BA, c:c + 1])
                    act(gi[:, :N], b2[:, :N], AF.Sigmoid, [b2, cvec], [gi], bias=cvec[:, BX_, c:c + 1])
                    act(a_[:, :N], gr[:, :N], AF.Exp, [gr, cvec], [a_], scale=cvec[:, NSP, c:c + 1])
                    tt(t1[:, :N], a_[:, :N], a_[:, :N], ALU.mult, [a_], [t1])
                    ts(t1[:, :N], t1[:, :N], -1.0, 1.0, ALU.mult, ALU.add, [t1], [t1])
                    act(t1[:, :N], t1[:, :N], AF.Sqrt, [t1], [t1])
                    tt(gi[:, :N], gi[:, :N], xc[:, :N], ALU.mult, [gi, xc], [gi])
                    tt(t1[:, :N], t1[:, :N], gi[:, :N], ALU.mult, [t1, gi], [t1])
                    if prm:
                        V(lambda h, c=c: h.tensor_tensor_scan(out=hs[:, c, :N], data0=a_[:, :N], data1=t1[:, :N],
                                                              initial=hprev[:, c:c + 1], op0=ALU.mult, op1=ALU.add),
                          reads=[a_, t1, hprev], writes=[hs])
                        vcopy(hprev[:, c:c + 1], hs[:, c, N - 1:N], [hs], [hprev])
                        vcopy(convtail[:, c, :], xbpad[:, c, N:N + 3], [xbpad], [convtail])
                    else:
                        tt(a_[:, :N], a_[:, :N], h0T[:, c, :], ALU.mult, [a_, h0T], [a_])
                        tt(hs[:, c, :N], a_[:, :N], t1[:, :N], ALU.add, [a_, t1], [hs])
                for c in range(8):
                    bk = proj_fm(W, 1024 + c * 128, 128, N)
                    g_ = tl[5]
                    act(g_[:, :N], bk[:, :N], AF.Gelu_apprx_tanh, [bk], [g_])
                    tt(yTin[:, c, :N], g_[:, :N], hs[:, c, :N], ALU.mult, [g_, hs], [yTin])
                if prm and last:
                    DS(lambda h: h.dma_start(out=OUT['lru_h_p'][0, :].rearrange("(c p) -> p c", p=128), in_=hprev[:, :]),
                       reads=[hprev], writes=[OUT['lru_h_p']])
                    for j3 in range(3):
                        DS(lambda h, j3=j3: h.dma_start(out=OUT['lru_conv_p'][j3, :].rearrange("(c p) -> p c", p=128), in_=convtail[:, :, j3]),
                           reads=[convtail], writes=[OUT['lru_conv_p']])
                if not prm:
                    for c in range(8):
                        cs = slice(c * 128, (c + 1) * 128)
                        DS(lambda h, c=c, cs=cs: h.dma_start(out=OUT['lru_h_s'][:, cs].rearrange("b p -> p b"), in_=hs[:, c, :N]),
                           reads=[hs], writes=[OUT['lru_h_s']])
                        vcopy(csout[:, c, 0:2, :], convT[:, c, 1:3, :], [convT], [csout])
                        vcopy(csout[:, c, 2, :], xbpad[:, c, 3:3 + N], [xbpad], [csout])
                        for j3 in range(3):
                            DS(lambda h, c=c, cs=cs, j3=j3: h.dma_start(out=OUT['lru_conv_s'][:, cs].rearrange("(b j) p -> p j b", j=3)[:, j3, :], in_=csout[:, c, j3, :]),
                               reads=[csout], writes=[OUT['lru_conv_s']])
                fw.barrier()
            with ExitStack() as es2:
                def mk(nm, w=N):
                    return fw.sb(nm, [128, w], F32, es=es2)
                rs24, rs25, rs26, th, sg25, sg26 = (mk(x) for x in ("rs24", "rs25", "rs26", "th", "sg25", "sg26"))
                pad = [mk("pad0", N + 1), mk("pad1", N + 1)]
                tmp = mk("tmp")
                r_, k_, v_, lw, aicl, gate, kk, kmod, bb, bonus, Wt, Winv, Wprev, yT, t2, t3 = (mk(x) for x in (
                    "r_", "k_", "v_", "lw", "aicl", "gate", "kk", "kmod", "bb", "bonus", "Wt", "Winv", "Wprev", "yT", "t2", "t3"))
                Lpad = mk("Lpad", N + 1)
                negL = mk("negL", N + 1)
                if not prm:
                    shiftT = fw.sb("shiftT", [128, 27, N], F32, es=es2)
                    rawS = fw.sb("rawS", [128, 27, N], F32, es=es2)
                    SXi = [fw.sb(f"SXi{i}", [128, 128], F32, es=es2) for i in range(2)]
                    SXo = [fw.sb(f"SXo{i}", [128, 128], F32, es=es2) for i in range(2)]
                    Ps = [fw.sb(f"Ps{i}", [128, 128], F32, es=es2) for i in range(4)]
                    for t_ in SXi:
                        V(lambda h, t_=t_: h.memset(t_[:], 0.0), writes=[t_])
                    for q in range(27):
                        rows = 128 if q < 26 else 32
                        DS(lambda h, q=q, rows=rows: h.dma_start(out=shiftT[0:rows, q, :], in_=IN['s_shift'][:, q * 128:q * 128 + rows].rearrange("b p -> p b")),
                           writes=[shiftT])
                SS = []
                for k in range(2):
                    S = {}
                    for nm, w in (("AR", 256), ("BX", 128), ("KX", 128), ("VXf", 128), ("NN1", 256), ("NN2", 256),
                                  ("A0", 128), ("A1", 128), ("N0", 128), ("N1", 128), ("G0", 128), ("G1", 128),
                                  ("T3", 384), ("R0", 128), ("U", 128)):
                        S[nm] = fw.sb(f"S{k}{nm}", [128, w], F32, es=es2)
                    for nm in ("AR", "BX", "KX", "VXf"):
                        V(lambda h, t_=S[nm]: h.memset(t_[:], 0.0), writes=[S[nm]])
                    SS.append(S)

                def shifted(q, dest, dbuf, rows=128):
                    bk = proj_fm(W, 2048 + q * 128, rows, N)
                    pd = pad[q % 2]
                    acopy(pd[0:rows, 1:N + 1], bk[0:rows, :N], [bk], [pd])
                    if prm:
                        vcopy(pd[0:rows, 0:1], lastcol[0:rows, q:q + 1], [lastcol], [pd])
                        prev = pd[0:rows, 0:N]
                        prd = [pd]
                    else:
                        prev = shiftT[0:rows, q, :]
                        prd = [pd, shiftT]
                        vcopy(rawS[0:rows, q, :], pd[0:rows, 1:N + 1], [pd], [rawS])
                    cur = pd[0:rows, 1:N + 1]
                    tt(tmp[0:rows, :N], prev, cur, ALU.subtract, prd, [tmp])
                    stt(dest, tmp[0:rows, :N], muT[0:rows, q:q + 1], cur, ALU.mult, ALU.add, [tmp, muT, pd], [dbuf])
                    if prm:
                        vcopy(lastcol[0:rows, q:q + 1], pd[0:rows, N:N + 1], [pd], [lastcol])

                shifted(24, rs24[:, :N], rs24)
                shifted(25, rs25[:, :N], rs25)
                shifted(26, rs26[0:32, :N], rs26, rows=32)
                act(th[0:64, :N], rs24[0:64, :N], AF.Tanh, [rs24], [th])
                act(sg25[:, :N], rs25[:, :N], AF.Sigmoid, [rs25], [sg25])
                act(sg26[0:32, :N], rs26[0:32, :N], AF.Sigmoid, [rs26], [sg26])
                ucount = 0
                for j in range(8):
                    jc = slice(j * 128, (j + 1) * 128)
                    shifted(j, r_[:, :N], r_)
                    shifted(8 + j, k_[:, :N], k_)
                    shifted(16 + j, v_[:, :N], v_)
                    bd = bank()
                    mm(bd[:, :N], w2a2[0:64, jc], th[0:64, :N], [w2a2, th], [bd])
                    act(t2[:, :N], bd[:, :N], AF.Sigmoid, [bd, cvec], [t2], bias=cvec[:, W0, j:j + 1])
                    ts(lw[:, :N], t2[:, :N], -math.exp(-0.5), None, ALU.mult, None, [t2], [lw])
                    ba_ = bank()
                    mm(ba_[:, :N], w2a2[64:128, jc], rs24[64:128, :N], [w2a2, rs24], [ba_])
                    act(aicl[:, :N], ba_[:, :N], AF.Sigmoid, [ba_, cvec], [aicl], bias=cvec[:, A0, j:j + 1])
                    bg = bank()
                    mm(bg[:, :N], g2a[:, jc], sg25[:, :N], [g2a, sg25], [bg], start=True, stop=False)
                    mm(bg[:, :N], g2b[0:32, jc], sg26[0:32, :N], [g2b, sg26], [bg], start=False, stop=True)
                    acopy(gate[:, :N], bg[:, :N], [bg], [gate])
                    ts(kk[:, :N], k_[:, :N], cvec[:, KK_, j:j + 1], None, ALU.mult, None, [k_, cvec], [kk])
                    act(t2[:, :N], kk[:, :N], AF.Square, [kk], [t2])
                    bs = bank()
                    mm(bs[:, :N], blk1[:], t2[:, :N], [blk1, t2], [bs])
                    act(t3[:, :N], bs[:, :N], AF.Sqrt, [bs], [t3])
                    ts(t3[:, :N], t3[:, :N], 1e-12, None, ALU.max, None, [t3], [t3])
                    V(lambda h: h.reciprocal(out=t3[:, :N], in_=t3[:, :N]), reads=[t3], writes=[t3])
                    tt(kk[:, :N], kk[:, :N], t3[:, :N], ALU.mult, [kk, t3], [kk])
                    ts(t2[:, :N], aicl[:, :N], -1.0, cvec[:, KA, j:j + 1], ALU.add, ALU.mult, [aicl, cvec], [t2])
                    stt(kmod[:, :N], t2[:, :N], 1.0, k_[:, :N], ALU.add, ALU.mult, [t2, k_], [kmod])
                    tt(bb[:, :N], kk[:, :N], aicl[:, :N], ALU.mult, [kk, aicl], [bb])
                    stt(t2[:, :N], r_[:, :N], cvec[:, RK, j:j + 1], kmod[:, :N], ALU.mult, ALU.mult, [r_, cvec, kmod], [t2])
                    bb2 = bank()
                    mm(bb2[:, :N], blk1[:], t2[:, :N], [blk1, t2], [bb2])
                    tt(bonus[:, :N], bb2[:, :N], v_[:, :N], ALU.mult, [bb2, v_], [bonus])
                    if prm:
                        V(lambda h: h.memset(Lpad[:, 0:1], 0.0), writes=[Lpad])
                        V(lambda h: h.tensor_tensor_scan(out=Lpad[:, 1:N + 1], data0=ones_tt[:, :N], data1=lw[:, :N], initial=0.0,
                                                         op0=ALU.mult, op1=ALU.add), reads=[ones_tt, lw], writes=[Lpad])
                        ts(negL[:, :], Lpad[:, :], -1.0, None, ALU.mult, None, [Lpad], [negL])
                        for ci in range(NU):
                            c0 = ci * 64
                            act(Wt[:, c0:c0 + 64], Lpad[:, c0 + 1:c0 + 65], AF.Exp, [Lpad, negL], [Wt], bias=negL[:, c0:c0 + 1])
                            act(Winv[:, c0:c0 + 64], Lpad[:, c0 + 1:c0 + 65], AF.Exp, [Lpad, negL], [Winv], bias=Lpad[:, c0:c0 + 1], scale=-1.0)
                            act(Wprev[:, c0:c0 + 64], Lpad[:, c0:c0 + 64], AF.Exp, [Lpad, negL], [Wprev], bias=negL[:, c0:c0 + 1])
                    else:
                        act(Wt[:, :N], lw[:, :N], AF.Exp, [lw], [Wt])
                        act(Winv[:, :N], lw[:, :N], AF.Exp, [lw], [Winv], scale=-1.0)
                        vcopy(Wprev[:, :N], ones_tt[:, :N], [ones_tt], [Wprev])
                    for u in range(NU):
                        S = SS[ucount % 2]
                        cols = slice(u * n, (u + 1) * n)
                        if prm:
                            P0 = Pst[j][pcur[j]]
                            P1 = Pst[j][1 - pcur[j]]
                            pcur[j] = 1 - pcur[j]
                        else:
                            sx = SXi[ucount % 2]
                            for hh in range(2):
                                sl = slice(64 * hh, 64 * hh + 64)
                                DS(lambda h, sx=sx, sl=sl, u=u, hh=hh, j=j: h.dma_start(out=sx[sl, sl], in_=IN['s_wkv'][u, 2 * j + hh]), writes=[sx])
                            bt0 = bank()
                            PE(lambda h, bt0=bt0, sx=sx: h.transpose(out=bt0[:, 0:128], in_=sx[:], identity=ident[:]), reads=[sx, ident], writes=[bt0])
                            P0 = Ps[(ucount % 2) * 2]
                            P1 = Ps[(ucount % 2) * 2 + 1]
                            acopy(P0[:], bt0[:, 0:128], [bt0], [P0])
                        ucount += 1
                        for hh in range(2):
                            ps_ = slice(64 * hh, 64 * hh + 64)
                            f0 = 64 * hh
                            stt(S["AR"][ps_, f0:f0 + n], kk[ps_, cols], -1.0, Wprev[ps_, cols], ALU.mult, ALU.mult, [kk, Wprev], [S["AR"]])
                            tt(S["AR"][ps_, 128 + f0:128 + f0 + n], r_[ps_, cols], Wt[ps_, cols], ALU.mult, [r_, Wt], [S["AR"]])
                            tt(S["BX"][ps_, f0:f0 + n], bb[ps_, cols], Winv[ps_, cols], ALU.mult, [bb, Winv], [S["BX"]])
                            tt(S["KX"][ps_, f0:f0 + n], kmod[ps_, cols], Winv[ps_, cols], ALU.mult, [kmod, Winv], [S["KX"]])
                            acopy(S["VXf"][ps_, f0:f0 + n], v_[ps_, cols], [v_], [S["VXf"]])
                        b1, b2 = bank(), bank()
                        mm(b1[:, 0:256], S["BX"][:], S["AR"][:], [S["BX"], S["AR"]], [b1])
                        tt(S["NN1"][:], b1[:, 0:256], M_ar[:], ALU.mult, [b1, M_ar], [S["NN1"]])
                        mm(b2[:, 0:256], S["KX"][:], S["AR"][:], [S["KX"], S["AR"]], [b2])
                        tt(S["NN2"][:], b2[:, 0:256], M_ar[:], ALU.mult, [b2, M_ar], [S["NN2"]])
                        if n > 1:
                            b3 = bank()
                            mm(b3[:, 0:128], S["AR"][:, 0:128], S["BX"][:], [S["AR"], S["BX"]], [b3])
                            tt(S["A0"][:], b3[:, 0:128], M_sl[:], ALU.mult, [b3, M_sl], [S["A0"]])
                            tt(S["G0"][:], S["NN1"][:, 0:128], ident[:], ALU.add, [S["NN1"], ident], [S["G0"]])
                            Ncur, Nb = S["NN1"][:, 0:128], S["NN1"]
                            Ab = [S["A0"], S["A1"]]
                            Nbufs = [S["N0"], S["N1"]]
                            Gb = [S["G0"], S["G1"]]
                            for i in range(5):
                                Acur = Ab[i % 2]
                                Anew = Ab[(i + 1) % 2]
                                bA = bank()
                                mm(bA[:, 0:128], Ncur, Acur[:], [Nb, Acur], [bA])
                                if i < 4:
                                    bN = bank()
                                    mm(bN[:, 0:128], Acur[:], Ncur, [Acur, Nb], [bN])
                                acopy(Anew[:], bA[:, 0:128], [bA], [Anew])
                                if i < 4:
                                    Nn = Nbufs[i % 2]
                                    acopy(Nn[:], bN[:, 0:128], [bN], [Nn])
                                bG = bank()
                                mm(bG[:, 0:128], Anew[:], Gb[i % 2][:], [Anew, Gb[i % 2]], [bG])
                                tt(Gb[(i + 1) % 2][:], bG[:, 0:128], Gb[i % 2][:], ALU.add, [bG, Gb[i % 2]], [Gb[(i + 1) % 2]])
                                if i < 4:
                                    Ncur, Nb = Nn[:], Nn
                            Gf = Gb[1]
                        bt = bank()
                        for ti, nm in enumerate(("BX", "KX", "VXf")):
                            PE(lambda h, bt=bt, ti=ti, src=S[nm]: h.transpose(out=bt[:, ti * 128:(ti + 1) * 128], in_=src[:], identity=ident[:]),
                               reads=[S[nm], ident], writes=[bt])
                        acopy(S["T3"][:], bt[:, 0:384], [bt], [S["T3"]])
                        Bt, Kt, VX = S["T3"][:, 0:128], S["T3"][:, 128:256], S["T3"][:, 256:384]
                        bR = bank()
                        mm(bR[:, 0:128], S["AR"][:, 0:128], P0[:], [S["AR"], P0], [bR], start=True, stop=False)
                        mm(bR[:, 0:128], S["NN2"][:, 0:128], VX, [S["NN2"], S["T3"]], [bR], start=False, stop=True)
                        vcopy(S["R0"][:], bR[:, 0:128], [bR], [S["R0"]])
                        if n > 1:
                            bU = bank()
                            mm(bU[:, 0:128], Gf[:], S["R0"][:], [Gf, S["R0"]], [bU])
                            vcopy(S["U"][:], bU[:, 0:128], [bU], [S["U"]])
                            Ub = S["U"]
                        else:
                            Ub = S["R0"]
                        bY = bank()
                        mm(bY[:, 0:128], P0[:], S["AR"][:, 128:256], [P0, S["AR"]], [bY], start=True, stop=False)
                        mm(bY[:, 0:128], Ub[:], S["NN1"][:, 128:256], [Ub, S["NN1"]], [bY], start=False, stop=False)
                        mm(bY[:, 0:128], VX, S["NN2"][:, 128:256], [S["T3"], S["NN2"]], [bY], start=False, stop=True)
                        for hh in range(2):
                            ps_ = slice(64 * hh, 64 * hh + 64)
                            acopy(yT[ps_, cols], bY[ps_, 64 * hh:64 * hh + n], [bY], [yT])
                        bP = bank()
                        mm(bP[:, 0:128], ident[:], P0[:], [ident, P0], [bP], start=True, stop=False)
                        mm(bP[:, 0:128], Bt, Ub[:], [S["T3"], Ub], [bP], start=False, stop=False)
                        mm(bP[:, 0:128], Kt, VX, [S["T3"]], [bP], start=False, stop=True)
                        wc = (u + 1) * n - 1
                        ts(P1[:], bP[:, 0:128], Wt[:, wc:wc + 1], None, ALU.mult, None, [bP, Wt], [P1])
                        if not prm:
                            bt1 = bank()
                            PE(lambda h, bt1=bt1, P1=P1: h.transpose(out=bt1[:, 0:128], in_=P1[:], identity=ident[:]), reads=[P1, ident], writes=[bt1])
                            so = SXo[u % 2]
                            vcopy(so[:], bt1[:, 0:128], [bt1], [so])
                            for hh in range(2):
                                sl = slice(64 * hh, 64 * hh + 64)
                                DS(lambda h, so=so, sl=sl, u=u, hh=hh, j=j: h.dma_start(out=OUT['wkv_s'][u, 2 * j + hh], in_=so[sl, sl]),
                                   reads=[so], writes=[OUT['wkv_s']])
                    bm = bank()
                    mm(bm[:, :N], blk1[:], yT[:, :N], [blk1, yT], [bm])
                    stt(t2[:, :N], bm[:, :N], -1.0 / 64, yT[:, :N], ALU.mult, ALU.add, [bm, yT], [t2])
                    act(t3[:, :N], t2[:, :N], AF.Square, [t2], [t3])
                    bv = bank()
                    mm(bv[:, :N], blk1[:], t3[:, :N], [blk1, t3], [bv])
                    act(t3[:, :N], bv[:, :N], AF.Sqrt, [bv, eps_gn], [t3], bias=eps_gn[:, 0:1], scale=1.0 / 64)
                    V(lambda h: h.reciprocal(out=t3[:, :N], in_=t3[:, :N]), reads=[t3], writes=[t3])
                    tt(t2[:, :N], t2[:, :N], t3[:, :N], ALU.mult, [t2, t3], [t2])
                    ts(t2[:, :N], t2[:, :N], cvec[:, LNG, j:j + 1], cvec[:, LNB, j:j + 1], ALU.mult, ALU.add, [t2, cvec], [t2])
                    tt(t2[:, :N], t2[:, :N], bonus[:, :N], ALU.add, [t2, bonus], [t2])
                    tt(yTin[:, 8 + j, :N], t2[:, :N], gate[:, :N], ALU.mult, [t2, gate], [yTin])
                    if prm and last:
                        bt1 = bank()
                        Pf = Pst[j][pcur[j]]
                        PE(lambda h, bt1=bt1, Pf=Pf: h.transpose(out=bt1[:, 0:128], in_=Pf[:], identity=ident[:]), reads=[Pf, ident], writes=[bt1])
                        vcopy(t3[:, 0:128], bt1[:, 0:128], [bt1], [t3])
                        for hh in range(2):
                            sl = slice(64 * hh, 64 * hh + 64)
                            DS(lambda h, sl=sl, hh=hh, j=j: h.dma_start(out=OUT['wkv_p'][2 * j + hh], in_=t3[sl, sl]), reads=[t3], writes=[OUT['wkv_p']])
                if prm and last:
                    DS(lambda h: h.dma_start(out=OUT['shift_p'][0, 0:3328].rearrange("(c p) -> p c", p=128), in_=lastcol[:, 0:26]),
                       reads=[lastcol], writes=[OUT['shift_p']])
                    DS(lambda h: h.dma_start(out=OUT['shift_p'][0, 3328:3360].rearrange("(c p) -> p c", p=32), in_=lastcol[0:32, 26:27]),
                       reads=[lastcol], writes=[OUT['shift_p']])
                if not prm:
                    for q in range(27):
                        rows = 128 if q < 26 else 32
                        DS(lambda h, q=q, rows=rows: h.dma_start(out=OUT['shift_s'][:, q * 128:q * 128 + rows].rearrange("b p -> p b"), in_=rawS[0:rows, q, :]),
                           reads=[rawS], writes=[OUT['shift_s']])
                fw.barrier()
            Wo = IN['ab_w_out'][0]
            for db in range(KC):
                bk = proj_fm(Wo, db * 128, 128, N, src=yTin)
                tt(xT[:, db, :N], xT[:, db, :N], bk[:, :N], ALU.add, [xT, bk], [xT])
            fw.barrier()

    Gq = fw.sb("Gq", [128, 64], F32)
    Gk = fw.sb("Gk", [128, 3, 64], F32)
    rdec = fw.sb("rdec", [128, 16], F32)
    S2 = fw.sb("S2", [128, 4, 128], F32)
    eps5 = fw.sb("eps5", [128, 1], F32)
    DS(lambda h: h.dma_start(out=Gq[:], in_=IN['nsa_q_norm'][0:1, :].to_broadcast([128, 64])), writes=[Gq])
    for i_ in range(3):
        DS(lambda h, i_=i_: h.dma_start(out=Gk[:, i_, :], in_=IN['nsa_k_norm'][0, i_:i_ + 1, :].to_broadcast([128, 64])), writes=[Gk])
    DS(lambda h: h.dma_start(out=rdec[:], in_=IN['ret_dec'][:, :]), writes=[rdec])
    V(lambda h: h.memset(S2[:], 0.0), writes=[S2])
    V(lambda h: h.memset(eps5[:], 1e-5), writes=[eps5])
    _lg = np.log1p(-np.exp2(-5.0 - np.arange(8, dtype=np.float32))).astype(np.float32)
    GAMMA_C = [float(np.exp(np.float32(128.0) * _lg[h_])) for h_ in range(8)]
    GAMMA_1 = [float(np.exp(_lg[h_])) for h_ in range(8)]

    def odd_mixer(kind, N, jt, last):
        Wc = IN['cd_w_in'][0]
        prm = (kind == "p")
        pos0 = jt * TT if prm else T
        rmsnorm(N, 4)
        nsub = (N + 127) // 128
        allsubs = [(s, min(128, N - s * 128)) for s in range(nsub)]
        with ExitStack() as es:
            oT = fw.sb("oT", [128, KC, N], BF16, es=es)
            wt2 = [fw.sb(f"wt2{i}", [128, KC, 512], BF16, es=es) for i in range(2)]
            wt2_i = [0]
            V(lambda h: h.memset(oT[:, 0:8, :], 0.0), writes=[oT])
            tA = fw.sb("tA", [128, 1024], F32, es=es)
            tS = fw.sb("tS", [128, 16], F32, es=es)
            tR = [fw.sb(f"tR{i}", [128, 256], F32, es=es) for i in range(4)]
            ropN = [fw.sb(f"ropN{i}", [128, 16], F32, es=es) for i in range(2)]
            ropR = [fw.sb(f"ropR{i}", [128, 64], F32, es=es) for i in range(2)]

            def rms_heads(dst3, src3, r, H, gain_ap, sbuf_, dbuf_):
                act(tA[0:r, 0:H * 64].rearrange("p (h d) -> p h d", d=64), src3, AF.Square, [sbuf_], [tA])
                V(lambda h: h.tensor_reduce(out=tS[0:r, 0:H], in_=tA[0:r, 0:H * 64].rearrange("p (h d) -> p h d", d=64), axis=AX.X, op=ALU.add),
                  reads=[tA], writes=[tS])
                act(tS[0:r, 0:H], tS[0:r, 0:H], AF.Sqrt, [tS, eps6], [tS], bias=eps6[0:r, 0:1], scale=1.0 / 64)
                V(lambda h: h.reciprocal(out=tS[0:r, 0:H], in_=tS[0:r, 0:H]), reads=[tS], writes=[tS])
                V(lambda h: h.tensor_tensor(out=dst3, in0=src3, in1=tS[0:r, 0:H].unsqueeze(2).to_broadcast([r, H, 64]), op=ALU.mult),
                  reads=[tS, sbuf_], writes=[dbuf_])
                V(lambda h: h.tensor_tensor(out=dst3, in0=dst3, in1=gain_ap.unsqueeze(1).to_broadcast([r, H, 64]), op=ALU.mult),
                  reads=[Gq, Gk, dbuf_], writes=[dbuf_])

            def rope_ip(x3, r, H, half, cs, sn, ropb, xbuf):
                x1 = x3[:, :, 0:half]
                x2 = x3[:, :, half:2 * half]
                cb = cs.unsqueeze(1).to_broadcast([r, H, half])
                sb_ = sn.unsqueeze(1).to_broadcast([r, H, half])
                tv = [t_[0:r, 0:H * half].rearrange("p (h e) -> p h e", e=half) for t_ in tR]
                crd = [ropb, xbuf]
                V(lambda h: h.tensor_tensor(out=tv[0], in0=x1, in1=cb, op=ALU.mult), reads=crd, writes=[tR[0]])
                V(lambda h: h.tensor_tensor(out=tv[1], in0=x2, in1=sb_, op=ALU.mult), reads=crd, writes=[tR[1]])
                V(lambda h: h.tensor_tensor(out=tv[2], in0=x2, in1=cb, op=ALU.mult), reads=crd, writes=[tR[2]])
                V(lambda h: h.tensor_tensor(out=tv[3], in0=x1, in1=sb_, op=ALU.mult), reads=crd, writes=[tR[3]])
                V(lambda h: h.tensor_tensor(out=x1, in0=tv[0], in1=tv[1], op=ALU.subtract), reads=[tR[0], tR[1]], writes=[xbuf])
                V(lambda h: h.tensor_tensor(out=x2, in0=tv[2], in1=tv[3], op=ALU.add), reads=[tR[2], tR[3]], writes=[xbuf])

            def group(col0, ncols, subs, fn):
                wt = wt2[wt2_i[0] % 2]
                wt2_i[0] += 1
                DG(lambda h: h.dma_start(out=wt[:, :, 0:ncols], in_=Wc[:, col0:col0 + ncols].rearrange("(kc p) n -> p kc n", p=128)), writes=[wt])
                for si, (s, r) in enumerate(subs):
                    bk = bank()
                    for kc in range(KC):
                        PE(lambda h, kc=kc, s=s, r=r, bk=bk: h.matmul(bk[0:r, 0:ncols], lhsT=hT[:, kc, s * 128:s * 128 + r], rhs=wt[:, kc, 0:ncols],
                                                                      start=(kc == 0), stop=(kc == KC - 1)),
                           reads=[wt, hT], writes=[bk], inc=(kc == KC - 1))
                    fn(si, s, r, bk)

            halves = [allsubs[0:2], allsubs[2:4]] if nsub == 4 else [allsubs]
            for subs in halves:
                for si, (s, r) in enumerate(subs):
                    if prm:
                        DS(lambda h, si=si, s=s, r=r: h.dma_start(out=ropN[si][0:r, :], in_=IN['rope_nsa'][pos0 + s * 128:pos0 + s * 128 + r, :]), writes=[ropN[si]])
                        DS(lambda h, si=si, s=s, r=r: h.dma_start(out=ropR[si][0:r, :], in_=IN['rope_ret'][pos0 + s * 128:pos0 + s * 128 + r, :]), writes=[ropR[si]])
                    else:
                        DS(lambda h, si=si, r=r: h.dma_start(out=ropN[si][0:r, :], in_=IN['rope_nsa'][T:T + 1, :].to_broadcast([r, 16])), writes=[ropN[si]])
                        DS(lambda h, si=si, r=r: h.dma_start(out=ropR[si][0:r, :], in_=IN['rope_ret'][T:T + 1, :].to_broadcast([r, 64])), writes=[ropR[si]])

                es_kv = ExitStack()
                rowb = [fw.sb(f"rowb{i}", [128, 1024], F32, es=es_kv) for i in range(2)]
                winb = [fw.sb(f"winb{i}", [128, 512], F32, es=es_kv) for i in range(2)]

                def f_kv0(si, s, r, bk):
                    acopy(rowb[si][0:r, 0:512], bk[0:r, 0:512], [bk], [rowb[si]])
                group(1024, 512, subs, f_kv0)

                def f_kv1(si, s, r, bk):
                    d3 = rowb[si][0:r, 512:768].rearrange("p (h d) -> p h d", d=64)
                    rms_heads(d3, bk[0:r, 0:256].rearrange("p (h d) -> p h d", d=64), r, 4, Gk[0:r, 1, :], bk, rowb[si])
                    rope_ip(d3, r, 4, 8, ropN[si][0:r, 0:8], ropN[si][0:r, 8:16], ropN[si], rowb[si])
                    acopy(rowb[si][0:r, 768:1024], bk[0:r, 256:512], [bk], [rowb[si]])
                    dst = OUT['kv_p'][pos0 + s * 128:pos0 + s * 128 + r, :] if prm else OUT['kv_s'][0:r, :]
                    DS(lambda h, si=si, r=r, dst=dst: h.dma_start(out=dst, in_=rowb[si][0:r, :]), reads=[rowb[si]], writes=[OUT['kv_p'], OUT['kv_s']])
                group(1536, 512, subs, f_kv1)

                def f_kvw(si, s, r, bk):
                    d3 = winb[si][0:r, 0:256].rearrange("p (h d) -> p h d", d=64)
                    rms_heads(d3, bk[0:r, 0:256].rearrange("p (h d) -> p h d", d=64), r, 4, Gk[0:r, 2, :], bk, winb[si])
                    rope_ip(d3, r, 4, 8, ropN[si][0:r, 0:8], ropN[si][0:r, 8:16], ropN[si], winb[si])
                    acopy(winb[si][0:r, 256:512], bk[0:r, 256:512], [bk], [winb[si]])
                    if prm and last:
                        DS(lambda h, si=si, s=s, r=r: h.dma_start(out=OUT['win_p'][s * 128:s * 128 + r, :], in_=winb[si][0:r, :]), reads=[winb[si]], writes=[OUT['win_p']])
                    if prm:
                        DS(lambda h, si=si, s=s, r=r: h.dma_start(out=WIN[pos0 + s * 128:pos0 + s * 128 + r, :], in_=winb[si][0:r, :]), reads=[winb[si]], writes=[WIN])
                    if not prm:
                        DS(lambda h, si=si, r=r: h.dma_start(out=OUT['win_s'][:, 511, :], in_=winb[si][0:r, :]), reads=[winb[si]], writes=[OUT['win_s']])
                        DS(lambda h: h.dma_start(out=OUT['win_s'][:, 0:511, :], in_=IN['cache_win'][:, 1:512, :]), writes=[OUT['win_s']])
                group(2048, 512, subs, f_kvw)
                fw.barrier()
                es_kv.close()
                es_rt = ExitStack()
                gng = fw.sb("gng", [128, 1024], F32, es=es_rt)
                gnb = fw.sb("gnb", [128, 1024], F32, es=es_rt)
                dmk = fw.sb("dmk", [128, 8, 128], F32, es=es_rt)
                DS(lambda h, gng=gng: h.dma_start(out=gng[:], in_=IN['ret_gn_g'][0:1, :].to_broadcast([128, 1024])), writes=[gng])
                DS(lambda h, gnb=gnb: h.dma_start(out=gnb[:], in_=IN['ret_gn_b'][0:1, :].to_broadcast([128, 1024])), writes=[gnb])
                for h_ in range(8):
                    DS(lambda h, h_=h_, dmk=dmk: h.dma_start(out=dmk[:, h_, :], in_=IN['ret_dmaskT'][h_]), writes=[dmk])
                rqb = [fw.sb(f"rqb{i}", [128, 512], F32, es=es_rt) for i in range(2)]
                rkb = [fw.sb(f"rkb{i}", [128, 512], F32, es=es_rt) for i in range(2)]
                rvb = [fw.sb(f"rvb{i}", [128, 1024], F32, es=es_rt) for i in range(2)]
                ynb = [fw.sb(f"ynb{i}", [128, 8, 128], F32, es=es_rt) for i in range(2)]
                qkT = [fw.sb(f"qkT{i}", [128, 4, 128], F32, es=es_rt) for i in range(2)]
                smb = [fw.sb(f"smb{i}", [128, 128], F32, es=es_rt) for i in range(2)]
                Asb = [fw.sb(f"Asb{i}", [128, 128], F32, es=es_rt) for i in range(2)]
                ktl = fw.sb("ktl", [128, 512], F32, es=es_rt)

                def f_rq(si, s, r, bk):
                    acopy(rqb[si][0:r, :], bk[0:r, 0:512], [bk], [rqb[si]])
                    rope_ip(rqb[si][0:r, :].rearrange("p (h d) -> p h d", d=64), r, 8, 32, ropR[si][0:r, 0:32], ropR[si][0:r, 32:64], ropR[si], rqb[si])
                group(2608, 512, subs, f_rq)

                def f_rk(si, s, r, bk):
                    A(lambda h, si=si, r=r, bk=bk: h.mul(out=rkb[si][0:r, :], in_=bk[0:r, 0:512], mul=0.125), reads=[bk], writes=[rkb[si]])
                    rope_ip(rkb[si][0:r, :].rearrange("p (h d) -> p h d", d=64), r, 8, 32, ropR[si][0:r, 0:32], ropR[si][0:r, 32:64], ropR[si], rkb[si])
                group(3120, 512, subs, f_rk)

                def f_rv0(si, s, r, bk):
                    acopy(rvb[si][0:r, 0:512], bk[0:r, 0:512], [bk], [rvb[si]])
                group(3632, 512, subs, f_rv0)

                def f_rv1(si, s, r, bk):
                    acopy(rvb[si][0:r, 512:1024], bk[0:r, 0:512], [bk], [rvb[si]])
                group(4144, 512, subs, f_rv1)

                if prm:
                    for si, (s, r) in enumerate(subs):
                        for wh, srcb in ((0, rqb[si]), (1, rkb[si])):
                            bt = bank()
                            for c4 in range(4):
                                PE(lambda h, bt=bt, c4=c4, srcb=srcb: h.transpose(out=bt[:, c4 * 128:(c4 + 1) * 128], in_=srcb[:, c4 * 128:(c4 + 1) * 128], identity=ident[:]),
                                   reads=[srcb, ident], writes=[bt])
                            acopy(qkT[wh][:, :, :], bt[:, :].rearrange("p (c n) -> p c n", n=128), [bt], [qkT[wh]])
                        qT_, kT_ = qkT
                        for hh_ in range(8):
                            V(lambda h, si=si, hh_=hh_: h.tensor_scalar(out=ktl[:, hh_ * 64:(hh_ + 1) * 64], in0=rkb[si][:, hh_ * 64:(hh_ + 1) * 64],
                                                                        scalar1=rdec[:, 8 + hh_:9 + hh_], scalar2=None, op0=ALU.mult),
                              reads=[rkb[si], rdec], writes=[ktl])
                        for h8 in range(8):
                            hp, hh = h8 // 2, h8 % 2
                            pr_ = slice(64 * hh, 64 * hh + 64)
                            bS = bank()
                            mm(bS[:, 0:128], kT_[pr_, hp, :], qT_[pr_, hp, :], [qkT[0], qkT[1]], [bS])
                            sm = smb[h8 % 2]
                            tt(sm[:], bS[:, 0:128], dmk[:, h8, :], ALU.mult, [bS, dmk], [sm])
                            bA = bank()
                            mm(bA[:, 0:128], sm[:], rvb[si][:, h8 * 128:(h8 + 1) * 128], [sm, rvb[si]], [bA])
                            bB = bank()
                            mm(bB[:, 0:128], qT_[pr_, hp, :], S2[pr_, hp, :], [qkT[0], S2], [bB])
                            As = Asb[h8 % 2]
                            acopy(As[:], bA[:, 0:128], [bA], [As])
                            stt(ynb[si][:, h8, :], bB[:, 0:128], rdec[:, h8:h8 + 1], As[:], ALU.mult, ALU.add, [bB, rdec, As], [ynb[si]])
                            bK = bank()
                            mm(bK[:, 0:128], ktl[:, hp * 128:(hp + 1) * 128], rvb[si][:, h8 * 128:(h8 + 1) * 128], [ktl, rvb[si]], [bK])
                            stt(S2[pr_, hp, :], S2[pr_, hp, :], GAMMA_C[h8], bK[pr_, 0:128], ALU.mult, ALU.add, [S2, bK], [S2])
                if not prm:
                    r = N
                    for nm_, srcb_ in (("q", rqb[0]), ("k", rkb[0])):
                        DS(lambda h, nm_=nm_, srcb_=srcb_: h.dma_start(out=SCR[nm_][:, :], in_=srcb_[0:r, :]), reads=[srcb_], writes=[SCR[nm_]])
                    DS(lambda h: h.dma_start(out=SCR["v"][:, :], in_=rvb[0][0:r, :]), reads=[rvb[0]], writes=[SCR["v"]])
                    es_s = ExitStack()
                    S0p = fw.sb("S0p", [128, NSMP, 128], F32, es=es_s)
                    vB = fw.sb("vB", [128, NSMP, 128], F32, es=es_s)
                    kTp = fw.sb("kTp", [128, NSMP], F32, es=es_s)
                    qTp = fw.sb("qTp", [128, NSMP], F32, es=es_s)
                    qTm = fw.sb("qTm", [128, NSMP, NSMP], F32, es=es_s)
                    qkp = fw.sb("qkp", [128, 8], F32, es=es_s)
                    V(lambda h: h.memset(qTm[:], 0.0), writes=[qTm])
                    tt(tA[0:r, 0:512], rqb[0][0:r, :], rkb[0][0:r, :], ALU.mult, [rqb[0], rkb[0]], [tA])
                    V(lambda h: h.tensor_reduce(out=qkp[0:r, 0:8], in_=tA[0:r, 0:512].rearrange("p (h d) -> p h d", d=64), axis=AX.X, op=ALU.add),
                      reads=[tA], writes=[qkp])
                    for hp in range(4):
                        for hh in range(2):
                            h8 = 2 * hp + hh
                            pr_ = slice(64 * hh, 64 * hh + 64)
                            DS(lambda h, h8=h8, pr_=pr_: h.dma_start(out=S0p[pr_, :, :], in_=IN['s_ret'][:, h8].rearrange("b d e -> d b e")), writes=[S0p])
                            DS(lambda h, h8=h8, pr_=pr_: h.dma_start(out=vB[pr_, :, :], in_=SCR["v"][:, h8 * 128:(h8 + 1) * 128].unsqueeze(0).to_broadcast([64, NSMP, 128])),
                               reads=[SCR["v"]], writes=[vB])
                        DS(lambda h, hp=hp: h.dma_start(out=kTp[:, :], in_=SCR["k"][:, hp * 128:(hp + 1) * 128].rearrange("b p -> p b")), reads=[SCR["k"]], writes=[kTp])
                        DS(lambda h, hp=hp: h.dma_start(out=qTp[:, :], in_=SCR["q"][:, hp * 128:(hp + 1) * 128].rearrange("b p -> p b")), reads=[SCR["q"]], writes=[qTp])
                        vcopy(qTm[:, :, :].rearrange("p a b -> p (a b)")[:, 0:NSMP * NSMP:NSMP + 1], qTp[:, :], [qTp], [qTm])
                        for hh in range(2):
                            h8 = 2 * hp + hh
                            pr_ = slice(64 * hh, 64 * hh + 64)
                            bq = bank()
                            for b_ in range(NSMP):
                                mm(bq[0:NSMP, 0:128], qTm[pr_, b_, :], S0p[pr_, b_, :], [qTm, S0p], [bq], start=(b_ == 0), stop=(b_ == NSMP - 1), inc=(b_ == NSMP - 1))
                            ts(tA[0:r, 0:128], rvb[0][0:r, h8 * 128:(h8 + 1) * 128], qkp[0:r, h8:h8 + 1], None, ALU.mult, None, [rvb[0], qkp], [tA])
                            stt(ynb[0][0:r, h8, :], bq[0:r, 0:128], GAMMA_1[h8], tA[0:r, 0:128], ALU.mult, ALU.add, [bq, tA], [ynb[0]])
                            ts(S0p[pr_, :, :], S0p[pr_, :, :], GAMMA_1[h8], None, ALU.mult, None, [S0p], [S0p])
                            for b_ in range(NSMP):
                                stt(vB[pr_, b_, :], vB[pr_, b_, :], kTp[pr_, b_:b_ + 1], S0p[pr_, b_, :], ALU.mult, ALU.add, [vB, kTp, S0p], [vB])
                            DS(lambda h, h8=h8, pr_=pr_: h.dma_start(out=OUT['ret_s'][:, h8].rearrange("b d e -> d b e"), in_=vB[pr_, :, :]), reads=[vB], writes=[OUT['ret_s']])
                    fw.barrier()
                    es_s.close()
                for si, (s, r) in enumerate(subs):
                    y3 = ynb[si][0:r, :, :]
                    V(lambda h, y3=y3, r=r: h.tensor_reduce(out=tS[0:r, 0:8], in_=y3, axis=AX.X, op=ALU.add), reads=[ynb[si]], writes=[tS])
                    ts(tS[0:r, 0:8], tS[0:r, 0:8], -1.0 / 128, None, ALU.mult, None, [tS], [tS])
                    V(lambda h, y3=y3, r=r: h.tensor_tensor(out=y3, in0=y3, in1=tS[0:r, 0:8].unsqueeze(2).to_broadcast([r, 8, 128]), op=ALU.add),
                      reads=[tS, ynb[si]], writes=[ynb[si]])
                    act(tA[0:r, :].rearrange("p (h d) -> p h d", d=128), y3, AF.Square, [ynb[si]], [tA])
                    V(lambda h, r=r: h.tensor_reduce(out=tS[0:r, 0:8], in_=tA[0:r, :].rearrange("p (h d) -> p h d", d=128), axis=AX.X, op=ALU.add),
                      reads=[tA], writes=[tS])
                    act(tS[0:r, 0:8], tS[0:r, 0:8], AF.Sqrt, [tS, eps5], [tS], bias=eps5[0:r, 0:1], scale=1.0 / 128)
                    V(lambda h, r=r: h.reciprocal(out=tS[0:r, 0:8], in_=tS[0:r, 0:8]), reads=[tS], writes=[tS])
                    V(lambda h, y3=y3, r=r: h.tensor_tensor(out=y3, in0=y3, in1=tS[0:r, 0:8].unsqueeze(2).to_broadcast([r, 8, 128]), op=ALU.mult),
                      reads=[tS, ynb[si]], writes=[ynb[si]])
                    y2 = ynb[si][0:r, :, :].rearrange("p h d -> p (h d)")
                    tt(y2, y2, gng[0:r, :], ALU.mult, [ynb[si], gng], [ynb[si]])
                    tt(y2, y2, gnb[0:r, :], ALU.add, [ynb[si], gnb], [ynb[si]])

                def f_rg(half_):
                    def f(si, s, r, bk):
                        act(tA[0:r, 0:512], bk[0:r, 0:512], AF.Silu, [bk], [tA])
                        y2 = ynb[si][0:r, :, :].rearrange("p h d -> p (h d)")[:, half_ * 512:(half_ + 1) * 512]
                        tt(y2, y2, tA[0:r, 0:512], ALU.mult, [ynb[si], tA], [ynb[si]])
                    return f
                group(4656, 512, subs, f_rg(0))
                group(5168, 512, subs, f_rg(1))
                for si, (s, r) in enumerate(subs):
                    y2 = ynb[si][:, :, :].rearrange("p h d -> p (h d)")
                    for c4 in range(2):
                        bt = bank()
                        for q in range(4):
                            c = c4 * 4 + q
                            PE(lambda h, bt=bt, q=q, c=c, r=r, y2=y2: h.transpose(out=bt[:, q * 128:q * 128 + r], in_=y2[0:r, c * 128:(c + 1) * 128], identity=ident[0:r, 0:r]),
                               reads=[ynb[si], ident], writes=[bt])
                        V(lambda h, bt=bt, c4=c4, s=s, r=r: h.tensor_copy(out=oT[:, 8 + c4 * 4:12 + c4 * 4, s * 128:s * 128 + r],
                                                                         in_=bt[:, :].rearrange("p (q n) -> p q n", n=128)[:, :, 0:r]), reads=[bt], writes=[oT])
                fw.barrier()
                es_rt.close()
            if NSA_ON and (prm or NSA_SAMPLE):
                fw.barrier()
                NQ = N if prm else 128
                nqs = NQ // 128
                q0 = jt * TT if prm else T
                qcoef = 1 if prm else 0
                es_n = ExitStack()
                qnT = fw.sb("qnT", [128, 8, NQ], BF16, es=es_n)
                qrT = fw.sb("qrT", [128, 8, NQ], BF16, es=es_n)
                gat = fw.sb("gat", [128, nqs, 48], F32, es=es_n)
                KcT = fw.sb("KcT", [128, 4, 128], BF16, es=es_n)
                Vc = fw.sb("Vc", [128, 4, 97], BF16, es=es_n)
                ropq = fw.sb("ropq", [128, nqs, 16], F32, es=es_n)
                qtm = [fw.sb(f"qtm{i}", [128, 512], F32, es=es_n) for i in range(2)]
                seltab = fw.sb("seltab", [128, nqs, 96], F32, es=es_n)
                ovc = fw.sb("ovc", [128, 33], F32, es=es_n)
                ones_bf = fw.sb("ones_bf", [128, NQ], BF16, es=es_n)
                zer_bf = fw.sb("zer_bf", [128, 128], BF16, es=es_n)
                cmask = fw.sb("cmask", [128, NQ], BF16, es=es_n)
                w2kf = fw.sb("w2kf", [128, 2, 64], F32, es=es_n)
                w2vf = fw.sb("w2vf", [128, 2, 64], F32, es=es_n)
                w2kd = fw.sb("w2kd", [128, 2, 128], BF16, es=es_n)
                w2v = fw.sb("w2v", [128, 2, 64], BF16, es=es_n)
                cst = fw.sb("cst", [128, 8], F32, es=es_n)
                b2vb = fw.sb("b2vb", [128, 64], F32, es=es_n)
                if not prm:
                    V(lambda h: h.memset(qnT[:], 0.0), writes=[qnT])
                    V(lambda h: h.memset(qrT[:], 0.0), writes=[qrT])
                    V(lambda h: h.memset(gat[:], 0.0), writes=[gat])
                    onsa_tot = fw.sb("onsa_tot", [128, 4, 256], F32, es=es_n)
                    V(lambda h: h.memset(onsa_tot[:], 0.0), writes=[onsa_tot])
                    row0m = fw.sb("row0m", [128, NQ], BF16, es=es_n)
                    ptb_i = fw.sb("ptb_i", [128, NSMP * 16], I32, es=es_n)
                    ptb_f = fw.sb("ptb_f", [128, NSMP * 16], F32, es=es_n)
                    pidx = fw.sb("pidx", [128, 1], F32, es=es_n)
                    idx_i = fw.sb("idx_i", [128, NSMP * 16], I32, es=es_n)
                    newrow = fw.sb("newrow", [128, 1024], F32, es=es_n)
                    DS(lambda h: h.dma_start(out=ptb_i[:], in_=IN['page_table'][:, :].rearrange("b k -> (b k)").unsqueeze(0).to_broadcast([128, NSMP * 16])), writes=[ptb_i])
                    vcopy(ptb_f[:], ptb_i[:], [ptb_i], [ptb_f])
                    G(lambda h: h.iota(pidx[:], pattern=[[0, 1]], base=0, channel_multiplier=1, allow_small_or_imprecise_dtypes=True), writes=[pidx])
                    ts(ptb_f[:], ptb_f[:], 128.0, pidx[:, 0:1], ALU.mult, ALU.add, [ptb_f, pidx], [ptb_f])
                    ts(ptb_f[:], ptb_f[:], 2.0, None, ALU.mult, None, [ptb_f], [ptb_f])
                    vcopy(idx_i[:], ptb_f[:], [ptb_f], [idx_i])
                    idx_b = fw.sb("idx_b", [128, NSMP * 16], I32, es=es_n)
                    ts(ptb_f[:], ptb_f[:], 1.0, None, ALU.add, None, [ptb_f], [ptb_f])
                    vcopy(idx_b[:], ptb_f[:], [ptb_f], [idx_b])
                    stg = [fw.sb(f"stg{i}", [128, 512], F32, es=es_n) for i in range(2)]
                    stg_i = [0]
                for s_ in range(nqs):
                    if prm:
                        DS(lambda h, s_=s_: h.dma_start(out=ropq[:, s_, :], in_=IN['rope_nsa'][q0 + s_ * 128:q0 + (s_ + 1) * 128, :]), writes=[ropq])
                        DS(lambda h, s_=s_: h.dma_start(out=seltab[:, s_, :], in_=IN['sel_tab'][jt * 4 + s_]), writes=[seltab])
                    else:
                        DS(lambda h: h.dma_start(out=ropq[:, 0, :], in_=IN['rope_nsa'][T:T + 1, :].to_broadcast([128, 16])), writes=[ropq])
                        DS(lambda h: h.dma_start(out=seltab[:, 0, :], in_=IN['sel_tab'][16]), writes=[seltab])
                DS(lambda h: h.dma_start(out=ovc[:], in_=IN['cmp_ov'][:, :]), writes=[ovc])
                V(lambda h: h.memset(ones_bf[:], 1.0), writes=[ones_bf])
                V(lambda h: h.memset(zer_bf[:], 0.0), writes=[zer_bf])
                G(lambda h: h.affine_select(out=cmask[:], in_=ones_bf[:], pattern=[[qcoef, NQ]], compare_op=ALU.is_ge, fill=0.0,
                                            base=q0 - 31, channel_multiplier=-16), reads=[ones_bf], writes=[cmask])
                if not prm:
                    G(lambda h: h.affine_select(out=row0m[:], in_=ones_bf[:], pattern=[[0, NQ]], compare_op=ALU.is_ge, fill=0.0,
                                                base=0, channel_multiplier=-1), reads=[ones_bf], writes=[row0m])
                DS(lambda h: h.dma_start(out=w2kf[:], in_=IN['cmp_k_w2'][0].rearrange("(c p) d -> p c d", p=128)), writes=[w2kf])
                DS(lambda h: h.dma_start(out=w2vf[:], in_=IN['cmp_v_w2'][0].rearrange("(c p) d -> p c d", p=128)), writes=[w2vf])
                for hc_ in range(2):
                    V(lambda h, hc_=hc_: h.tensor_copy(out=w2kd[:, hc_, :].rearrange("p (t d) -> p t d", d=64),
                                                       in_=w2kf[:, hc_, :].unsqueeze(1).to_broadcast([128, 2, 64])), reads=[w2kf], writes=[w2kd])
                vcopy(w2v[:], w2vf[:], [w2vf], [w2v])
                DS(lambda h: h.dma_start(out=cst[:, 0:2], in_=IN['cmp_k_b1'][0].rearrange("(c p) -> p c", p=128)), writes=[cst])
                DS(lambda h: h.dma_start(out=cst[:, 2:4], in_=IN['cmp_v_b1'][0].rearrange("(c p) -> p c", p=128)), writes=[cst])
                for hf_ in range(2):
                    DS(lambda h, hf_=hf_: h.dma_start(out=cst[64 * hf_:64 * hf_ + 64, 4:5], in_=IN['cmp_k_b2'][0].rearrange("(p c) -> p c", c=1)), writes=[cst])
                    DS(lambda h, hf_=hf_: h.dma_start(out=cst[64 * hf_:64 * hf_ + 64, 5:6], in_=IN['nsa_k_norm'][0, 0].rearrange("(p c) -> p c", c=1)), writes=[cst])
                DS(lambda h: h.dma_start(out=b2vb[:], in_=IN['cmp_v_b2'][0:1, :].to_broadcast([128, 64])), writes=[b2vb])

                def f_q(gi):
                    def f(si, s, r, bk):
                        qt = qtm[si % 2]
                        q3 = qt[0:r, :].rearrange("p (h d) -> p h d", d=64)
                        rms_heads(q3, bk[0:r, 0:512].rearrange("p (h d) -> p h d", d=64), r, 8, Gq[0:r, :], bk, qt)
                        for dstT in (qnT, qrT):
                            bt = bank()
                            for c4 in range(4):
                                PE(lambda h, bt=bt, c4=c4, qt=qt, r=r: h.transpose(out=bt[:, c4 * 128:c4 * 128 + r], in_=qt[0:r, c4 * 128:(c4 + 1) * 128], identity=ident[0:r, 0:r]),
                                   reads=[qt, ident], writes=[bt])
                            acopy(dstT[:, gi * 4:gi * 4 + 4, s * 128:s * 128 + r], bt[:, :].rearrange("p (c n) -> p c n", n=128)[:, :, 0:r], [bt], [dstT])
                            if dstT is qnT:
                                rope_ip(q3, r, 8, 8, ropq[0:r, s, 0:8], ropq[0:r, s, 8:16], ropq, qt)
                    return f
                group(0, 512, allsubs, f_q(0))
                group(512, 512, allsubs, f_q(1))

                def f_gt(si, s, r, bk):
                    act(gat[0:r, s, :], bk[0:r, 0:48], AF.Sigmoid, [bk], [gat])
                group(2560, 48, allsubs, f_gt)

                saved_pool = list(bank_pool)
                for bseq in ([None] if prm else list(range(NSMP))):
                    bank_pool[:] = saved_pool
                    if prm:
                        nkc = 4 * (jt + 1)
                        nks = nkc
                    else:
                        nkc = 16
                        nks = 17
                        V(lambda h: h.memset(newrow[:], 0.0), writes=[newrow])
                        DS(lambda h, bseq=bseq: h.dma_start(out=newrow[0:1, :], in_=OUT['kv_s'][bseq:bseq + 1, :]), reads=[OUT['kv_s']], writes=[newrow])
                    nch = 8 * nkc
                    ncmp = nch - 1

                    def load_rows(dst_ap, dbuf, kb, c0, c1, bseq=bseq):
                        if prm:
                            DS(lambda h: h.dma_start(out=dst_ap, in_=OUT['kv_p'][kb * 128:(kb + 1) * 128, c0:c1]), reads=[OUT['kv_p']], writes=[dbuf])
                        elif kb < 16:
                            col = bseq * 16 + kb
                            half = 0 if c0 < 512 else 1
                            ixt = idx_i if half == 0 else idx_b
                            if c1 - c0 == 512:
                                fw.dma("gpsimd", lambda h: h.indirect_dma_start(out=dst_ap, out_offset=None, in_=POOL2D[:, :],
                                                                               in_offset=bass.IndirectOffsetOnAxis(ap=ixt[:, col:col + 1], axis=0)),
                                       reads=[ixt], writes=[dbuf])
                            else:
                                st_ = stg[stg_i[0] % 2]
                                stg_i[0] += 1
                                fw.dma("gpsimd", lambda h: h.indirect_dma_start(out=st_[:], out_offset=None, in_=POOL2D[:, :],
                                                                               in_offset=bass.IndirectOffsetOnAxis(ap=ixt[:, col:col + 1], axis=0)),
                                       reads=[ixt], writes=[st_])
                                vcopy(dst_ap, st_[:, c0 - 512 * half:c1 - 512 * half], [st_], [dbuf])
                        else:
                            vcopy(dst_ap, newrow[:, c0:c1], [newrow], [dbuf])

                    es_c = ExitStack()
                    kvcT = fw.sb("kvcT", [128, 4, nkc * 128], BF16, es=es_c)
                    geluT = fw.sb("geluT", [128, 2, 2, 4, 128], BF16, es=es_c)
                    tokr = [fw.sb(f"tokr{i}", [128, 512], F32, es=es_c) for i in range(2)]
                    xs_ = fw.sb("xs_", [128, 128], F32, es=es_c)
                    xq_ = fw.sb("xq_", [128, 128], F32, es=es_c)
                    V(lambda h, geluT=geluT: h.memset(geluT[:], 0.0), writes=[geluT])
                    for kb in range(nkc):
                        tr_ = tokr[kb % 2]
                        load_rows(tr_[:], tr_, kb, 0, 512)
                        bt = bank()
                        for c4 in range(4):
                            PE(lambda h, bt=bt, c4=c4, tr_=tr_: h.transpose(out=bt[:, c4 * 128:(c4 + 1) * 128], in_=tr_[:, c4 * 128:(c4 + 1) * 128], identity=ident[:]),
                               reads=[tr_, ident], writes=[bt])
                        vcopy(kvcT[:, 0:4, kb * 128:(kb + 1) * 128], bt[:, :].rearrange("p (c n) -> p c n", n=128), [bt], [kvcT])
                    for kvi, wname in enumerate(('cmp_k_w1', 'cmp_v_w1')):
                        for rr in range(2):
                            for hf_ in range(2):
                                DG(lambda h, rr=rr, hf_=hf_, wname=wname: h.dma_start(out=wt2[rr][64 * hf_:64 * hf_ + 64, :, 0:256],
                                                                                      in_=IN[wname][0, rr].rearrange("(l d) h -> d l h", d=64)), writes=[wt2[rr]])
                        for g in range(4):
                            gh = g % 2
                            pr_ = slice(64 * gh, 64 * gh + 64)
                            for hc in range(2):
                                b0 = bank()
                                for rr in range(2):
                                    for l in range(16):
                                        mm(b0[:, rr * 128:rr * 128 + nch], wt2[rr][pr_, l, hc * 128:(hc + 1) * 128],
                                           kvcT[pr_, kvi * 2 + g // 2, l:nch * 16:16], [wt2[rr], kvcT], [b0], start=(l == 0), stop=(l == 15), inc=(l == 15))
                                ts(xs_[:, 0:ncmp], b0[:, 0:ncmp], cst[:, kvi * 2 + hc:kvi * 2 + hc + 1], None, ALU.add, None, [b0, cst], [xs_])
                                tt(xs_[:, 0:ncmp], xs_[:, 0:ncmp], b0[:, 129:129 + ncmp], ALU.add, [xs_, b0], [xs_])
                                act(geluT[:, kvi, hc, g, 0:ncmp], xs_[:, 0:ncmp], AF.Gelu_apprx_tanh, [xs_], [geluT])
                    for g in range(4):
                        bk_ = bank()
                        for hc in range(2):
                            mm(bk_[:, 0:128], w2kd[:, hc, :], geluT[:, 0, hc, g, :], [w2kd, geluT], [bk_], start=(hc == 0), stop=(hc == 1))
                        ts(xs_[:], bk_[:, 0:128], cst[:, 4:5], None, ALU.add, None, [bk_, cst], [xs_])
                        act(xq_[:], xs_[:], AF.Square, [xs_], [xq_])
                        bss = bank()
                        mm(bss[:, 0:128], blk1[:], xq_[:], [blk1, xq_], [bss])
                        act(xq_[:], bss[:, 0:128], AF.Sqrt, [bss, eps6], [xq_], bias=eps6[:, 0:1], scale=1.0 / 64)
                        V(lambda h, xq_=xq_: h.reciprocal(out=xq_[:], in_=xq_[:]), reads=[xq_], writes=[xq_])
                        tt(xs_[:], xs_[:], xq_[:], ALU.mult, [xs_, xq_], [xs_])
                        ts(KcT[:, g, :], xs_[:], cst[:, 5:6], None, ALU.mult, None, [xs_, cst], [KcT])
                        bv_ = bank()
                        for hc in range(2):
                            mm(bv_[:, 0:64], geluT[:, 1, hc, g, :], w2v[:, hc, :], [geluT, w2v], [bv_], start=(hc == 0), stop=(hc == 1))
                        tt(Vc[:, g, 0:64], bv_[:, 0:64], b2vb[:], ALU.add, [bv_, b2vb], [Vc])
                        vcopy(Vc[:, g, 64:97], ovc[:], [ovc], [Vc])
                    fw.barrier()
                    es_c.close()

                    es_a = ExitStack()
                    ebuf = [fw.sb(f"ebuf{i}", [128, NQ], BF16, es=es_a) for i in range(3)]
                    mskb = [fw.sb(f"mskb{i}", [128, NQ], BF16, es=es_a) for i in range(2)]
                    wmk = [fw.sb(f"wmk{i}", [128, NQ], BF16, es=es_a) for i in range(2)]
                    G3 = fw.sb("G3", [128, 32, 32], F32, es=es_a)
                    sel_all = fw.sb("sel_all", [128, nqs, 32], F32, es=es_a)
                    selx = fw.sb("selx", [128, nqs, 128], F32, es=es_a)
                    impg = fw.sb("impg", [128, nqs, 32], F32, es=es_a)
                    onsa = fw.sb("onsa", [128, nqs, 256], F32, es=es_a)
                    sc_ = fw.sb("sc_", [128, 32], F32, es=es_a)
                    cnt_ = fw.sb("cnt_", [128, 32], F32, es=es_a)
                    w4 = fw.sb("w4", [128, 4], F32, es=es_a)
                    w4g = fw.sb("w4g", [128, 4], F32, es=es_a)
                    ktk = [fw.sb(f"ktk{i}", [128, 64], F32, es=es_a) for i in range(2)]
                    vtk = [fw.sb(f"vtk{i}", [128, 64], F32, es=es_a) for i in range(2)]
                    ksd = [fw.sb(f"ksd{i}", [128, 128], F32, es=es_a) for i in range(2)]
                    kTb = [fw.sb(f"kTb{i}", [128, 128], BF16, es=es_a) for i in range(2)]
                    VSb = [fw.sb(f"VSb{i}", [128, 65], BF16, es=es_a) for i in range(2)]
                    for t_ in VSb:
                        V(lambda h, t_=t_: h.memset(t_[:], 1.0), writes=[t_])
                    bank_pool[:] = banks[0:4]
                    eb_i = [0]
                    kv_i = [0]

                    def combine(g, hi, br, first, with_imp):
                        h16 = 4 * g + hi
                        accb = banks[4 + hi]
                        a3 = accb[:, 0:nqs * 128].rearrange("p (s n) -> p s n", n=128)
                        ts(w4[:, 0:nqs], a3[:, :, 64], 1e-30, None, ALU.max, None, [accb], [w4])
                        V(lambda h: h.reciprocal(out=w4[:, 0:nqs], in_=w4[:, 0:nqs]), reads=[w4], writes=[w4])
                        tt(w4g[:, 0:nqs], w4[:, 0:nqs], gat[:, :, 3 * h16 + br], ALU.mult, [w4, gat], [w4g])
                        for qs in range(nqs):
                            dst = onsa[:, qs, hi * 64:(hi + 1) * 64]
                            if first:
                                ts(dst, a3[:, qs, 0:64], w4g[:, qs:qs + 1], None, ALU.mult, None, [accb, w4g], [onsa])
                            else:
                                stt(dst, a3[:, qs, 0:64], w4g[:, qs:qs + 1], dst, ALU.mult, ALU.add, [accb, w4g, onsa], [onsa])
                            if with_imp:
                                stt(impg[:, qs, :], a3[:, qs, 65:97], w4[:, qs:qs + 1], impg[:, qs, :], ALU.mult, ALU.add, [accb, w4, impg], [impg])

                    def attend_block(g, KT_ap_fn, KT_buf, Vblk, Vw, msk, qT_, first):
                        for hi in range(4):
                            h16 = 4 * g + hi
                            hh, hp = h16 % 2, h16 // 2
                            pr_ = slice(64 * hh, 64 * hh + 64)
                            bS = bank()
                            mm(bS[:, 0:NQ], KT_ap_fn(pr_), qT_[pr_, hp, :], [KT_buf, qT_], [bS])
                            e = ebuf[eb_i[0] % 3]
                            eb_i[0] += 1
                            act(e[:], bS[:, 0:NQ], AF.Exp, [bS], [e], scale=0.125)
                            if msk is not None:
                                tt(e[:], e[:], msk[:], ALU.mult, [e, msk], [e])
                            accb = banks[4 + hi]
                            if first:
                                for z_ in range(nqs):
                                    PE(lambda h, accb=accb, z_=z_: h.matmul(accb[:, z_ * 128:(z_ + 1) * 128], lhsT=zer_bf[:], rhs=ones_bf[:, 0:128], start=(z_ == 0), stop=True,
                                                                            skip_group_check=True),
                                       reads=[zer_bf, ones_bf], writes=[accb])
                            for qs in range(nqs):
                                PE(lambda h, accb=accb, qs=qs, e=e: h.matmul(accb[:, qs * 128:qs * 128 + Vw], lhsT=e[:, qs * 128:(qs + 1) * 128], rhs=Vblk,
                                                                             start=False, stop=True, skip_group_check=True),
                                   reads=[e, Vc, VSb[0], VSb[1]], writes=[accb])

                    def load_kv_block(loader):
                        i_ = kv_i[0] % 2
                        kv_i[0] += 1
                        loader(ktk[i_], vtk[i_])
                        V(lambda h: h.tensor_copy(out=ksd[i_][:, :].rearrange("p (t d) -> p t d", d=64), in_=ktk[i_][:, :].unsqueeze(1).to_broadcast([128, 2, 64])),
                          reads=[ktk[i_]], writes=[ksd[i_]])
                        bt = bank()
                        PE(lambda h: h.transpose(out=bt[:, 0:128], in_=ksd[i_][:], identity=ident[:]), reads=[ksd[i_], ident], writes=[bt])
                        acopy(kTb[i_][:], bt[:, 0:128], [bt], [kTb[i_]])
                        vcopy(VSb[i_][:, 0:64], vtk[i_][:], [vtk[i_]], [VSb[i_]])
                        return kTb[i_], VSb[i_]

                    for g in range(4):
                        V(lambda h, impg=impg: h.memset(impg[:], 0.0), writes=[impg])
                        attend_block(g, lambda pr_, g=g: KcT[pr_, g, :], KcT, Vc[:, g, 0:97], 97, cmask, qnT, True)
                        for hi in range(4):
                            combine(g, hi, 0, True, True)
                        for qs in range(nqs):
                            tt(sc_[:], impg[:, qs, :], seltab[:, qs, 0:32], ALU.mult, [impg, seltab], [sc_])
                            tt(sc_[:], sc_[:], seltab[:, qs, 32:64], ALU.add, [sc_, seltab], [sc_])
                            V(lambda h, G3=G3, sc_=sc_: h.tensor_tensor(out=G3[:], in0=sc_[:, :].unsqueeze(1).to_broadcast([128, 32, 32]),
                                                                        in1=sc_[:, :].unsqueeze(2).to_broadcast([128, 32, 32]), op=ALU.is_gt), reads=[sc_], writes=[G3])
                            V(lambda h, G3=G3, cnt_=cnt_: h.tensor_reduce(out=cnt_[:], in_=G3[:], axis=AX.X, op=ALU.add), reads=[G3], writes=[cnt_])
                            ts(cnt_[:], cnt_[:], 16.0 if prm else 15.0, None, ALU.is_lt, None, [cnt_], [cnt_])
                            tt(sel_all[:, qs, :], cnt_[:], seltab[:, qs, 64:96], ALU.mult, [cnt_, seltab], [sel_all])
                        for kb in range(nks):
                            def ld(kt, vt, kb=kb, g=g):
                                load_rows(kt[:], kt, kb, 512 + g * 64, 576 + g * 64)
                                load_rows(vt[:], vt, kb, 768 + g * 64, 832 + g * 64)
                            KT_, VS_ = load_kv_block(ld)
                            if kb < 16:
                                V(lambda h, kb=kb, selx=selx, sel_all=sel_all: h.tensor_copy(out=selx[:, :, :].rearrange("p s (t d) -> p s t d", d=64),
                                                                                             in_=sel_all[:, :, 2 * kb:2 * kb + 2].unsqueeze(3).to_broadcast([128, nqs, 2, 64])),
                                  reads=[sel_all], writes=[selx])
                                bm = bank()
                                for qs in range(nqs):
                                    PE(lambda h, bm=bm, qs=qs, selx=selx: h.transpose(out=bm[:, qs * 128:(qs + 1) * 128], in_=selx[:, qs, :], identity=ident[:]),
                                       reads=[selx, ident], writes=[bm])
                                mk_ = mskb[kb % 2]
                                if prm and kb >= 4 * jt:
                                    i_ = kb - 4 * jt
                                    wm_ = wmk[kb % 2]
                                    G(lambda h, wm_=wm_, i_=i_: h.affine_select(out=wm_[:], in_=ones_bf[:], pattern=[[1, NQ]], compare_op=ALU.is_ge, fill=0.0,
                                                                                base=-128 * i_, channel_multiplier=-1), reads=[ones_bf], writes=[wm_])
                                    tt(mk_[:], bm[:, 0:NQ], wm_[:], ALU.mult, [bm, wm_], [mk_])
                                else:
                                    acopy(mk_[:], bm[:, 0:NQ], [bm], [mk_])
                            else:
                                mk_ = row0m
                            attend_block(g, lambda pr_, KT_=KT_: KT_[pr_, :], KT_, VS_[:, 0:65], 65, mk_, qrT, kb == 0)
                        for hi in range(4):
                            combine(g, hi, 1, False, False)
                        if prm:
                            wlist = [i_ for i_ in range(8) if 4 * jt - 4 + i_ >= 0]
                        else:
                            wlist = [0, 1, 2, 3]
                        for i_ in wlist:
                            if prm:
                                kbw = 4 * jt - 4 + i_

                                def ldw(kt, vt, kbw=kbw, g=g):
                                    DS(lambda h: h.dma_start(out=kt[:], in_=WIN[kbw * 128:(kbw + 1) * 128, g * 64:g * 64 + 64]), reads=[WIN], writes=[kt])
                                    DS(lambda h: h.dma_start(out=vt[:], in_=WIN[kbw * 128:(kbw + 1) * 128, 256 + g * 64:320 + g * 64]), reads=[WIN], writes=[vt])
                                wm_ = wmk[i_ % 2]
                                if i_ < 4:
                                    G(lambda h, wm_=wm_, i_=i_: h.affine_select(out=wm_[:], in_=ones_bf[:], pattern=[[-1, NQ]], compare_op=ALU.is_ge, fill=0.0,
                                                                                base=128 * i_ - 1, channel_multiplier=1), reads=[ones_bf], writes=[wm_])
                                else:
                                    G(lambda h, wm_=wm_, i_=i_: h.affine_select(out=wm_[:], in_=ones_bf[:], pattern=[[1, NQ]], compare_op=ALU.is_ge, fill=0.0,
                                                                                base=512 - 128 * i_, channel_multiplier=-1), reads=[ones_bf], writes=[wm_])
                            else:
                                def ldw(kt, vt, i_=i_, g=g, bseq=bseq):
                                    DS(lambda h: h.dma_start(out=kt[:], in_=OUT['win_s'][bseq, i_ * 128:(i_ + 1) * 128, g * 64:g * 64 + 64]), reads=[OUT['win_s']], writes=[kt])
                                    DS(lambda h: h.dma_start(out=vt[:], in_=OUT['win_s'][bseq, i_ * 128:(i_ + 1) * 128, 256 + g * 64:320 + g * 64]), reads=[OUT['win_s']], writes=[vt])
                                wm_ = None
                            KT_, VS_ = load_kv_block(ldw)
                            attend_block(g, lambda pr_, KT_=KT_: KT_[pr_, :], KT_, VS_[:, 0:65], 65, wm_, qrT, i_ == wlist[0])
                        for hi in range(4):
                            combine(g, hi, 2, False, False)
                        if prm:
                            for qs in range(nqs):
                                bt = bank()
                                for c2 in range(2):
                                    PE(lambda h, bt=bt, c2=c2, qs=qs, onsa=onsa: h.transpose(out=bt[:, c2 * 128:(c2 + 1) * 128], in_=onsa[:, qs, c2 * 128:(c2 + 1) * 128], identity=ident[:]),
                                       reads=[onsa, ident], writes=[bt])
                                vcopy(oT[:, 2 * g:2 * g + 2, qs * 128:(qs + 1) * 128], bt[:, 0:256].rearrange("p (c n) -> p c n", n=128), [bt], [oT])
                        else:
                            stt(onsa_tot[:, g, :], onsa[:, 0, :], ident[:, bseq:bseq + 1], onsa_tot[:, g, :], ALU.mult, ALU.add, [onsa, ident, onsa_tot], [onsa_tot])
                    fw.barrier()
                    es_a.close()
                bank_pool[:] = saved_pool
                if not prm:
                    for g in range(4):
                        bt = bank()
                        for c2 in range(2):
                            PE(lambda h, bt=bt, c2=c2, g=g: h.transpose(out=bt[:, c2 * 128:c2 * 128 + NSMP], in_=onsa_tot[0:NSMP, g, c2 * 128:(c2 + 1) * 128], identity=ident[0:NSMP, 0:NSMP]),
                               reads=[onsa_tot, ident], writes=[bt])
                        vcopy(oT[:, 2 * g:2 * g + 2, 0:NSMP], bt[:, 0:256].rearrange("p (c n) -> p c n", n=128)[:, :, 0:NSMP], [bt], [oT])
                    fw.barrier()
                es_n.close()
            if prm and last:
                for h8 in range(8):
                    hp, hh = h8 // 2, h8 % 2
                    DS(lambda h, h8=h8, hp=hp, hh=hh: h.dma_start(out=OUT['ret_p'][h8], in_=S2[64 * hh:64 * hh + 64, hp, :]), reads=[S2], writes=[OUT['ret_p']])
            fw.barrier()
            Wo = IN['cd_w_out'][0]
            for db in range(KC):
                bk = proj_fm(Wo, db * 128, 128, N, src=oT)
                tt(xT[:, db, :N], xT[:, db, :N], bk[:, :N], ALU.add, [xT, bk], [xT])
            fw.barrier()

    tiles = [("p", j) for j in range(NTILE)] + [("s", 0)]
    if stage in (0, 1, 2):
        tiles = [("p", 0), ("s", 0)]
    for kind, j in tiles:
        if kind == "p":
            N = TT
            load_xT(IN['xp'][j * TT:(j + 1) * TT, :], N)
        else:
            N = NSMP
            load_xT(IN['xs'][:, :], N)
        last = (kind == "s") or (j == NTILE - 1) or stage in (0, 1, 2)
        ffn(N, 0, 1)
        if stage >= 1:
            even_mixer(kind, N, last)
        if stage >= 2:
            import os as _os
            if _os.environ.get("DBG_FFNMID", "1") == "1":
                ffn(N, 0, 2)
                ffn(N, 1, 1)
            if _os.environ.get("DBG_ODD_" + kind.upper(), "1") == "1":
                odd_mixer(kind, N, j, last)
        if stage >= 3:
            ffn(N, 1, 2)
        if kind == "p":
            store_xT(OUT['yp'][j * TT:(j + 1) * TT, :], OUT['yp'], N)
        else:
            store_xT(OUT['ys'][:, :], OUT['ys'], N)
    counts = fw.finish()
    return nc, counts


OUT_ORDER = ['yp', 'ys', 'lru_h_p', 'lru_h_s', 'lru_conv_p', 'lru_conv_s', 'shift_p', 'shift_s', 'wkv_p', 'wkv_s',
             'kv_p', 'kv_s', 'win_p', 'win_s', 'ret_p', 'ret_s']


def make_in_maps(inp, cores):
    maps = []
    consts = make_consts()
    f = lambda a: np.ascontiguousarray(np.asarray(a, dtype=np.float32))
    for c in cores:
        b = c % 4
        s0 = c * NSMP
        m = {
            'xp': f(inp['x_prompt'][b]),
            'xs': f(inp['x_sample'][s0:s0 + NSMP, 0]),
            's_lru_h': f(inp['state_lru_h'][0, s0:s0 + NSMP]),
            's_lru_conv': f(inp['state_lru_conv'][0, s0:s0 + NSMP]).reshape(NSMP * 3, 1024),
            's_shift': f(inp['state_rwkv_shift'][0, s0:s0 + NSMP]),
            's_wkv': f(inp['state_rwkv_wkv'][0, s0:s0 + NSMP]),
            'cache_kv': f(inp['cache_nsa_kv'][0]).reshape(NPOOL * 256, 512),
            'cache_win': f(inp['cache_nsa_win'][0, s0:s0 + NSMP]).reshape(NSMP, 512, 512),
            's_ret': f(inp['state_ret'][0, s0:s0 + NSMP]),
            'page_table': np.ascontiguousarray(np.asarray(inp['page_table'][s0:s0 + NSMP], dtype=np.int32)),
        }
        for k in WEIGHT_NAMES:
            m[k] = f(inp[k])
        m.update(consts)
        maps.append(m)
    return maps


_CACHE = {}


def kernel(**inputs):
    if 'nc' not in _CACHE:
        _CACHE['nc'] = build()[0]
    nc = _CACHE['nc']
    cores = list(range(8))
    maps = make_in_maps(inputs, cores)
    res = run_bass_kernel_spmd(nc, maps, core_ids=cores)
    R = res.results
    B = 4
    o = {}
    o['yp'] = np.stack([R[b]['yp'] for b in range(B)])
    o['ys'] = np.concatenate([R[c]['ys'] for c in cores])[:, None, :]
    o['lru_h_p'] = np.stack([R[b]['lru_h_p'][0] for b in range(B)])[None]
    o['lru_h_s'] = np.concatenate([R[c]['lru_h_s'] for c in cores])[None]
    o['lru_conv_p'] = np.stack([R[b]['lru_conv_p'] for b in range(B)])[None]
    o['lru_conv_s'] = np.concatenate([R[c]['lru_conv_s'].reshape(NSMP, 3, 1024) for c in cores])[None]
    o['shift_p'] = np.stack([R[b]['shift_p'][0] for b in range(B)])[None]
    o['shift_s'] = np.concatenate([R[c]['shift_s'] for c in cores])[None]
    o['wkv_p'] = np.stack([R[b]['wkv_p'] for b in range(B)])[None]
    o['wkv_s'] = np.concatenate([R[c]['wkv_s'] for c in cores])[None]
    o['kv_p'] = np.stack([R[b]['kv_p'].reshape(T, 4, 4, 64) for b in range(B)])[None]
    o['kv_s'] = np.concatenate([R[c]['kv_s'].reshape(NSMP, 1, 4, 4, 64) for c in cores])[None]
    o['win_p'] = np.stack([R[b]['win_p'].reshape(512, 2, 4, 64) for b in range(B)])[None]
    o['win_s'] = np.concatenate([R[c]['win_s'].reshape(NSMP, 512, 2, 4, 64) for c in cores])[None]
    o['ret_p'] = np.stack([R[b]['ret_p'] for b in range(B)])[None]
    o['ret_s'] = np.concatenate([R[c]['ret_s'] for c in cores])[None]
    return tuple(np.ascontiguousarray(o[k], dtype=np.float32) for k in OUT_ORDER)
```

```python
import math
import numpy as np
from contextlib import ExitStack
import concourse.bass as bass
import concourse.mybir as mybir
from concourse.bass_utils import run_bass_kernel_spmd

F32 = mybir.dt.float32
BF16 = mybir.dt.bfloat16
I32 = mybir.dt.int32
AF = mybir.ActivationFunctionType
ALU = mybir.AluOpType
AX = mybir.AxisListType

SAME_ENGINE_SYNC = True
SEM_ROTATE = 30000
N_DMA_SLOTS = 8


class Buf:
    __slots__ = ("name", "last_w", "readers", "t")

    def __init__(self, name, t=None):
        self.name = name
        self.last_w = None
        self.readers = []
        self.t = t

    def __getitem__(self, k):
        return self.t[k]


class Eng:
    def __init__(self, name):
        self.name = name
        self.ops = []
        self.known = {}
        self.cnt = 0
        self.sem_id = None
        self.slots = []
        self.slot_next = 0
        self.pending = False


def _compact(readers):
    m = {}
    for s, v in readers:
        m[s] = max(m.get(s, 0), v)
    return list(m.items())


class FW:
    def __init__(self, nc):
        self.nc = nc
        self.es = ExitStack()
        self.engs = {n: Eng(n) for n in ("sync", "scalar", "gpsimd", "vector", "tensor")}
        self.sems = {}
        self.nsem = 0
        self.nbuf = 0
        for e in self.engs.values():
            self._new_eng_sem(e)
        for qn in ("sync", "scalar", "gpsimd"):
            q = self.engs[qn]
            for i in range(N_DMA_SLOTS):
                q.slots.append([self._alloc_sem(f"d_{qn}_{i}"), 0])

    def _alloc_sem(self, name):
        h = self.es.enter_context(self.nc.semaphore(f"{name}_{self.nsem}"))
        sid = self.nsem
        self.nsem += 1
        self.sems[sid] = h
        return sid

    def _new_eng_sem(self, e):
        e.sem_id = self._alloc_sem(f"e_{e.name}")
        e.cnt = 0

    def sb(self, name, shape, dtype=F32, es=None):
        t = (es or self.es).enter_context(self.nc.sbuf_tensor(f"{name}_{self.nbuf}", list(shape), dtype))
        self.nbuf += 1
        return Buf(name, t)

    def ps(self, name, shape, dtype=F32):
        t = self.es.enter_context(self.nc.psum_tensor(f"{name}_{self.nbuf}", list(shape), dtype))
        self.nbuf += 1
        return Buf(name, t)

    def dram(self, name, shape, dtype=F32, kind="Internal"):
        t = self.nc.dram_tensor(name, list(shape), dtype, kind=kind)
        return Buf(name, t.ap())

    def _waits(self, e, reads, writes):
        need = {}
        for b in reads:
            if b.last_w is not None:
                s, v = b.last_w
                need[s] = max(need.get(s, 0), v)
        for b in writes:
            if b.last_w is not None:
                s, v = b.last_w
                need[s] = max(need.get(s, 0), v)
            for (s, v) in b.readers:
                need[s] = max(need.get(s, 0), v)
        for s, v in need.items():
            if s == e.sem_id and (not SAME_ENGINE_SYNC or v > e.cnt):
                continue
            if e.known.get(s, 0) >= v:
                continue
            e.known[s] = v
            e.ops.append(("w", s, v))

    def _record(self, ev, reads, writes):
        for b in writes:
            b.last_w = ev
            b.readers = []
        for b in reads:
            if b not in writes:
                b.readers.append(ev)
                if len(b.readers) > 48:
                    b.readers = _compact(b.readers)

    def op(self, eng, fn, reads=(), writes=(), inc=True):
        e = self.engs[eng]
        self._waits(e, reads, writes)
        if inc:
            if e.cnt >= SEM_ROTATE and not e.pending:
                self._new_eng_sem(e)
            e.cnt += 1
            e.pending = False
            e.ops.append(("o", fn, e.sem_id, 1))
            self._record((e.sem_id, e.cnt), reads, writes)
        else:
            e.pending = True
            e.ops.append(("o", fn, None, 0))
            self._record((e.sem_id, e.cnt + 1), reads, writes)

    def dma(self, q, fn, reads=(), writes=()):
        e = self.engs[q]
        self._waits(e, reads, writes)
        slot = e.slots[e.slot_next % N_DMA_SLOTS]
        e.slot_next += 1
        sid, val = slot
        if val > 0 and e.known.get(sid, 0) < val:
            e.known[sid] = val
            e.ops.append(("w", sid, val))
        slot[1] = val + 16
        e.ops.append(("o", fn, sid, 16))
        self._record((sid, val + 16), reads, writes)

    def barrier(self):
        evs = []
        for e in self.engs.values():
            if e.cnt > 0:
                evs.append((e.sem_id, e.cnt))
            for sid, val in e.slots:
                if val > 0:
                    evs.append((sid, val))
        for e in self.engs.values():
            assert not e.pending
            for s, v in evs:
                if s == e.sem_id:
                    continue
                if e.known.get(s, 0) < v:
                    e.known[s] = v
                    e.ops.append(("w", s, v))

    def finish(self):
        self.barrier()
        nc, sems, engs = self.nc, self.sems, self.engs

        def replay(name):
            def f(h):
                for o in engs[name].ops:
                    if o[0] == "w":
                        h.wait_ge(sems[o[1]], o[2])
                    else:
                        ins = o[1](h)
                        if o[2] is not None:
                            ins.then_inc(sems[o[2]], o[3])
            return f

        with nc.allow_non_contiguous_dma(reason="small strided param loads"), nc.Block() as block:
            block.sync(replay("sync"))
            block.scalar(replay("scalar"))
            block.gpsimd(replay("gpsimd"))
            block.vector(replay("vector"))
            block.tensor(replay("tensor"))
        counts = {n: len(x.ops) for n, x in engs.items()}
        self.es.close()
        return counts


D = 2048
DFF = 5504
T = 2048
TT = 512
NTILE = T // TT
NSMP = 16
KC = 16
FC = 43
LRU_W = 1024
RW = 1024
SHIFT_W = 3360
AB_COLS = 5408
CD_COLS = 5680
NPOOL = 2560

WEIGHT_NAMES = [
    'norm_ffn1', 'ffn1_w_in', 'ffn1_w_out', 'norm_mix', 'norm_ffn2', 'ffn2_w_in', 'ffn2_w_out',
    'ab_w_in', 'lru_conv_w', 'lru_conv_b', 'lru_wa', 'lru_ba', 'lru_wx', 'lru_bx', 'lru_lambda',
    'rwkv_mu', 'rwkv_w0', 'rwkv_w2', 'rwkv_a0', 'rwkv_a2', 'rwkv_g2', 'rwkv_k_k', 'rwkv_k_a', 'rwkv_r_k',
    'rwkv_ln_g', 'rwkv_ln_b', 'ab_w_out',
    'cd_w_in', 'nsa_q_norm', 'nsa_k_norm', 'cmp_k_w1', 'cmp_k_b1', 'cmp_k_w2', 'cmp_k_b2',
    'cmp_v_w1', 'cmp_v_b1', 'cmp_v_w2', 'cmp_v_b2', 'ret_gn_g', 'ret_gn_b', 'cd_w_out']

WEIGHT_SHAPES = {
    'norm_ffn1': (2, 2048), 'ffn1_w_in': (2, 2048, 11008), 'ffn1_w_out': (2, 5504, 2048), 'norm_mix': (2, 2048),
    'norm_ffn2': (2, 2048), 'ffn2_w_in': (2, 2048, 11008), 'ffn2_w_out': (2, 5504, 2048),
    'ab_w_in': (1, 2048, 5408), 'lru_conv_w': (1, 4, 1024), 'lru_conv_b': (1, 1024), 'lru_wa': (1, 16, 64, 64),
    'lru_ba': (1, 1024), 'lru_wx': (1, 16, 64, 64), 'lru_bx': (1, 1024), 'lru_lambda': (1, 1024),
    'rwkv_mu': (1, 3360), 'rwkv_w0': (1, 1024), 'rwkv_w2': (1, 64, 1024), 'rwkv_a0': (1, 1024),
    'rwkv_a2': (1, 64, 1024), 'rwkv_g2': (1, 160, 1024), 'rwkv_k_k': (1, 1024), 'rwkv_k_a': (1, 1024),
    'rwkv_r_k': (1, 16, 64), 'rwkv_ln_g': (1, 1024), 'rwkv_ln_b': (1, 1024), 'ab_w_out': (1, 2048, 2048),
    'cd_w_in': (1, 2048, 5680), 'nsa_q_norm': (1, 64), 'nsa_k_norm': (1, 3, 64),
    'cmp_k_w1': (1, 2, 1024, 256), 'cmp_k_b1': (1, 256), 'cmp_k_w2': (1, 256, 64), 'cmp_k_b2': (1, 64),
    'cmp_v_w1': (1, 2, 1024, 256), 'cmp_v_b1': (1, 256), 'cmp_v_w2': (1, 256, 64), 'cmp_v_b2': (1, 64),
    'ret_gn_g': (1, 1024), 'ret_gn_b': (1, 1024), 'cd_w_out': (1, 2048, 2048)}

STATE_SHAPES = {
    'xp': (T, D), 'xs': (NSMP, D), 's_lru_h': (NSMP, 1024), 's_lru_conv': (NSMP * 3, 1024),
    's_shift': (NSMP, SHIFT_W), 's_wkv': (NSMP, 16, 64, 64), 'cache_kv': (NPOOL * 256, 512),
    'cache_win': (NSMP, 512, 512), 's_ret': (NSMP, 8, 64, 128)}

CONST_SHAPES = {'rope_nsa': (T + 8, 16), 'rope_ret': (T + 8, 64), 'ret_dmaskT': (8, 128, 128), 'ret_dec': (128, 16),
                'sel_tab': (17, 128, 96), 'cmp_ov': (128, 33)}


def make_consts():
    f32 = np.float32
    pos = np.arange(T + 8, dtype=f32)

    def tab(half, theta):
        inv = np.exp(-np.log(f32(theta)) * np.arange(half, dtype=f32) / f32(half)).astype(f32)
        ang = (pos[:, None] * inv[None, :]).astype(f32)
        return np.concatenate([np.cos(ang), np.sin(ang)], axis=1).astype(f32)
    lg = np.log1p(-np.exp2(-5.0 - np.arange(8, dtype=f32))).astype(f32)
    i = np.arange(128, dtype=f32)
    diff = i[:, None] - i[None, :]
    dm = np.where(diff >= 0, np.exp(np.where(diff >= 0, diff, 0.0)[None] * lg[:, None, None]), 0.0).astype(f32)
    dec = np.zeros((128, 16), f32)
    dec[:, 0:8] = np.exp((i[:, None] + 1.0) * lg[None, :])
    dec[:, 8:16] = np.exp((128 - 1.0 - i)[:, None] * lg[None, :])
    sel = np.zeros((17, 128, 96), f32)
    sel[16, :, 0:32] = 1.0
    sel[16, :, [0, 31]] = 0.0
    sel[16, :, [32, 63]] = 1e4
    sel[16, :, 64:96] = 1.0
    for t_ in range(16):
        for p_ in range(128):
            qb = (128 * t_ + p_) // 64
            for j_ in range(32):
                valid = j_ <= qb
                forced = valid and (j_ == 0 or j_ == qb or j_ == qb - 1)
                sel[t_, p_, j_] = 1.0 if (valid and not forced) else 0.0
                sel[t_, p_, 32 + j_] = 1e4 if forced else (0.0 if valid else -1.0)
                sel[t_, p_, 64 + j_] = 1.0 if valid else 0.0
    ov = np.zeros((128, 33), f32)
    ov[:, 0] = 1.0
    for c_ in range(127):
        for j_ in range(32):
            o_ = min(16 * c_ + 32, 64 * j_ + 64) - max(16 * c_, 64 * j_)
            ov[c_, 1 + j_] = max(o_, 0) / 32.0
    return {'sel_tab': sel, 'cmp_ov': ov, 'rope_nsa': tab(8, 500000.0), 'rope_ret': tab(32, 10000.0),
            'ret_dmaskT': np.ascontiguousarray(dm.transpose(0, 2, 1)), 'ret_dec': dec}


OUT_SHAPES = {
    'yp': (T, D), 'ys': (NSMP, D), 'lru_h_p': (1, 1024), 'lru_h_s': (NSMP, 1024),
    'lru_conv_p': (3, 1024), 'lru_conv_s': (NSMP * 3, 1024), 'shift_p': (1, SHIFT_W), 'shift_s': (NSMP, SHIFT_W),
    'wkv_p': (16, 64, 64), 'wkv_s': (NSMP, 16, 64, 64), 'kv_p': (T, 1024), 'kv_s': (NSMP, 1024),
    'win_p': (512, 512), 'win_s': (NSMP, 512, 512), 'ret_p': (8, 64, 128), 'ret_s': (NSMP, 8, 64, 128)}


def build(stage=99):
    nc = bass.Bass("TRN2", target_bir_lowering=False)
    fw = FW(nc)
    IN = {}
    for k, s in STATE_SHAPES.items():
        IN[k] = fw.dram(k, s, F32, kind="ExternalInput")
    IN['page_table'] = fw.dram('page_table', (NSMP, 16), I32, kind="ExternalInput")
    for k_, s_ in CONST_SHAPES.items():
        IN[k_] = fw.dram(k_, s_, F32, kind="ExternalInput")
    for k in WEIGHT_NAMES:
        IN[k] = fw.dram(k, WEIGHT_SHAPES[k], F32, kind="ExternalInput")
    OUT = {k: fw.dram(k, s, F32, kind="ExternalOutput") for k, s in OUT_SHAPES.items()}
    WIN = fw.dram('scr_win', (T, 512))
    NSA_ON = True
    NSA_SAMPLE = True
    POOL2D = IN['cache_kv']
    SCR = {'q': fw.dram('scr_q', (NSMP, 512)), 'k': fw.dram('scr_k', (NSMP, 512)), 'v': fw.dram('scr_v', (NSMP, 1024))}

    def V(fn, reads=(), writes=(), inc=True):
        fw.op("vector", fn, reads, writes, inc)

    def A(fn, reads=(), writes=(), inc=True):
        fw.op("scalar", fn, reads, writes, inc)

    def PE(fn, reads=(), writes=(), inc=True):
        fw.op("tensor", fn, reads, writes, inc)

    def G(fn, reads=(), writes=(), inc=True):
        fw.op("gpsimd", fn, reads, writes, inc)

    def DS(fn, reads=(), writes=()):
        fw.dma("sync", fn, reads, writes)

    def DG(fn, reads=(), writes=()):
        fw.dma("gpsimd", fn, reads, writes)

    xT = fw.sb("xT", [128, KC, TT], F32)
    hT = fw.sb("hT", [128, KC, TT], BF16)
    banks = [fw.ps(f"bank{i}", [128, 512], F32) for i in range(8)]
    bank_i = [0]
    bank_pool = list(banks)

    def bank():
        b = bank_pool[bank_i[0] % len(bank_pool)]
        bank_i[0] += 1
        return b

    wsm = [fw.sb(f"wsm{i}", [128, KC, 128], BF16) for i in range(3)]
    wsm_i = [0]

    def proj_fm(w_ap, col0, ncols, N, src=None):
        src = src or hT
        wt = wsm[wsm_i[0] % 3]
        wsm_i[0] += 1
        DG(lambda h: h.dma_start(out=wt[:, :, 0:ncols],
                                 in_=w_ap[:, col0:col0 + ncols].rearrange("(kc p) n -> p kc n", p=128)),
           reads=[], writes=[wt])
        bk = bank()
        for kc in range(KC):
            PE(lambda h, kc=kc: h.matmul(bk[0:ncols, :N], lhsT=wt[:, kc, 0:ncols], rhs=src[:, kc, :N],
                                         start=(kc == 0), stop=(kc == KC - 1)),
               reads=[wt, src], writes=[bk], inc=(kc == KC - 1))
        return bk

    ident = fw.sb("ident", [128, 128], F32)
    ones_f = fw.sb("ones_f", [128, 128], F32)
    eps6 = fw.sb("eps6", [128, 1], F32)
    gains = fw.sb("gains", [128, 6, KC], F32)
    rstd = fw.sb("rstd", [128, TT], F32)
    sqb = [fw.sb(f"sqb{i}", [128, TT], F32) for i in range(2)]

    G(lambda h: h.memset(ident[:], 1.0), writes=[ident])
    G(lambda h: h.affine_select(out=ident[:], in_=ident[:], pattern=[[-1, 128]], compare_op=ALU.is_equal,
                                fill=0.0, base=0, channel_multiplier=1), reads=[ident], writes=[ident])
    V(lambda h: h.memset(ones_f[:], 1.0), writes=[ones_f])
    V(lambda h: h.memset(eps6[:], 1e-6), writes=[eps6])
    with nc.allow_non_contiguous_dma(reason="small param vectors"):
        for li in range(2):
            for ni, nm in enumerate(("norm_ffn1", "norm_mix", "norm_ffn2")):
                DS(lambda h, li=li, ni=ni, nm=nm: h.dma_start(
                    out=gains[:, li * 3 + ni, :], in_=IN[nm][li, :].rearrange("(kc p) -> p kc", p=128)),
                    reads=[IN[nm]], writes=[gains])

    def load_xT(src_rows, N):
        nsub = (N + 127) // 128
        es_ = ExitStack()
        tok = [fw.sb(f"tok{i}", [128, D], F32, es=es_) for i in range(2)]
        for s in range(nsub):
            r = min(128, N - s * 128)
            tb = tok[s % 2]
            DS(lambda h, tb=tb, s=s, r=r: h.dma_start(out=tb[0:r, :], in_=src_rows[s * 128:s * 128 + r, :]),
               reads=[IN['xp'], IN['xs']], writes=[tb])
            for kc4 in range(4):
                bk = bank()
                for q in range(4):
                    kc = kc4 * 4 + q
                    PE(lambda h, bk=bk, tb=tb, kc=kc, q=q, r=r: h.transpose(
                        out=bk[:, q * 128:q * 128 + r], in_=tb[0:r, kc * 128:(kc + 1) * 128], identity=ident[0:r, 0:r]),
                        reads=[tb, ident], writes=[bk])
                V(lambda h, bk=bk, kc4=kc4, s=s, r=r: h.tensor_copy(
                    out=xT[:, kc4 * 4:kc4 * 4 + 4, s * 128:s * 128 + r],
                    in_=bk[:, :].rearrange("p (q n) -> p q n", n=128)[:, :, 0:r]), reads=[bk], writes=[xT])
        fw.barrier()
        es_.close()

    def store_xT(dst_rows, dst_buf, N):
        nsub = (N + 127) // 128
        es_ = ExitStack()
        tok = [fw.sb(f"tok{i}", [128, D], F32, es=es_) for i in range(2)]
        for s in range(nsub):
            r = min(128, N - s * 128)
            tb = tok[s % 2]
            for kc4 in range(4):
                bk = bank()
                for q in range(4):
                    kc = kc4 * 4 + q
                    PE(lambda h, bk=bk, kc=kc, q=q, r=r, s=s: h.transpose(
                        out=bk[0:r, q * 128:(q + 1) * 128], in_=xT[:, kc, s * 128:s * 128 + r], identity=ident[:]),
                        reads=[xT, ident], writes=[bk])
                A(lambda h, bk=bk, kc4=kc4, r=r, tb=tb: h.copy(out=tb[0:r, kc4 * 512:(kc4 + 1) * 512], in_=bk[0:r, :]),
                  reads=[bk], writes=[tb])
            DS(lambda h, tb=tb, s=s, r=r: h.dma_start(out=dst_rows[s * 128:s * 128 + r, :], in_=tb[0:r, :]),
               reads=[tb], writes=[dst_buf])
        fw.barrier()
        es_.close()

    def rmsnorm(N, gi):
        bk = bank()
        for kc in range(KC):
            sq = sqb[kc % 2]
            A(lambda h, sq=sq, kc=kc: h.activation(out=sq[:, :N], in_=xT[:, kc, :N], func=AF.Square),
              reads=[xT], writes=[sq])
            PE(lambda h, sq=sq, kc=kc, bk=bk: h.matmul(bk[:, :N], lhsT=ones_f[:], rhs=sq[:, :N],
                                                     start=(kc == 0), stop=(kc == KC - 1)),
               reads=[sq, ones_f], writes=[bk])
        A(lambda h, bk=bk: h.activation(out=rstd[:, :N], in_=bk[:, :N], func=AF.Sqrt, scale=1.0 / D, bias=eps6[:, 0:1]),
          reads=[bk, eps6], writes=[rstd])
        V(lambda h: h.reciprocal(out=rstd[:, :N], in_=rstd[:, :N]), reads=[rstd], writes=[rstd])
        for kc in range(KC):
            V(lambda h, kc=kc: h.scalar_tensor_tensor(out=hT[:, kc, :N], in0=xT[:, kc, :N],
                                                      scalar=gains[:, gi, kc:kc + 1], in1=rstd[:, :N],
                                                      op0=ALU.mult, op1=ALU.mult),
              reads=[xT, gains, rstd], writes=[hT])

    def ffn(N, layer, which):
        gi = layer * 3 + (0 if which == 1 else 2)
        w_in = IN[f'ffn{which}_w_in']
        w_out = IN[f'ffn{which}_w_out']
        rmsnorm(N, gi)
        with ExitStack() as es:
            actT = fw.sb("actT", [128, FC, N], BF16, es=es)
            sg = [fw.sb(f"sg{i}", [128, N], F32, es=es) for i in range(2)]
            wo = [fw.sb(f"wo{i}", [128, FC, 128], BF16, es=es) for i in range(2)]
            wbufs = []
            for i in range(2):
                t = fw.sb(f"wbuf{i}", [128, KC, 256], BF16, es=es)
                wbufs.append({"t": t, "a": Buf("wa"), "b": Buf("wb")})
            wb_i = [0]

            def wbuf():
                w = wbufs[wb_i[0] % 2]
                wb_i[0] += 1
                return w
            for fb in range(FC):
                f0 = fb * 128
                w = wbuf()
                wt = w["t"]
                DG(lambda h, wt=wt, f0=f0: h.dma_start(
                    out=wt[:, :, 0:128], in_=w_in[layer, :, f0:f0 + 128].rearrange("(kc p) n -> p kc n", p=128)),
                    writes=[w["a"]])
                DG(lambda h, wt=wt, f0=f0: h.dma_start(
                    out=wt[:, :, 128:256],
                    in_=w_in[layer, :, DFF + f0:DFF + f0 + 128].rearrange("(kc p) n -> p kc n", p=128)),
                    writes=[w["b"]])
                bg, bu = bank(), bank()
                for kc in range(KC):
                    PE(lambda h, bg=bg, wt=wt, kc=kc: h.matmul(
                        bg[:, :N], lhsT=wt[:, kc, 0:128], rhs=hT[:, kc, :N],
                        start=(kc == 0), stop=(kc == KC - 1)),
                        reads=[w["a"], hT], writes=[bg], inc=(kc == KC - 1))
                for kc in range(KC):
                    PE(lambda h, bu=bu, wt=wt, kc=kc: h.matmul(
                        bu[:, :N], lhsT=wt[:, kc, 128:256], rhs=hT[:, kc, :N],
                        start=(kc == 0), stop=(kc == KC - 1)),
                        reads=[w["b"], hT], writes=[bu], inc=(kc == KC - 1))
                s_ = sg[fb % 2]
                A(lambda h, s_=s_, bg=bg: h.activation(out=s_[:, :N], in_=bg[:, :N], func=AF.Silu),
                  reads=[bg], writes=[s_])
                V(lambda h, s_=s_, bu=bu, fb=fb: h.tensor_tensor(out=actT[:, fb, :N], in0=s_[:, :N], in1=bu[:, :N],
                                                                 op=ALU.mult),
                  reads=[s_, bu], writes=[actT])
            for db in range(KC):
                w2 = wo[db % 2]
                DG(lambda h, w2=w2, db=db: h.dma_start(
                    out=w2[:], in_=w_out[layer, :, db * 128:(db + 1) * 128].rearrange("(fc p) n -> p fc n", p=128)),
                    reads=[w_out], writes=[w2])
                bk = bank()
                for fc in range(FC):
                    PE(lambda h, bk=bk, w2=w2, fc=fc: h.matmul(bk[:, :N], lhsT=w2[:, fc, :], rhs=actT[:, fc, :N],
                                                             start=(fc == 0), stop=(fc == FC - 1)),
                       reads=[w2, actT], writes=[bk], inc=(fc == FC - 1))
                V(lambda h, bk=bk, db=db: h.scalar_tensor_tensor(out=xT[:, db, :N], in0=bk[:, :N], scalar=0.5,
                                                                 in1=xT[:, db, :N], op0=ALU.mult, op1=ALU.add),
                  reads=[bk, xT], writes=[xT])
            fw.barrier()

    CB, BA, BX_, NSP, W0, A0, KK_, KA, LNG, LNB, RK, CW0 = 0, 1, 2, 3, 4, 5, 6, 7, 8, 9, 10, 11
    cvec = fw.sb("cvec", [128, 15, 8], F32)
    muT = fw.sb("muT", [128, 27], F32)
    blk1 = fw.sb("blk1", [128, 128], F32)
    M_ar = fw.sb("M_ar", [128, 256], F32)
    M_sl = fw.sb("M_sl", [128, 128], F32)
    eps_gn = fw.sb("eps_gn", [128, 1], F32)
    ones_tt = fw.sb("ones_tt", [128, TT], F32)
    convtail = fw.sb("convtail", [128, 8, 3], F32)
    hprev = fw.sb("hprev", [128, 8], F32)
    lastcol = fw.sb("lastcol", [128, 27], F32)
    Pst = [[fw.sb(f"Pst{j}_{k}", [128, 128], F32) for k in range(2)] for j in range(8)]
    pcur = [0] * 8

    def vload(idx, ap1d):
        DS(lambda h: h.dma_start(out=cvec[:, idx, :], in_=ap1d.rearrange("(c p) -> p c", p=128)), writes=[cvec])

    vload(CB, IN['lru_conv_b'][0, :])
    vload(BA, IN['lru_ba'][0, :])
    vload(BX_, IN['lru_bx'][0, :])
    vload(NSP, IN['lru_lambda'][0, :])
    vload(W0, IN['rwkv_w0'][0, :])
    vload(A0, IN['rwkv_a0'][0, :])
    vload(KK_, IN['rwkv_k_k'][0, :])
    vload(KA, IN['rwkv_k_a'][0, :])
    vload(LNG, IN['rwkv_ln_g'][0, :])
    vload(LNB, IN['rwkv_ln_b'][0, :])
    vload(RK, IN['rwkv_r_k'][0].rearrange("h d -> (h d)"))
    for j_ in range(4):
        vload(CW0 + j_, IN['lru_conv_w'][0, j_, :])
    A(lambda h: h.activation(out=cvec[:, NSP, :], in_=cvec[:, NSP, :], func=AF.Exp, scale=-1.0), reads=[cvec], writes=[cvec])
    A(lambda h: h.activation(out=cvec[:, NSP, :], in_=cvec[:, NSP, :], func=AF.Ln, bias=ones_f[:, 0:1]), reads=[cvec, ones_f], writes=[cvec])
    V(lambda h: h.tensor_scalar(out=cvec[:, NSP, :], in0=cvec[:, NSP, :], scalar1=-8.0, scalar2=None, op0=ALU.mult), reads=[cvec], writes=[cvec])
    V(lambda h: h.memset(muT[:], 0.0), writes=[muT])
    DS(lambda h: h.dma_start(out=muT[:, 0:26], in_=IN['rwkv_mu'][0, 0:3328].rearrange("(c p) -> p c", p=128)), writes=[muT])
    DS(lambda h: h.dma_start(out=muT[0:32, 26:27], in_=IN['rwkv_mu'][0, 3328:3360].rearrange("(c p) -> p c", p=32)), writes=[muT])
    V(lambda h: h.memset(blk1[:], 0.0), writes=[blk1])
    V(lambda h: h.memset(blk1[0:64, 0:64], 1.0), writes=[blk1])
    V(lambda h: h.memset(blk1[64:128, 64:128], 1.0), writes=[blk1])
    G(lambda h: h.affine_select(out=M_ar[:, 0:128], in_=blk1[:], pattern=[[1, 128]], compare_op=ALU.is_gt, fill=0.0,
                                base=0, channel_multiplier=-1), reads=[blk1], writes=[M_ar])
    G(lambda h: h.affine_select(out=M_ar[:, 128:256], in_=blk1[:], pattern=[[1, 128]], compare_op=ALU.is_ge, fill=0.0,
                                base=0, channel_multiplier=-1), reads=[blk1], writes=[M_ar])
    G(lambda h: h.affine_select(out=M_sl[:], in_=blk1[:], pattern=[[-1, 128]], compare_op=ALU.is_gt, fill=0.0,
                                base=0, channel_multiplier=1), reads=[blk1], writes=[M_sl])
    V(lambda h: h.memset(eps_gn[:], 64e-5), writes=[eps_gn])
    V(lambda h: h.memset(ones_tt[:], 1.0), writes=[ones_tt])
    V(lambda h: h.memset(convtail[:], 0.0), writes=[convtail])
    V(lambda h: h.memset(hprev[:], 0.0), writes=[hprev])
    V(lambda h: h.memset(lastcol[:], 0.0), writes=[lastcol])
    for j_ in range(8):
        V(lambda h, j_=j_: h.memset(Pst[j_][0][:], 0.0), writes=[Pst[j_][0]])

    def tt(out, in0, in1, op, reads, writes, eng=None):
        (eng or V)(lambda h: h.tensor_tensor(out=out, in0=in0, in1=in1, op=op), reads=reads, writes=writes)

    def stt(out, in0, scalar, in1, op0, op1, reads, writes):
        V(lambda h: h.scalar_tensor_tensor(out=out, in0=in0, scalar=scalar, in1=in1, op0=op0, op1=op1), reads=reads, writes=writes)

    def ts(out, in0, s1, s2, op0, op1, reads, writes):
        if s2 is None:
            V(lambda h: h.tensor_scalar(out=out, in0=in0, scalar1=s1, scalar2=None, op0=op0), reads=reads, writes=writes)
        else:
            V(lambda h: h.tensor_scalar(out=out, in0=in0, scalar1=s1, scalar2=s2, op0=op0, op1=op1), reads=reads, writes=writes)

    def act(out, in_, func, reads, writes, bias=None, scale=None):
        kw = {}
        if bias is not None:
            kw["bias"] = bias
        if scale is not None:
            kw["scale"] = scale
        A(lambda h: h.activation(out=out, in_=in_, func=func, **kw), reads=reads, writes=writes)

    def mm(out, lhsT, rhs, reads, writes, start=True, stop=True, inc=True):
        PE(lambda h: h.matmul(out, lhsT=lhsT, rhs=rhs, start=start, stop=stop), reads=reads, writes=writes, inc=inc)

    def vcopy(out, in_, reads, writes):
        V(lambda h: h.tensor_copy(out=out, in_=in_), reads=reads, writes=writes)

    def acopy(out, in_, reads, writes):
        A(lambda h: h.copy(out=out, in_=in_), reads=reads, writes=writes)

    def even_mixer(kind, N, last):
        W = IN['ab_w_in'][0]
        prm = (kind == "p")
        n = 64 if prm else 1
        NU = N // n
        rmsnorm(N, 1)
        with ExitStack() as es:
            yTin = fw.sb("yTin", [128, KC, N], BF16, es=es)
            w2a2 = fw.sb("w2a2", [128, 1024], F32, es=es)
            g2a = fw.sb("g2a", [128, 1024], F32, es=es)
            g2b = fw.sb("g2b", [32, 1024], F32, es=es)
            WAbd = fw.sb("WAbd", [128, 8, 128], F32, es=es)
            WXbd = fw.sb("WXbd", [128, 8, 128], F32, es=es)
            DS(lambda h: h.dma_start(out=w2a2[0:64, :], in_=IN['rwkv_w2'][0]), writes=[w2a2])
            DS(lambda h: h.dma_start(out=w2a2[64:128, :], in_=IN['rwkv_a2'][0]), writes=[w2a2])
            DS(lambda h: h.dma_start(out=g2a[:], in_=IN['rwkv_g2'][0, 0:128, :]), writes=[g2a])
            DS(lambda h: h.dma_start(out=g2b[:], in_=IN['rwkv_g2'][0, 128:160, :]), writes=[g2b])
            V(lambda h: h.memset(WAbd[:], 0.0), writes=[WAbd])
            V(lambda h: h.memset(WXbd[:], 0.0), writes=[WXbd])
            for n_ in range(16):
                c_, hh_ = n_ // 2, n_ % 2
                sl_ = slice(64 * hh_, 64 * hh_ + 64)
                DS(lambda h, n_=n_, c_=c_, sl_=sl_: h.dma_start(out=WAbd[sl_, c_, sl_], in_=IN['lru_wa'][0, n_]), writes=[WAbd])
                DS(lambda h, n_=n_, c_=c_, sl_=sl_: h.dma_start(out=WXbd[sl_, c_, sl_], in_=IN['lru_wx'][0, n_]), writes=[WXbd])

            with ExitStack() as es2:
                xbpad = fw.sb("xbpad", [128, 8, N + 3], F32, es=es2)
                hs = fw.sb("hs", [128, 8, N], F32, es=es2)
                tl = [fw.sb(f"tl{i}", [128, N], F32, es=es2) for i in range(6)]
                if not prm:
                    convT = fw.sb("convT", [128, 8, 3, N], F32, es=es2)
                    h0T = fw.sb("h0T", [128, 8, N], F32, es=es2)
                    csout = fw.sb("csout", [128, 8, 3, N], F32, es=es2)
                    for c in range(8):
                        cs = slice(c * 128, (c + 1) * 128)
                        for j3 in range(3):
                            DS(lambda h, c=c, cs=cs, j3=j3: h.dma_start(out=convT[:, c, j3, :], in_=IN['s_lru_conv'][:, cs].rearrange("(b j) p -> p j b", j=3)[:, j3, :]), writes=[convT])
                        DS(lambda h, c=c, cs=cs: h.dma_start(out=h0T[:, c, :], in_=IN['s_lru_h'][:, cs].rearrange("b p -> p b")), writes=[h0T])
                for c in range(8):
                    bk = proj_fm(W, c * 128, 128, N)
                    acopy(xbpad[:, c, 3:3 + N], bk[:, :N], [bk], [xbpad])
                    if prm:
                        vcopy(xbpad[:, c, 0:3], convtail[:, c, :], [convtail], [xbpad])
                        srcs = [xbpad[:, c, j:j + N] for j in range(4)]
                        srd = [xbpad]
                    else:
                        srcs = [convT[:, c, 0, :], convT[:, c, 1, :], convT[:, c, 2, :], xbpad[:, c, 3:3 + N]]
                        srd = [xbpad, convT]
                    xc, gr, gi, a_, t1, g_ = tl
                    ts(xc[:, :N], srcs[0], cvec[:, CW0, c:c + 1], cvec[:, CB, c:c + 1], ALU.mult, ALU.add, srd + [cvec], [xc])
                    for j in range(1, 4):
                        stt(xc[:, :N], srcs[j], cvec[:, CW0 + j, c:c + 1], xc[:, :N], ALU.mult, ALU.add, srd + [cvec, xc], [xc])
                    b1, b2 = bank(), bank()
                    mm(b1[:, :N], WAbd[:, c, :], xc[:, :N], [WAbd, xc], [b1])
                    mm(b2[:, :N], WXbd[:, c, :], xc[:, :N], [WXbd, xc], [b2])
                    act(gr[:, :N], b1[:, :N], AF.Sigmoid, [b1, cvec], [gr], bias=cvec[:, BA, c:c + 1])
                    act(gi[:, :N], b2[:, :N], AF.Sigmoid, [b2, cvec], [gi], bias=cvec[:, BX_, c:c + 1])
                    act(a_[:, :N], gr[:, :N], AF.Exp, [gr, cvec], [a_], scale=cvec[:, NSP, c:c + 1])
                    tt(t1[:, :N], a_[:, :N], a_[:, :N], ALU.mult, [a_], [t1])
                    ts(t1[:, :N], t1[:, :N], -1.0, 1.0, ALU.mult, ALU.add, [t1], [t1])
                    act(t1[:, :N], t1[:, :N], AF.Sqrt, [t1], [t1])
                    tt(gi[:, :N], gi[:, :N], xc[:, :N], ALU.mult, [gi, xc], [gi])
                    tt(t1[:, :N], t1[:, :N], gi[:, :N], ALU.mult, [t1, gi], [t1])
                    if prm:
                        V(lambda h, c=c: h.tensor_tensor_scan(out=hs[:, c, :N], data0=a_[:, :N], data1=t1[:, :N],
                                                              initial=hprev[:, c:c + 1], op0=ALU.mult, op1=ALU.add),
                          reads=[a_, t1, hprev], writes=[hs])
                        vcopy(hprev[:, c:c + 1], hs[:, c, N - 1:N], [hs], [hprev])
                        vcopy(convtail[:, c, :], xbpad[:, c, N:N + 3], [xbpad], [convtail])
                    else:
                        tt(a_[:, :N], a_[:, :N], h0T[:, c, :], ALU.mult, [a_, h0T], [a_])
                        tt(hs[:, c, :N], a_[:, :N], t1[:, :N], ALU.add, [a_, t1], [hs])
                for c in range(8):
                    bk = proj_fm(W, 1024 + c * 128, 128, N)
                    g_ = tl[5]
                    act(g_[:, :N], bk[:, :N], AF.Gelu_apprx_tanh, [bk], [g_])
                    tt(yTin[:, c, :N], g_[:, :N], hs[:, c, :N], ALU.mult, [g_, hs], [yTin])
                if prm and last:
                    DS(lambda h: h.dma_start(out=OUT['lru_h_p'][0, :].rearrange("(c p) -> p c", p=128), in_=hprev[:, :]),
                       reads=[hprev], writes=[OUT['lru_h_p']])
                    for j3 in range(3):
                        DS(lambda h, j3=j3: h.dma_start(out=OUT['lru_conv_p'][j3, :].rearrange("(c p) -> p c", p=128), in_=convtail[:, :, j3]),
                           reads=[convtail], writes=[OUT['lru_conv_p']])
                if not prm:
                    for c in range(8):
                        cs = slice(c * 128, (c + 1) * 128)
                        DS(lambda h, c=c, cs=cs: h.dma_start(out=OUT['lru_h_s'][:, cs].rearrange("b p -> p b"), in_=hs[:, c, :N]),
                           reads=[hs], writes=[OUT['lru_h_s']])
                        vcopy(csout[:, c, 0:2, :], convT[:, c, 1:3, :], [convT], [csout])
                        vcopy(csout[:, c, 2, :], xbpad[:, c, 3:3 + N], [xbpad], [csout])
                        for j3 in range(3):
                            DS(lambda h, c=c, cs=cs, j3=j3: h.dma_start(out=OUT['lru_conv_s'][:, cs].rearrange("(b j) p -> p j b", j=3)[:, j3, :], in_=csout[:, c, j3, :]),
                               reads=[csout], writes=[OUT['lru_conv_s']])
                fw.barrier()
            with ExitStack() as es2:
                def mk(nm, w=N):
                    return fw.sb(nm, [128, w], F32, es=es2)
                rs24, rs25, rs26, th, sg25, sg26 = (mk(x) for x in ("rs24", "rs25", "rs26", "th", "sg25", "sg26"))
                pad = [mk("pad0", N + 1), mk("pad1", N + 1)]
                tmp = mk("tmp")
                r_, k_, v_, lw, aicl, gate, kk, kmod, bb, bonus, Wt, Winv, Wprev, yT, t2, t3 = (mk(x) for x in (
                    "r_", "k_", "v_", "lw", "aicl", "gate", "kk", "kmod", "bb", "bonus", "Wt", "Winv", "Wprev", "yT", "t2", "t3"))
                Lpad = mk("Lpad", N + 1)
                negL = mk("negL", N + 1)
                if not prm:
                    shiftT = fw.sb("shiftT", [128, 27, N], F32, es=es2)
                    rawS = fw.sb("rawS", [128, 27, N], F32, es=es2)
                    SXi = [fw.sb(f"SXi{i}", [128, 128], F32, es=es2) for i in range(2)]
                    SXo = [fw.sb(f"SXo{i}", [128, 128], F32, es=es2) for i in range(2)]
                    Ps = [fw.sb(f"Ps{i}", [128, 128], F32, es=es2) for i in range(4)]
                    for t_ in SXi:
                        V(lambda h, t_=t_: h.memset(t_[:], 0.0), writes=[t_])
                    for q in range(27):
                        rows = 128 if q < 26 else 32
                        DS(lambda h, q=q, rows=rows: h.dma_start(out=shiftT[0:rows, q, :], in_=IN['s_shift'][:, q * 128:q * 128 + rows].rearrange("b p -> p b")),
                           writes=[shiftT])
                SS = []
                for k in range(2):
                    S = {}
                    for nm, w in (("AR", 256), ("BX", 128), ("KX", 128), ("VXf", 128), ("NN1", 256), ("NN2", 256),
                                  ("A0", 128), ("A1", 128), ("N0", 128), ("N1", 128), ("G0", 128), ("G1", 128),
                                  ("T3", 384), ("R0", 128), ("U", 128)):
                        S[nm] = fw.sb(f"S{k}{nm}", [128, w], F32, es=es2)
                    for nm in ("AR", "BX", "KX", "VXf"):
                        V(lambda h, t_=S[nm]: h.memset(t_[:], 0.0), writes=[S[nm]])
                    SS.append(S)

                def shifted(q, dest, dbuf, rows=128):
                    bk = proj_fm(W, 2048 + q * 128, rows, N)
                    pd = pad[q % 2]
                    acopy(pd[0:rows, 1:N + 1], bk[0:rows, :N], [bk], [pd])
                    if prm:
                        vcopy(pd[0:rows, 0:1], lastcol[0:rows, q:q + 1], [lastcol], [pd])
                        prev = pd[0:rows, 0:N]
                        prd = [pd]
                    else:
                        prev = shiftT[0:rows, q, :]
                        prd = [pd, shiftT]
                        vcopy(rawS[0:rows, q, :], pd[0:rows, 1:N + 1], [pd], [rawS])
                    cur = pd[0:rows, 1:N + 1]
                    tt(tmp[0:rows, :N], prev, cur, ALU.subtract, prd, [tmp])
                    stt(dest, tmp[0:rows, :N], muT[0:rows, q:q + 1], cur, ALU.mult, ALU.add, [tmp, muT, pd], [dbuf])
                    if prm:
                        vcopy(lastcol[0:rows, q:q + 1], pd[0:rows, N:N + 1], [pd], [lastcol])

                shifted(24, rs24[:, :N], rs24)
                shifted(25, rs25[:, :N], rs25)
                shifted(26, rs26[0:32, :N], rs26, rows=32)
                act(th[0:64, :N], rs24[0:64, :N], AF.Tanh, [rs24], [th])
                act(sg25[:, :N], rs25[:, :N], AF.Sigmoid, [rs25], [sg25])
                act(sg26[0:32, :N], rs26[0:32, :N], AF.Sigmoid, [rs26], [sg26])
                ucount = 0
                for j in range(8):
                    jc = slice(j * 128, (j + 1) * 128)
                    shifted(j, r_[:, :N], r_)
                    shifted(8 + j, k_[:, :N], k_)
                    shifted(16 + j, v_[:, :N], v_)
                    bd = bank()
                    mm(bd[:, :N], w2a2[0:64, jc], th[0:64, :N], [w2a2, th], [bd])
                    act(t2[:, :N], bd[:, :N], AF.Sigmoid, [bd, cvec], [t2], bias=cvec[:, W0, j:j + 1])
                    ts(lw[:, :N], t2[:, :N], -math.exp(-0.5), None, ALU.mult, None, [t2], [lw])
                    ba_ = bank()
                    mm(ba_[:, :N], w2a2[64:128, jc], rs24[64:128, :N], [w2a2, rs24], [ba_])
                    act(aicl[:, :N], ba_[:, :N], AF.Sigmoid, [ba_, cvec], [aicl], bias=cvec[:, A0, j:j + 1])
                    bg = bank()
                    mm(bg[:, :N], g2a[:, jc], sg25[:, :N], [g2a, sg25], [bg], start=True, stop=False)
                    mm(bg[:, :N], g2b[0:32, jc], sg26[0:32, :N], [g2b, sg26], [bg], start=False, stop=True)
                    acopy(gate[:, :N], bg[:, :N], [bg], [gate])
                    ts(kk[:, :N], k_[:, :N], cvec[:, KK_, j:j + 1], None, ALU.mult, None, [k_, cvec], [kk])
                    act(t2[:, :N], kk[:, :N], AF.Square, [kk], [t2])
                    bs = bank()
                    mm(bs[:, :N], blk1[:], t2[:, :N], [blk1, t2], [bs])
                    act(t3[:, :N], bs[:, :N], AF.Sqrt, [bs], [t3])
                    ts(t3[:, :N], t3[:, :N], 1e-12, None, ALU.max, None, [t3], [t3])
                    V(lambda h: h.reciprocal(out=t3[:, :N], in_=t3[:, :N]), reads=[t3], writes=[t3])
                    tt(kk[:, :N], kk[:, :N], t3[:, :N], ALU.mult, [kk, t3], [kk])
                    ts(t2[:, :N], aicl[:, :N], -1.0, cvec[:, KA, j:j + 1], ALU.add, ALU.mult, [aicl, cvec], [t2])
                    stt(kmod[:, :N], t2[:, :N], 1.0, k_[:, :N], ALU.add, ALU.mult, [t2, k_], [kmod])
                    tt(bb[:, :N], kk[:, :N], aicl[:, :N], ALU.mult, [kk, aicl], [bb])
                    stt(t2[:, :N], r_[:, :N], cvec[:, RK, j:j + 1], kmod[:, :N], ALU.mult, ALU.mult, [r_, cvec, kmod], [t2])
                    bb2 = bank()
                    mm(bb2[:, :N], blk1[:], t2[:, :N], [blk1, t2], [bb2])
                    tt(bonus[:, :N], bb2[:, :N], v_[:, :N], ALU.mult, [bb2, v_], [bonus])
                    if prm:
                        V(lambda h: h.memset(Lpad[:, 0:1], 0.0), writes=[Lpad])
                        V(lambda h: h.tensor_tensor_scan(out=Lpad[:, 1:N + 1], data0=ones_tt[:, :N], data1=lw[:, :N], initial=0.0,
                                                         op0=ALU.mult, op1=ALU.add), reads=[ones_tt, lw], writes=[Lpad])
                        ts(negL[:, :], Lpad[:, :], -1.0, None, ALU.mult, None, [Lpad], [negL])
                        for ci in range(NU):
                            c0 = ci * 64
                            act(Wt[:, c0:c0 + 64], Lpad[:, c0 + 1:c0 + 65], AF.Exp, [Lpad, negL], [Wt], bias=negL[:, c0:c0 + 1])
                            act(Winv[:, c0:c0 + 64], Lpad[:, c0 + 1:c0 + 65], AF.Exp, [Lpad, negL], [Winv], bias=Lpad[:, c0:c0 + 1], scale=-1.0)
                            act(Wprev[:, c0:c0 + 64], Lpad[:, c0:c0 + 64], AF.Exp, [Lpad, negL], [Wprev], bias=negL[:, c0:c0 + 1])
                    else:
                        act(Wt[:, :N], lw[:, :N], AF.Exp, [lw], [Wt])
                        act(Winv[:, :N], lw[:, :N], AF.Exp, [lw], [Winv], scale=-1.0)
                        vcopy(Wprev[:, :N], ones_tt[:, :N], [ones_tt], [Wprev])
                    for u in range(NU):
                        S = SS[ucount % 2]
                        cols = slice(u * n, (u + 1) * n)
                        if prm:
                            P0 = Pst[j][pcur[j]]
                            P1 = Pst[j][1 - pcur[j]]
                            pcur[j] = 1 - pcur[j]
                        else:
                            sx = SXi[ucount % 2]
                            for hh in range(2):
                                sl = slice(64 * hh, 64 * hh + 64)
                                DS(lambda h, sx=sx, sl=sl, u=u, hh=hh, j=j: h.dma_start(out=sx[sl, sl], in_=IN['s_wkv'][u, 2 * j + hh]), writes=[sx])
                            bt0 = bank()
                            PE(lambda h, bt0=bt0, sx=sx: h.transpose(out=bt0[:, 0:128], in_=sx[:], identity=ident[:]), reads=[sx, ident], writes=[bt0])
                            P0 = Ps[(ucount % 2) * 2]
                            P1 = Ps[(ucount % 2) * 2 + 1]
                            acopy(P0[:], bt0[:, 0:128], [bt0], [P0])
                        ucount += 1
                        for hh in range(2):
                            ps_ = slice(64 * hh, 64 * hh + 64)
                            f0 = 64 * hh
                            stt(S["AR"][ps_, f0:f0 + n], kk[ps_, cols], -1.0, Wprev[ps_, cols], ALU.mult, ALU.mult, [kk, Wprev], [S["AR"]])
                            tt(S["AR"][ps_, 128 + f0:128 + f0 + n], r_[ps_, cols], Wt[ps_, cols], ALU.mult, [r_, Wt], [S["AR"]])
                            tt(S["BX"][ps_, f0:f0 + n], bb[ps_, cols], Winv[ps_, cols], ALU.mult, [bb, Winv], [S["BX"]])
                            tt(S["KX"][ps_, f0:f0 + n], kmod[ps_, cols], Winv[ps_, cols], ALU.mult, [kmod, Winv], [S["KX"]])
                            acopy(S["VXf"][ps_, f0:f0 + n], v_[ps_, cols], [v_], [S["VXf"]])
                        b1, b2 = bank(), bank()
                        mm(b1[:, 0:256], S["BX"][:], S["AR"][:], [S["BX"], S["AR"]], [b1])
                        tt(S["NN1"][:], b1[:, 0:256], M_ar[:], ALU.mult, [b1, M_ar], [S["NN1"]])
                        mm(b2[:, 0:256], S["KX"][:], S["AR"][:], [S["KX"], S["AR"]], [b2])
                        tt(S["NN2"][:], b2[:, 0:256], M_ar[:], ALU.mult, [b2, M_ar], [S["NN2"]])
                        if n > 1:
                            b3 = bank()
                            mm(b3[:, 0:128], S["AR"][:, 0:128], S["BX"][:], [S["AR"], S["BX"]], [b3])
                            tt(S["A0"][:], b3[:, 0:128], M_sl[:], ALU.mult, [b3, M_sl], [S["A0"]])
                            tt(S["G0"][:], S["NN1"][:, 0:128], ident[:], ALU.add, [S["NN1"], ident], [S["G0"]])
                            Ncur, Nb = S["NN1"][:, 0:128], S["NN1"]
                            Ab = [S["A0"], S["A1"]]
                            Nbufs = [S["N0"], S["N1"]]
                            Gb = [S["G0"], S["G1"]]
                            for i in range(5):
                                Acur = Ab[i % 2]
                                Anew = Ab[(i + 1) % 2]
                                bA = bank()
                                mm(bA[:, 0:128], Ncur, Acur[:], [Nb, Acur], [bA])
                                if i < 4:
                                    bN = bank()
                                    mm(bN[:, 0:128], Acur[:], Ncur, [Acur, Nb], [bN])
                                acopy(Anew[:], bA[:, 0:128], [bA], [Anew])
                                if i < 4:
                                    Nn = Nbufs[i % 2]
                                    acopy(Nn[:], bN[:, 0:128], [bN], [Nn])
                                bG = bank()
                                mm(bG[:, 0:128], Anew[:], Gb[i % 2][:], [Anew, Gb[i % 2]], [bG])
                                tt(Gb[(i + 1) % 2][:], bG[:, 0:128], Gb[i % 2][:], ALU.add, [bG, Gb[i % 2]], [Gb[(i + 1) % 2]])
                                if i < 4:
                                    Ncur, Nb = Nn[:], Nn
                            Gf = Gb[1]
                        bt = bank()
                        for ti, nm in enumerate(("BX", "KX", "VXf")):
                            PE(lambda h, bt=bt, ti=ti, src=S[nm]: h.transpose(out=bt[:, ti * 128:(ti + 1) * 128], in_=src[:], identity=ident[:]),
                               reads=[S[nm], ident], writes=[bt])
                        acopy(S["T3"][:], bt[:, 0:384], [bt], [S["T3"]])
                        Bt, Kt, VX = S["T3"][:, 0:128], S["T3"][:, 128:256], S["T3"][:, 256:384]
                        bR = bank()
                        mm(bR[:, 0:128], S["AR"][:, 0:128], P0[:], [S["AR"], P0], [bR], start=True, stop=False)
                        mm(bR[:, 0:128], S["NN2"][:, 0:128], VX, [S["NN2"], S["T3"]], [bR], start=False, stop=True)
                        vcopy(S["R0"][:], bR[:, 0:128], [bR], [S["R0"]])
                        if n > 1:
                            bU = bank()
                            mm(bU[:, 0:128], Gf[:], S["R0"][:], [Gf, S["R0"]], [bU])
                            vcopy(S["U"][:], bU[:, 0:128], [bU], [S["U"]])
                            Ub = S["U"]
                        else:
                            Ub = S["R0"]
                        bY = bank()
                        mm(bY[:, 0:128], P0[:], S["AR"][:, 128:256], [P0, S["AR"]], [bY], start=True, stop=False)
                        mm(bY[:, 0:128], Ub[:], S["NN1"][:, 128:256], [Ub, S["NN1"]], [bY], start=False, stop=False)
                        mm(bY[:, 0:128], VX, S["NN2"][:, 128:256], [S["T3"], S["NN2"]], [bY], start=False, stop=True)
                        for hh in range(2):
                            ps_ = slice(64 * hh, 64 * hh + 64)
                            acopy(yT[ps_, cols], bY[ps_, 64 * hh:64 * hh + n], [bY], [yT])
                        bP = bank()
                        mm(bP[:, 0:128], ident[:], P0[:], [ident, P0], [bP], start=True, stop=False)
                        mm(bP[:, 0:128], Bt, Ub[:], [S["T3"], Ub], [bP], start=False, stop=False)
                        mm(bP[:, 0:128], Kt, VX, [S["T3"]], [bP], start=False, stop=True)
                        wc = (u + 1) * n - 1
                        ts(P1[:], bP[:, 0:128], Wt[:, wc:wc + 1], None, ALU.mult, None, [bP, Wt], [P1])
                        if not prm:
                            bt1 = bank()
                            PE(lambda h, bt1=bt1, P1=P1: h.transpose(out=bt1[:, 0:128], in_=P1[:], identity=ident[:]), reads=[P1, ident], writes=[bt1])
                            so = SXo[u % 2]
                            vcopy(so[:], bt1[:, 0:128], [bt1], [so])
                            for hh in range(2):
                                sl = slice(64 * hh, 64 * hh + 64)
                                DS(lambda h, so=so, sl=sl, u=u, hh=hh, j=j: h.dma_start(out=OUT['wkv_s'][u, 2 * j + hh], in_=so[sl, sl]),
                                   reads=[so], writes=[OUT['wkv_s']])
                    bm = bank()
                    mm(bm[:, :N], blk1[:], yT[:, :N], [blk1, yT], [bm])
                    stt(t2[:, :N], bm[:, :N], -1.0 / 64, yT[:, :N], ALU.mult, ALU.add, [bm, yT], [t2])
                    act(t3[:, :N], t2[:, :N], AF.Square, [t2], [t3])
                    bv = bank()
                    mm(bv[:, :N], blk1[:], t3[:, :N], [blk1, t3], [bv])
                    act(t3[:, :N], bv[:, :N], AF.Sqrt, [bv, eps_gn], [t3], bias=eps_gn[:, 0:1], scale=1.0 / 64)
                    V(lambda h: h.reciprocal(out=t3[:, :N], in_=t3[:, :N]), reads=[t3], writes=[t3])
                    tt(t2[:, :N], t2[:, :N], t3[:, :N], ALU.mult, [t2, t3], [t2])
                    ts(t2[:, :N], t2[:, :N], cvec[:, LNG, j:j + 1], cvec[:, LNB, j:j + 1], ALU.mult, ALU.add, [t2, cvec], [t2])
                    tt(t2[:, :N], t2[:, :N], bonus[:, :N], ALU.add, [t2, bonus], [t2])
                    tt(yTin[:, 8 + j, :N], t2[:, :N], gate[:, :N], ALU.mult, [t2, gate], [yTin])
                    if prm and last:
                        bt1 = bank()
                        Pf = Pst[j][pcur[j]]
                        PE(lambda h, bt1=bt1, Pf=Pf: h.transpose(out=bt1[:, 0:128], in_=Pf[:], identity=ident[:]), reads=[Pf, ident], writes=[bt1])
                        vcopy(t3[:, 0:128], bt1[:, 0:128], [bt1], [t3])
                        for hh in range(2):
                            sl = slice(64 * hh, 64 * hh + 64)
                            DS(lambda h, sl=sl, hh=hh, j=j: h.dma_start(out=OUT['wkv_p'][2 * j + hh], in_=t3[sl, sl]), reads=[t3], writes=[OUT['wkv_p']])
                if prm and last:
                    DS(lambda h: h.dma_start(out=OUT['shift_p'][0, 0:3328].rearrange("(c p) -> p c", p=128), in_=lastcol[:, 0:26]),
                       reads=[lastcol], writes=[OUT['shift_p']])
                    DS(lambda h: h.dma_start(out=OUT['shift_p'][0, 3328:3360].rearrange("(c p) -> p c", p=32), in_=lastcol[0:32, 26:27]),
                       reads=[lastcol], writes=[OUT['shift_p']])
                if not prm:
                    for q in range(27):
                        rows = 128 if q < 26 else 32
                        DS(lambda h, q=q, rows=rows: h.dma_start(out=OUT['shift_s'][:, q * 128:q * 128 + rows].rearrange("b p -> p b"), in_=rawS[0:rows, q, :]),
                           reads=[rawS], writes=[OUT['shift_s']])
                fw.barrier()
            Wo = IN['ab_w_out'][0]
            for db in range(KC):
                bk = proj_fm(Wo, db * 128, 128, N, src=yTin)
                tt(xT[:, db, :N], xT[:, db, :N], bk[:, :N], ALU.add, [xT, bk], [xT])
            fw.barrier()

    Gq = fw.sb("Gq", [128, 64], F32)
    Gk = fw.sb("Gk", [128, 3, 64], F32)
    rdec = fw.sb("rdec", [128, 16], F32)
    S2 = fw.sb("S2", [128, 4, 128], F32)
    eps5 = fw.sb("eps5", [128, 1], F32)
    DS(lambda h: h.dma_start(out=Gq[:], in_=IN['nsa_q_norm'][0:1, :].to_broadcast([128, 64])), writes=[Gq])
    for i_ in range(3):
        DS(lambda h, i_=i_: h.dma_start(out=Gk[:, i_, :], in_=IN['nsa_k_norm'][0, i_:i_ + 1, :].to_broadcast([128, 64])), writes=[Gk])
    DS(lambda h: h.dma_start(out=rdec[:], in_=IN['ret_dec'][:, :]), writes=[rdec])
    V(lambda h: h.memset(S2[:], 0.0), writes=[S2])
    V(lambda h: h.memset(eps5[:], 1e-5), writes=[eps5])
    _lg = np.log1p(-np.exp2(-5.0 - np.arange(8, dtype=np.float32))).astype(np.float32)
    GAMMA_C = [float(np.exp(np.float32(128.0) * _lg[h_])) for h_ in range(8)]
    GAMMA_1 = [float(np.exp(_lg[h_])) for h_ in range(8)]

    def odd_mixer(kind, N, jt, last):
        Wc = IN['cd_w_in'][0]
        prm = (kind == "p")
        pos0 = jt * TT if prm else T
        rmsnorm(N, 4)
        nsub = (N + 127) // 128
        allsubs = [(s, min(128, N - s * 128)) for s in range(nsub)]
        with ExitStack() as es:
            oT = fw.sb("oT", [128, KC, N], BF16, es=es)
            wt2 = [fw.sb(f"wt2{i}", [128, KC, 512], BF16, es=es) for i in range(2)]
            wt2_i = [0]
            V(lambda h: h.memset(oT[:, 0:8, :], 0.0), writes=[oT])
            tA = fw.sb("tA", [128, 1024], F32, es=es)
            tS = fw.sb("tS", [128, 16], F32, es=es)
            tR = [fw.sb(f"tR{i}", [128, 256], F32, es=es) for i in range(4)]
            ropN = [fw.sb(f"ropN{i}", [128, 16], F32, es=es) for i in range(2)]
            ropR = [fw.sb(f"ropR{i}", [128, 64], F32, es=es) for i in range(2)]

            def rms_heads(dst3, src3, r, H, gain_ap, sbuf_, dbuf_):
                act(tA[0:r, 0:H * 64].rearrange("p (h d) -> p h d", d=64), src3, AF.Square, [sbuf_], [tA])
                V(lambda h: h.tensor_reduce(out=tS[0:r, 0:H], in_=tA[0:r, 0:H * 64].rearrange("p (h d) -> p h d", d=64), axis=AX.X, op=ALU.add),
                  reads=[tA], writes=[tS])
                act(tS[0:r, 0:H], tS[0:r, 0:H], AF.Sqrt, [tS, eps6], [tS], bias=eps6[0:r, 0:1], scale=1.0 / 64)
                V(lambda h: h.reciprocal(out=tS[0:r, 0:H], in_=tS[0:r, 0:H]), reads=[tS], writes=[tS])
                V(lambda h: h.tensor_tensor(out=dst3, in0=src3, in1=tS[0:r, 0:H].unsqueeze(2).to_broadcast([r, H, 64]), op=ALU.mult),
                  reads=[tS, sbuf_], writes=[dbuf_])
                V(lambda h: h.tensor_tensor(out=dst3, in0=dst3, in1=gain_ap.unsqueeze(1).to_broadcast([r, H, 64]), op=ALU.mult),
                  reads=[Gq, Gk, dbuf_], writes=[dbuf_])

            def rope_ip(x3, r, H, half, cs, sn, ropb, xbuf):
                x1 = x3[:, :, 0:half]
                x2 = x3[:, :, half:2 * half]
                cb = cs.unsqueeze(1).to_broadcast([r, H, half])
                sb_ = sn.unsqueeze(1).to_broadcast([r, H, half])
                tv = [t_[0:r, 0:H * half].rearrange("p (h e) -> p h e", e=half) for t_ in tR]
                crd = [ropb, xbuf]
                V(lambda h: h.tensor_tensor(out=tv[0], in0=x1, in1=cb, op=ALU.mult), reads=crd, writes=[tR[0]])
                V(lambda h: h.tensor_tensor(out=tv[1], in0=x2, in1=sb_, op=ALU.mult), reads=crd, writes=[tR[1]])
                V(lambda h: h.tensor_tensor(out=tv[2], in0=x2, in1=cb, op=ALU.mult), reads=crd, writes=[tR[2]])
                V(lambda h: h.tensor_tensor(out=tv[3], in0=x1, in1=sb_, op=ALU.mult), reads=crd, writes=[tR[3]])
                V(lambda h: h.tensor_tensor(out=x1, in0=tv[0], in1=tv[1], op=ALU.subtract), reads=[tR[0], tR[1]], writes=[xbuf])
                V(lambda h: h.tensor_tensor(out=x2, in0=tv[2], in1=tv[3], op=ALU.add), reads=[tR[2], tR[3]], writes=[xbuf])

            def group(col0, ncols, subs, fn):
                wt = wt2[wt2_i[0] % 2]
                wt2_i[0] += 1
                DG(lambda h: h.dma_start(out=wt[:, :, 0:ncols], in_=Wc[:, col0:col0 + ncols].rearrange("(kc p) n -> p kc n", p=128)), writes=[wt])
                for si, (s, r) in enumerate(subs):
                    bk = bank()
                    for kc in range(KC):
                        PE(lambda h, kc=kc, s=s, r=r, bk=bk: h.matmul(bk[0:r, 0:ncols], lhsT=hT[:, kc, s * 128:s * 128 + r], rhs=wt[:, kc, 0:ncols],
                                                                      start=(kc == 0), stop=(kc == KC - 1)),
                           reads=[wt, hT], writes=[bk], inc=(kc == KC - 1))
                    fn(si, s, r, bk)

            halves = [allsubs[0:2], allsubs[2:4]] if nsub == 4 else [allsubs]
            for subs in halves:
                for si, (s, r) in enumerate(subs):
                    if prm:
                        DS(lambda h, si=si, s=s, r=r: h.dma_start(out=ropN[si][0:r, :], in_=IN['rope_nsa'][pos0 + s * 128:pos0 + s * 128 + r, :]), writes=[ropN[si]])
                        DS(lambda h, si=si, s=s, r=r: h.dma_start(out=ropR[si][0:r, :], in_=IN['rope_ret'][pos0 + s * 128:pos0 + s * 128 + r, :]), writes=[ropR[si]])
                    else:
                        DS(lambda h, si=si, r=r: h.dma_start(out=ropN[si][0:r, :], in_=IN['rope_nsa'][T:T + 1, :].to_broadcast([r, 16])), writes=[ropN[si]])
                        DS(lambda h, si=si, r=r: h.dma_start(out=ropR[si][0:r, :], in_=IN['rope_ret'][T:T + 1, :].to_broadcast([r, 64])), writes=[ropR[si]])

                es_kv = ExitStack()
                rowb = [fw.sb(f"rowb{i}", [128, 1024], F32, es=es_kv) for i in range(2)]
                winb = [fw.sb(f"winb{i}", [128, 512], F32, es=es_kv) for i in range(2)]

                def f_kv0(si, s, r, bk):
                    acopy(rowb[si][0:r, 0:512], bk[0:r, 0:512], [bk], [rowb[si]])
                group(1024, 512, subs, f_kv0)

                def f_kv1(si, s, r, bk):
                    d3 = rowb[si][0:r, 512:768].rearrange("p (h d) -> p h d", d=64)
                    rms_heads(d3, bk[0:r, 0:256].rearrange("p (h d) -> p h d", d=64), r, 4, Gk[0:r, 1, :], bk, rowb[si])
                    rope_ip(d3, r, 4, 8, ropN[si][0:r, 0:8], ropN[si][0:r, 8:16], ropN[si], rowb[si])
                    acopy(rowb[si][0:r, 768:1024], bk[0:r, 256:512], [bk], [rowb[si]])
                    dst = OUT['kv_p'][pos0 + s * 128:pos0 + s * 128 + r, :] if prm else OUT['kv_s'][0:r, :]
                    DS(lambda h, si=si, r=r, dst=dst: h.dma_start(out=dst, in_=rowb[si][0:r, :]), reads=[rowb[si]], writes=[OUT['kv_p'], OUT['kv_s']])
                group(1536, 512, subs, f_kv1)

                def f_kvw(si, s, r, bk):
                    d3 = winb[si][0:r, 0:256].rearrange("p (h d) -> p h d", d=64)
                    rms_heads(d3, bk[0:r, 0:256].rearrange("p (h d) -> p h d", d=64), r, 4, Gk[0:r, 2, :], bk, winb[si])
                    rope_ip(d3, r, 4, 8, ropN[si][0:r, 0:8], ropN[si][0:r, 8:16], ropN[si], winb[si])
                    acopy(winb[si][0:r, 256:512], bk[0:r, 256:512], [bk], [winb[si]])
                    if prm and last:
                        DS(lambda h, si=si, s=s, r=r: h.dma_start(out=OUT['win_p'][s * 128:s * 128 + r, :], in_=winb[si][0:r, :]), reads=[winb[si]], writes=[OUT['win_p']])
                    if prm:
                        DS(lambda h, si=si, s=s, r=r: h.dma_start(out=WIN[pos0 + s * 128:pos0 + s * 128 + r, :], in_=winb[si][0:r, :]), reads=[winb[si]], writes=[WIN])
                    if not prm:
                        DS(lambda h, si=si, r=r: h.dma_start(out=OUT['win_s'][:, 511, :], in_=winb[si][0:r, :]), reads=[winb[si]], writes=[OUT['win_s']])
                        DS(lambda h: h.dma_start(out=OUT['win_s'][:, 0:511, :], in_=IN['cache_win'][:, 1:512, :]), writes=[OUT['win_s']])
                group(2048, 512, subs, f_kvw)
                fw.barrier()
                es_kv.close()
                es_rt = ExitStack()
                gng = fw.sb("gng", [128, 1024], F32, es=es_rt)
                gnb = fw.sb("gnb", [128, 1024], F32, es=es_rt)
                dmk = fw.sb("dmk", [128, 8, 128], F32, es=es_rt)
                DS(lambda h, gng=gng: h.dma_start(out=gng[:], in_=IN['ret_gn_g'][0:1, :].to_broadcast([128, 1024])), writes=[gng])
                DS(lambda h, gnb=gnb: h.dma_start(out=gnb[:], in_=IN['ret_gn_b'][0:1, :].to_broadcast([128, 1024])), writes=[gnb])
                for h_ in range(8):
                    DS(lambda h, h_=h_, dmk=dmk: h.dma_start(out=dmk[:, h_, :], in_=IN['ret_dmaskT'][h_]), writes=[dmk])
                rqb = [fw.sb(f"rqb{i}", [128, 512], F32, es=es_rt) for i in range(2)]
                rkb = [fw.sb(f"rkb{i}", [128, 512], F32, es=es_rt) for i in range(2)]
                rvb = [fw.sb(f"rvb{i}", [128, 1024], F32, es=es_rt) for i in range(2)]
                ynb = [fw.sb(f"ynb{i}", [128, 8, 128], F32, es=es_rt) for i in range(2)]
                qkT = [fw.sb(f"qkT{i}", [128, 4, 128], F32, es=es_rt) for i in range(2)]
                smb = [fw.sb(f"smb{i}", [128, 128], F32, es=es_rt) for i in range(2)]
                Asb = [fw.sb(f"Asb{i}", [128, 128], F32, es=es_rt) for i in range(2)]
                ktl = fw.sb("ktl", [128, 512], F32, es=es_rt)

                def f_rq(si, s, r, bk):
                    acopy(rqb[si][0:r, :], bk[0:r, 0:512], [bk], [rqb[si]])
                    rope_ip(rqb[si][0:r, :].rearrange("p (h d) -> p h d", d=64), r, 8, 32, ropR[si][0:r, 0:32], ropR[si][0:r, 32:64], ropR[si], rqb[si])
                group(2608, 512, subs, f_rq)

                def f_rk(si, s, r, bk):
                    A(lambda h, si=si, r=r, bk=bk: h.mul(out=rkb[si][0:r, :], in_=bk[0:r, 0:512], mul=0.125), reads=[bk], writes=[rkb[si]])
                    rope_ip(rkb[si][0:r, :].rearrange("p (h d) -> p h d", d=64), r, 8, 32, ropR[si][0:r, 0:32], ropR[si][0:r, 32:64], ropR[si], rkb[si])
                group(3120, 512, subs, f_rk)

                def f_rv0(si, s, r, bk):
                    acopy(rvb[si][0:r, 0:512], bk[0:r, 0:512], [bk], [rvb[si]])
                group(3632, 512, subs, f_rv0)

                def f_rv1(si, s, r, bk):
                    acopy(rvb[si][0:r, 512:1024], bk[0:r, 0:512], [bk], [rvb[si]])
                group(4144, 512, subs, f_rv1)

                if prm:
                    for si, (s, r) in enumerate(subs):
                        for wh, srcb in ((0, rqb[si]), (1, rkb[si])):
                            bt = bank()
                            for c4 in range(4):
                                PE(lambda h, bt=bt, c4=c4, srcb=srcb: h.transpose(out=bt[:, c4 * 128:(c4 + 1) * 128], in_=srcb[:, c4 * 128:(c4 + 1) * 128], identity=ident[:]),
                                   reads=[srcb, ident], writes=[bt])
                            acopy(qkT[wh][:, :, :], bt[:, :].rearrange("p (c n) -> p c n", n=128), [bt], [qkT[wh]])
                        qT_, kT_ = qkT
                        for hh_ in range(8):
                            V(lambda h, si=si, hh_=hh_: h.tensor_scalar(out=ktl[:, hh_ * 64:(hh_ + 1) * 64], in0=rkb[si][:, hh_ * 64:(hh_ + 1) * 64],
                                                                        scalar1=rdec[:, 8 + hh_:9 + hh_], scalar2=None, op0=ALU.mult),
                              reads=[rkb[si], rdec], writes=[ktl])
                        for h8 in range(8):
                            hp, hh = h8 // 2, h8 % 2
                            pr_ = slice(64 * hh, 64 * hh + 64)
                            bS = bank()
                            mm(bS[:, 0:128], kT_[pr_, hp, :], qT_[pr_, hp, :], [qkT[0], qkT[1]], [bS])
                            sm = smb[h8 % 2]
                            tt(sm[:], bS[:, 0:128], dmk[:, h8, :], ALU.mult, [bS, dmk], [sm])
                            bA = bank()
                            mm(bA[:, 0:128], sm[:], rvb[si][:, h8 * 128:(h8 + 1) * 128], [sm, rvb[si]], [bA])
                            bB = bank()
                            mm(bB[:, 0:128], qT_[pr_, hp, :], S2[pr_, hp, :], [qkT[0], S2], [bB])
                            As = Asb[h8 % 2]
                            acopy(As[:], bA[:, 0:128], [bA], [As])
                            stt(ynb[si][:, h8, :], bB[:, 0:128], rdec[:, h8:h8 + 1], As[:], ALU.mult, ALU.add, [bB, rdec, As], [ynb[si]])
                            bK = bank()
                            mm(bK[:, 0:128], ktl[:, hp * 128:(hp + 1) * 128], rvb[si][:, h8 * 128:(h8 + 1) * 128], [ktl, rvb[si]], [bK])
                            stt(S2[pr_, hp, :], S2[pr_, hp, :], GAMMA_C[h8], bK[pr_, 0:128], ALU.mult, ALU.add, [S2, bK], [S2])
                if not prm:
                    r = N
                    for nm_, srcb_ in (("q", rqb[0]), ("k", rkb[0])):
                        DS(lambda h, nm_=nm_, srcb_=srcb_: h.dma_start(out=SCR[nm_][:, :], in_=srcb_[0:r, :]), reads=[srcb_], writes=[SCR[nm_]])
                    DS(lambda h: h.dma_start(out=SCR["v"][:, :], in_=rvb[0][0:r, :]), reads=[rvb[0]], writes=[SCR["v"]])
                    es_s = ExitStack()
                    S0p = fw.sb("S0p", [128, NSMP, 128], F32, es=es_s)
                    vB = fw.sb("vB", [128, NSMP, 128], F32, es=es_s)
                    kTp = fw.sb("kTp", [128, NSMP], F32, es=es_s)
                    qTp = fw.sb("qTp", [128, NSMP], F32, es=es_s)
                    qTm = fw.sb("qTm", [128, NSMP, NSMP], F32, es=es_s)
                    qkp = fw.sb("qkp", [128, 8], F32, es=es_s)
                    V(lambda h: h.memset(qTm[:], 0.0), writes=[qTm])
                    tt(tA[0:r, 0:512], rqb[0][0:r, :], rkb[0][0:r, :], ALU.mult, [rqb[0], rkb[0]], [tA])
                    V(lambda h: h.tensor_reduce(out=qkp[0:r, 0:8], in_=tA[0:r, 0:512].rearrange("p (h d) -> p h d", d=64), axis=AX.X, op=ALU.add),
                      reads=[tA], writes=[qkp])
                    for hp in range(4):
                        for hh in range(2):
                            h8 = 2 * hp + hh
                            pr_ = slice(64 * hh, 64 * hh + 64)
                            DS(lambda h, h8=h8, pr_=pr_: h.dma_start(out=S0p[pr_, :, :], in_=IN['s_ret'][:, h8].rearrange("b d e -> d b e")), writes=[S0p])
                            DS(lambda h, h8=h8, pr_=pr_: h.dma_start(out=vB[pr_, :, :], in_=SCR["v"][:, h8 * 128:(h8 + 1) * 128].unsqueeze(0).to_broadcast([64, NSMP, 128])),
                               reads=[SCR["v"]], writes=[vB])
                        DS(lambda h, hp=hp: h.dma_start(out=kTp[:, :], in_=SCR["k"][:, hp * 128:(hp + 1) * 128].rearrange("b p -> p b")), reads=[SCR["k"]], writes=[kTp])
                        DS(lambda h, hp=hp: h.dma_start(out=qTp[:, :], in_=SCR["q"][:, hp * 128:(hp + 1) * 128].rearrange("b p -> p b")), reads=[SCR["q"]], writes=[qTp])
                        vcopy(qTm[:, :, :].rearrange("p a b -> p (a b)")[:, 0:NSMP * NSMP:NSMP + 1], qTp[:, :], [qTp], [qTm])
                        for hh in range(2):
                            h8 = 2 * hp + hh
                            pr_ = slice(64 * hh, 64 * hh + 64)
                            bq = bank()
                            for b_ in range(NSMP):
                                mm(bq[0:NSMP, 0:128], qTm[pr_, b_, :], S0p[pr_, b_, :], [qTm, S0p], [bq], start=(b_ == 0), stop=(b_ == NSMP - 1), inc=(b_ == NSMP - 1))
                            ts(tA[0:r, 0:128], rvb[0][0:r, h8 * 128:(h8 + 1) * 128], qkp[0:r, h8:h8 + 1], None, ALU.mult, None, [rvb[0], qkp], [tA])
                            stt(ynb[0][0:r, h8, :], bq[0:r, 0:128], GAMMA_1[h8], tA[0:r, 0:128], ALU.mult, ALU.add, [bq, tA], [ynb[0]])
                            ts(S0p[pr_, :, :], S0p[pr_, :, :], GAMMA_1[h8], None, ALU.mult, None, [S0p], [S0p])
                            for b_ in range(NSMP):
                                stt(vB[pr_, b_, :], vB[pr_, b_, :], kTp[pr_, b_:b_ + 1], S0p[pr_, b_, :], ALU.mult, ALU.add, [vB, kTp, S0p], [vB])
                            DS(lambda h, h8=h8, pr_=pr_: h.dma_start(out=OUT['ret_s'][:, h8].rearrange("b d e -> d b e"), in_=vB[pr_, :, :]), reads=[vB], writes=[OUT['ret_s']])
                    fw.barrier()
                    es_s.close()
                for si, (s, r) in enumerate(subs):
                    y3 = ynb[si][0:r, :, :]
                    V(lambda h, y3=y3, r=r: h.tensor_reduce(out=tS[0:r, 0:8], in_=y3, axis=AX.X, op=ALU.add), reads=[ynb[si]], writes=[tS])
                    ts(tS[0:r, 0:8], tS[0:r, 0:8], -1.0 / 128, None, ALU.mult, None, [tS], [tS])
                    V(lambda h, y3=y3, r=r: h.tensor_tensor(out=y3, in0=y3, in1=tS[0:r, 0:8].unsqueeze(2).to_broadcast([r, 8, 128]), op=ALU.add),
                      reads=[tS, ynb[si]], writes=[ynb[si]])
                    act(tA[0:r, :].rearrange("p (h d) -> p h d", d=128), y3, AF.Square, [ynb[si]], [tA])
                    V(lambda h, r=r: h.tensor_reduce(out=tS[0:r, 0:8], in_=tA[0:r, :].rearrange("p (h d) -> p h d", d=128), axis=AX.X, op=ALU.add),
                      reads=[tA], writes=[tS])
                    act(tS[0:r, 0:8], tS[0:r, 0:8], AF.Sqrt, [tS, eps5], [tS], bias=eps5[0:r, 0:1], scale=1.0 / 128)
                    V(lambda h, r=r: h.reciprocal(out=tS[0:r, 0:8], in_=tS[0:r, 0:8]), reads=[tS], writes=[tS])
                    V(lambda h, y3=y3, r=r: h.tensor_tensor(out=y3, in0=y3, in1=tS[0:r, 0:8].unsqueeze(2).to_broadcast([r, 8, 128]), op=ALU.mult),
                      reads=[tS, ynb[si]], writes=[ynb[si]])
                    y2 = ynb[si][0:r, :, :].rearrange("p h d -> p (h d)")
                    tt(y2, y2, gng[0:r, :], ALU.mult, [ynb[si], gng], [ynb[si]])
                    tt(y2, y2, gnb[0:r, :], ALU.add, [ynb[si], gnb], [ynb[si]])

                def f_rg(half_):
                    def f(si, s, r, bk):
                        act(tA[0:r, 0:512], bk[0:r, 0:512], AF.Silu, [bk], [tA])
                        y2 = ynb[si][0:r, :, :].rearrange("p h d -> p (h d)")[:, half_ * 512:(half_ + 1) * 512]
                        tt(y2, y2, tA[0:r, 0:512], ALU.mult, [ynb[si], tA], [ynb[si]])
                    return f
                group(4656, 512, subs, f_rg(0))
                group(5168, 512, subs, f_rg(1))
                for si, (s, r) in enumerate(subs):
                    y2 = ynb[si][:, :, :].rearrange("p h d -> p (h d)")
                    for c4 in range(2):
                        bt = bank()
                        for q in range(4):
                            c = c4 * 4 + q
                            PE(lambda h, bt=bt, q=q, c=c, r=r, y2=y2: h.transpose(out=bt[:, q * 128:q * 128 + r], in_=y2[0:r, c * 128:(c + 1) * 128], identity=ident[0:r, 0:r]),
                               reads=[ynb[si], ident], writes=[bt])
                        V(lambda h, bt=bt, c4=c4, s=s, r=r: h.tensor_copy(out=oT[:, 8 + c4 * 4:12 + c4 * 4, s * 128:s * 128 + r],
                                                                         in_=bt[:, :].rearrange("p (q n) -> p q n", n=128)[:, :, 0:r]), reads=[bt], writes=[oT])
                fw.barrier()
                es_rt.close()
            if NSA_ON and (prm or NSA_SAMPLE):
                fw.barrier()
                NQ = N if prm else 128
                nqs = NQ // 128
                q0 = jt * TT if prm else T
                qcoef = 1 if prm else 0
                es_n = ExitStack()
                qnT = fw.sb("qnT", [128, 8, NQ], BF16, es=es_n)
                qrT = fw.sb("qrT", [128, 8, NQ], BF16, es=es_n)
                gat = fw.sb("gat", [128, nqs, 48], F32, es=es_n)
                KcT = fw.sb("KcT", [128, 4, 128], BF16, es=es_n)
                Vc = fw.sb("Vc", [128, 4, 97], BF16, es=es_n)
                ropq = fw.sb("ropq", [128, nqs, 16], F32, es=es_n)
                qtm = [fw.sb(f"qtm{i}", [128, 512], F32, es=es_n) for i in range(2)]
                seltab = fw.sb("seltab", [128, nqs, 96], F32, es=es_n)
                ovc = fw.sb("ovc", [128, 33], F32, es=es_n)
                ones_bf = fw.sb("ones_bf", [128, NQ], BF16, es=es_n)
                zer_bf = fw.sb("zer_bf", [128, 128], BF16, es=es_n)
                cmask = fw.sb("cmask", [128, NQ], BF16, es=es_n)
                w2kf = fw.sb("w2kf", [128, 2, 64], F32, es=es_n)
                w2vf = fw.sb("w2vf", [128, 2, 64], F32, es=es_n)
                w2kd = fw.sb("w2kd", [128, 2, 128], BF16, es=es_n)
                w2v = fw.sb("w2v", [128, 2, 64], BF16, es=es_n)
                cst = fw.sb("cst", [128, 8], F32, es=es_n)
                b2vb = fw.sb("b2vb", [128, 64], F32, es=es_n)
                if not prm:
                    V(lambda h: h.memset(qnT[:], 0.0), writes=[qnT])
                    V(lambda h: h.memset(qrT[:], 0.0), writes=[qrT])
                    V(lambda h: h.memset(gat[:], 0.0), writes=[gat])
                    onsa_tot = fw.sb("onsa_tot", [128, 4, 256], F32, es=es_n)
                    V(lambda h: h.memset(onsa_tot[:], 0.0), writes=[onsa_tot])
                    row0m = fw.sb("row0m", [128, NQ], BF16, es=es_n)
                    ptb_i = fw.sb("ptb_i", [128, NSMP * 16], I32, es=es_n)
                    ptb_f = fw.sb("ptb_f", [128, NSMP * 16], F32, es=es_n)
                    pidx = fw.sb("pidx", [128, 1], F32, es=es_n)
                    idx_i = fw.sb("idx_i", [128, NSMP * 16], I32, es=es_n)
                    newrow = fw.sb("newrow", [128, 1024], F32, es=es_n)
                    DS(lambda h: h.dma_start(out=ptb_i[:], in_=IN['page_table'][:, :].rearrange("b k -> (b k)").unsqueeze(0).to_broadcast([128, NSMP * 16])), writes=[ptb_i])
                    vcopy(ptb_f[:], ptb_i[:], [ptb_i], [ptb_f])
                    G(lambda h: h.iota(pidx[:], pattern=[[0, 1]], base=0, channel_multiplier=1, allow_small_or_imprecise_dtypes=True), writes=[pidx])
                    ts(ptb_f[:], ptb_f[:], 128.0, pidx[:, 0:1], ALU.mult, ALU.add, [ptb_f, pidx], [ptb_f])
                    ts(ptb_f[:], ptb_f[:], 2.0, None, ALU.mult, None, [ptb_f], [ptb_f])
                    vcopy(idx_i[:], ptb_f[:], [ptb_f], [idx_i])
                    idx_b = fw.sb("idx_b", [128, NSMP * 16], I32, es=es_n)
                    ts(ptb_f[:], ptb_f[:], 1.0, None, ALU.add, None, [ptb_f], [ptb_f])
                    vcopy(idx_b[:], ptb_f[:], [ptb_f], [idx_b])
                    stg = [fw.sb(f"stg{i}", [128, 512], F32, es=es_n) for i in range(2)]
                    stg_i = [0]
                for s_ in range(nqs):
                    if prm:
                        DS(lambda h, s_=s_: h.dma_start(out=ropq[:, s_, :], in_=IN['rope_nsa'][q0 + s_ * 128:q0 + (s_ + 1) * 128, :]), writes=[ropq])
                        DS(lambda h, s_=s_: h.dma_start(out=seltab[:, s_, :], in_=IN['sel_tab'][jt * 4 + s_]), writes=[seltab])
                    else:
                        DS(lambda h: h.dma_start(out=ropq[:, 0, :], in_=IN['rope_nsa'][T:T + 1, :].to_broadcast([128, 16])), writes=[ropq])
                        DS(lambda h: h.dma_start(out=seltab[:, 0, :], in_=IN['sel_tab'][16]), writes=[seltab])
                DS(lambda h: h.dma_start(out=ovc[:], in_=IN['cmp_ov'][:, :]), writes=[ovc])
                V(lambda h: h.memset(ones_bf[:], 1.0), writes=[ones_bf])
                V(lambda h: h.memset(zer_bf[:], 0.0), writes=[zer_bf])
                G(lambda h: h.affine_select(out=cmask[:], in_=ones_bf[:], pattern=[[qcoef, NQ]], compare_op=ALU.is_ge, fill=0.0,
                                            base=q0 - 31, channel_multiplier=-16), reads=[ones_bf], writes=[cmask])
                if not prm:
                    G(lambda h: h.affine_select(out=row0m[:], in_=ones_bf[:], pattern=[[0, NQ]], compare_op=ALU.is_ge, fill=0.0,
                                                base=0, channel_multiplier=-1), reads=[ones_bf], writes=[row0m])
                DS(lambda h: h.dma_start(out=w2kf[:], in_=IN['cmp_k_w2'][0].rearrange("(c p) d -> p c d", p=128)), writes=[w2kf])
                DS(lambda h: h.dma_start(out=w2vf[:], in_=IN['cmp_v_w2'][0].rearrange("(c p) d -> p c d", p=128)), writes=[w2vf])
                for hc_ in range(2):
                    V(lambda h, hc_=hc_: h.tensor_copy(out=w2kd[:, hc_, :].rearrange("p (t d) -> p t d", d=64),
                                                       in_=w2kf[:, hc_, :].unsqueeze(1).to_broadcast([128, 2, 64])), reads=[w2kf], writes=[w2kd])
                vcopy(w2v[:], w2vf[:], [w2vf], [w2v])
                DS(lambda h: h.dma_start(out=cst[:, 0:2], in_=IN['cmp_k_b1'][0].rearrange("(c p) -> p c", p=128)), writes=[cst])
                DS(lambda h: h.dma_start(out=cst[:, 2:4], in_=IN['cmp_v_b1'][0].rearrange("(c p) -> p c", p=128)), writes=[cst])
                for hf_ in range(2):
                    DS(lambda h, hf_=hf_: h.dma_start(out=cst[64 * hf_:64 * hf_ + 64, 4:5], in_=IN['cmp_k_b2'][0].rearrange("(p c) -> p c", c=1)), writes=[cst])
                    DS(lambda h, hf_=hf_: h.dma_start(out=cst[64 * hf_:64 * hf_ + 64, 5:6], in_=IN['nsa_k_norm'][0, 0].rearrange("(p c) -> p c", c=1)), writes=[cst])
                DS(lambda h: h.dma_start(out=b2vb[:], in_=IN['cmp_v_b2'][0:1, :].to_broadcast([128, 64])), writes=[b2vb])

                def f_q(gi):
                    def f(si, s, r, bk):
                        qt = qtm[si % 2]
                        q3 = qt[0:r, :].rearrange("p (h d) -> p h d", d=64)
                        rms_heads(q3, bk[0:r, 0:512].rearrange("p (h d) -> p h d", d=64), r, 8, Gq[0:r, :], bk, qt)
                        for dstT in (qnT, qrT):
                            bt = bank()
                            for c4 in range(4):
                                PE(lambda h, bt=bt, c4=c4, qt=qt, r=r: h.transpose(out=bt[:, c4 * 128:c4 * 128 + r], in_=qt[0:r, c4 * 128:(c4 + 1) * 128], identity=ident[0:r, 0:r]),
                                   reads=[qt, ident], writes=[bt])
                            acopy(dstT[:, gi * 4:gi * 4 + 4, s * 128:s * 128 + r], bt[:, :].rearrange("p (c n) -> p c n", n=128)[:, :, 0:r], [bt], [dstT])
                            if dstT is qnT:
                                rope_ip(q3, r, 8, 8, ropq[0:r, s, 0:8], ropq[0:r, s, 8:16], ropq, qt)
                    return f
                group(0, 512, allsubs, f_q(0))
                group(512, 512, allsubs, f_q(1))

                def f_gt(si, s, r, bk):
                    act(gat[0:r, s, :], bk[0:r, 0:48], AF.Sigmoid, [bk], [gat])
                group(2560, 48, allsubs, f_gt)

                saved_pool = list(bank_pool)
                for bseq in ([None] if prm else list(range(NSMP))):
                    bank_pool[:] = saved_pool
                    if prm:
                        nkc = 4 * (jt + 1)
                        nks = nkc
                    else:
                        nkc = 16
                        nks = 17
                        V(lambda h: h.memset(newrow[:], 0.0), writes=[newrow])
                        DS(lambda h, bseq=bseq: h.dma_start(out=newrow[0:1, :], in_=OUT['kv_s'][bseq:bseq + 1, :]), reads=[OUT['kv_s']], writes=[newrow])
                    nch = 8 * nkc
                    ncmp = nch - 1

                    def load_rows(dst_ap, dbuf, kb, c0, c1, bseq=bseq):
                        if prm:
                            DS(lambda h: h.dma_start(out=dst_ap, in_=OUT['kv_p'][kb * 128:(kb + 1) * 128, c0:c1]), reads=[OUT['kv_p']], writes=[dbuf])
                        elif kb < 16:
                            col = bseq * 16 + kb
                            half = 0 if c0 < 512 else 1
                            ixt = idx_i if half == 0 else idx_b
                            if c1 - c0 == 512:
                                fw.dma("gpsimd", lambda h: h.indirect_dma_start(out=dst_ap, out_offset=None, in_=POOL2D[:, :],
                                                                               in_offset=bass.IndirectOffsetOnAxis(ap=ixt[:, col:col + 1], axis=0)),
                                       reads=[ixt], writes=[dbuf])
                            else:
                                st_ = stg[stg_i[0] % 2]
                                stg_i[0] += 1
                                fw.dma("gpsimd", lambda h: h.indirect_dma_start(out=st_[:], out_offset=None, in_=POOL2D[:, :],
                                                                               in_offset=bass.IndirectOffsetOnAxis(ap=ixt[:, col:col + 1], axis=0)),
                                       reads=[ixt], writes=[st_])
                                vcopy(dst_ap, st_[:, c0 - 512 * half:c1 - 512 * half], [st_], [dbuf])
                        else:
                            vcopy(dst_ap, newrow[:, c0:c1], [newrow], [dbuf])

                    es_c = ExitStack()
                    kvcT = fw.sb("kvcT", [128, 4, nkc * 128], BF16, es=es_c)
                    geluT = fw.sb("geluT", [128, 2, 2, 4, 128], BF16, es=es_c)
                    tokr = [fw.sb(f"tokr{i}", [128, 512], F32, es=es_c) for i in range(2)]
                    xs_ = fw.sb("xs_", [128, 128], F32, es=es_c)
                    xq_ = fw.sb("xq_", [128, 128], F32, es=es_c)
                    V(lambda h, geluT=geluT: h.memset(geluT[:], 0.0), writes=[geluT])
                    for kb in range(nkc):
                        tr_ = tokr[kb % 2]
                        load_rows(tr_[:], tr_, kb, 0, 512)
                        bt = bank()
                        for c4 in range(4):
                            PE(lambda h, bt=bt, c4=c4, tr_=tr_: h.transpose(out=bt[:, c4 * 128:(c4 + 1) * 128], in_=tr_[:, c4 * 128:(c4 + 1) * 128], identity=ident[:]),
                               reads=[tr_, ident], writes=[bt])
                        vcopy(kvcT[:, 0:4, kb * 128:(kb + 1) * 128], bt[:, :].rearrange("p (c n) -> p c n", n=128), [bt], [kvcT])
                    for kvi, wname in enumerate(('cmp_k_w1', 'cmp_v_w1')):
                        for rr in range(2):
                            for hf_ in range(2):
                                DG(lambda h, rr=rr, hf_=hf_, wname=wname: h.dma_start(out=wt2[rr][64 * hf_:64 * hf_ + 64, :, 0:256],
                                                                                      in_=IN[wname][0, rr].rearrange("(l d) h -> d l h", d=64)), writes=[wt2[rr]])
                        for g in range(4):
                            gh = g % 2
                            pr_ = slice(64 * gh, 64 * gh + 64)
                            for hc in range(2):
                                b0 = bank()
                                for rr in range(2):
                                    for l in range(16):
                                        mm(b0[:, rr * 128:rr * 128 + nch], wt2[rr][pr_, l, hc * 128:(hc + 1) * 128],
                                           kvcT[pr_, kvi * 2 + g // 2, l:nch * 16:16], [wt2[rr], kvcT], [b0], start=(l == 0), stop=(l == 15), inc=(l == 15))
                                ts(xs_[:, 0:ncmp], b0[:, 0:ncmp], cst[:, kvi * 2 + hc:kvi * 2 + hc + 1], None, ALU.add, None, [b0, cst], [xs_])
                                tt(xs_[:, 0:ncmp], xs_[:, 0:ncmp], b0[:, 129:129 + ncmp], ALU.add, [xs_, b0], [xs_])
                                act(geluT[:, kvi, hc, g, 0:ncmp], xs_[:, 0:ncmp], AF.Gelu_apprx_tanh, [xs_], [geluT])
                    for g in range(4):
                        bk_ = bank()
                        for hc in range(2):
                            mm(bk_[:, 0:128], w2kd[:, hc, :], geluT[:, 0, hc, g, :], [w2kd, geluT], [bk_], start=(hc == 0), stop=(hc == 1))
                        ts(xs_[:], bk_[:, 0:128], cst[:, 4:5], None, ALU.add, None, [bk_, cst], [xs_])
                        act(xq_[:], xs_[:], AF.Square, [xs_], [xq_])
                        bss = bank()
                        mm(bss[:, 0:128], blk1[:], xq_[:], [blk1, xq_], [bss])
                        act(xq_[:], bss[:, 0:128], AF.Sqrt, [bss, eps6], [xq_], bias=eps6[:, 0:1], scale=1.0 / 64)
                        V(lambda h, xq_=xq_: h.reciprocal(out=xq_[:], in_=xq_[:]), reads=[xq_], writes=[xq_])
                        tt(xs_[:], xs_[:], xq_[:], ALU.mult, [xs_, xq_], [xs_])
                        ts(KcT[:, g, :], xs_[:], cst[:, 5:6], None, ALU.mult, None, [xs_, cst], [KcT])
                        bv_ = bank()
                        for hc in range(2):
                            mm(bv_[:, 0:64], geluT[:, 1, hc, g, :], w2v[:, hc, :], [geluT, w2v], [bv_], start=(hc == 0), stop=(hc == 1))
                        tt(Vc[:, g, 0:64], bv_[:, 0:64], b2vb[:], ALU.add, [bv_, b2vb], [Vc])
                        vcopy(Vc[:, g, 64:97], ovc[:], [ovc], [Vc])
                    fw.barrier()
                    es_c.close()

                    es_a = ExitStack()
                    ebuf = [fw.sb(f"ebuf{i}", [128, NQ], BF16, es=es_a) for i in range(3)]
                    mskb = [fw.sb(f"mskb{i}", [128, NQ], BF16, es=es_a) for i in range(2)]
                    wmk = [fw.sb(f"wmk{i}", [128, NQ], BF16, es=es_a) for i in range(2)]
                    G3 = fw.sb("G3", [128, 32, 32], F32, es=es_a)
                    sel_all = fw.sb("sel_all", [128, nqs, 32], F32, es=es_a)
                    selx = fw.sb("selx", [128, nqs, 128], F32, es=es_a)
                    impg = fw.sb("impg", [128, nqs, 32], F32, es=es_a)
                    onsa = fw.sb("onsa", [128, nqs, 256], F32, es=es_a)
                    sc_ = fw.sb("sc_", [128, 32], F32, es=es_a)
                    cnt_ = fw.sb("cnt_", [128, 32], F32, es=es_a)
                    w4 = fw.sb("w4", [128, 4], F32, es=es_a)
                    w4g = fw.sb("w4g", [128, 4], F32, es=es_a)
                    ktk = [fw.sb(f"ktk{i}", [128, 64], F32, es=es_a) for i in range(2)]
                    vtk = [fw.sb(f"vtk{i}", [128, 64], F32, es=es_a) for i in range(2)]
                    ksd = [fw.sb(f"ksd{i}", [128, 128], F32, es=es_a) for i in range(2)]
                    kTb = [fw.sb(f"kTb{i}", [128, 128], BF16, es=es_a) for i in range(2)]
                    VSb = [fw.sb(f"VSb{i}", [128, 65], BF16, es=es_a) for i in range(2)]
                    for t_ in VSb:
                        V(lambda h, t_=t_: h.memset(t_[:], 1.0), writes=[t_])
                    bank_pool[:] = banks[0:4]
                    eb_i = [0]
                    kv_i = [0]

                    def combine(g, hi, br, first, with_imp):
                        h16 = 4 * g + hi
                        accb = banks[4 + hi]
                        a3 = accb[:, 0:nqs * 128].rearrange("p (s n) -> p s n", n=128)
                        ts(w4[:, 0:nqs], a3[:, :, 64], 1e-30, None, ALU.max, None, [accb], [w4])
                        V(lambda h: h.reciprocal(out=w4[:, 0:nqs], in_=w4[:, 0:nqs]), reads=[w4], writes=[w4])
                        tt(w4g[:, 0:nqs], w4[:, 0:nqs], gat[:, :, 3 * h16 + br], ALU.mult, [w4, gat], [w4g])
                        for qs in range(nqs):
                            dst = onsa[:, qs, hi * 64:(hi + 1) * 64]
                            if first:
                                ts(dst, a3[:, qs, 0:64], w4g[:, qs:qs + 1], None, ALU.mult, None, [accb, w4g], [onsa])
                            else:
                                stt(dst, a3[:, qs, 0:64], w4g[:, qs:qs + 1], dst, ALU.mult, ALU.add, [accb, w4g, onsa], [onsa])
                            if with_imp:
                                stt(impg[:, qs, :], a3[:, qs, 65:97], w4[:, qs:qs + 1], impg[:, qs, :], ALU.mult, ALU.add, [accb, w4, impg], [impg])

                    def attend_block(g, KT_ap_fn, KT_buf, Vblk, Vw, msk, qT_, first):
                        for hi in range(4):
                            h16 = 4 * g + hi
                            hh, hp = h16 % 2, h16 // 2
                            pr_ = slice(64 * hh, 64 * hh + 64)
                            bS = bank()
                            mm(bS[:, 0:NQ], KT_ap_fn(pr_), qT_[pr_, hp, :], [KT_buf, qT_], [bS])
                            e = ebuf[eb_i[0] % 3]
                            eb_i[0] += 1
                            act(e[:], bS[:, 0:NQ], AF.Exp, [bS], [e], scale=0.125)
                            if msk is not None:
                                tt(e[:], e[:], msk[:], ALU.mult, [e, msk], [e])
                            accb = banks[4 + hi]
                            if first:
                                for z_ in range(nqs):
                                    PE(lambda h, accb=accb, z_=z_: h.matmul(accb[:, z_ * 128:(z_ + 1) * 128], lhsT=zer_bf[:], rhs=ones_bf[:, 0:128], start=(z_ == 0), stop=True,
                                                                            skip_group_check=True),
                                       reads=[zer_bf, ones_bf], writes=[accb])
                            for qs in range(nqs):
                                PE(lambda h, accb=accb, qs=qs, e=e: h.matmul(accb[:, qs * 128:qs * 128 + Vw], lhsT=e[:, qs * 128:(qs + 1) * 128], rhs=Vblk,
                                                                             start=False, stop=True, skip_group_check=True),
                                   reads=[e, Vc, VSb[0], VSb[1]], writes=[accb])

                    def load_kv_block(loader):
                        i_ = kv_i[0] % 2
                        kv_i[0] += 1
                        loader(ktk[i_], vtk[i_])
                        V(lambda h: h.tensor_copy(out=ksd[i_][:, :].rearrange("p (t d) -> p t d", d=64), in_=ktk[i_][:, :].unsqueeze(1).to_broadcast([128, 2, 64])),
                          reads=[ktk[i_]], writes=[ksd[i_]])
                        bt = bank()
                        PE(lambda h: h.transpose(out=bt[:, 0:128], in_=ksd[i_][:], identity=ident[:]), reads=[ksd[i_], ident], writes=[bt])
                        acopy(kTb[i_][:], bt[:, 0:128], [bt], [kTb[i_]])
                        vcopy(VSb[i_][:, 0:64], vtk[i_][:], [vtk[i_]], [VSb[i_]])
                        return kTb[i_], VSb[i_]

                    if not prm:
                        onsa4 = fw.sb("onsa4", [128, 4, 256], F32, es=es_a)
                        sel4 = fw.sb("sel4", [128, 4, 32], F32, es=es_a)
                        imp4 = fw.sb("imp4", [128, 4, 32], F32, es=es_a)
                        selx4 = fw.sb("selx4", [128, 4, 128], F32, es=es_a)
                        mska = [fw.sb(f"mska{i}", [128, 512], BF16, es=es_a) for i in range(2)]
                        rowsb = [fw.sb(f"rowsb{i}", [128, 512], F32, es=es_a) for i in range(2)]
                        V(lambda h, imp4=imp4: h.memset(imp4[:], 0.0), writes=[imp4])

                        def s_attend(g, KT_ap_fn, KT_buf, Vblk, Vw, msk_ap, msk_buf, qT_, zero):
                            for hi in range(4):
                                h16 = 4 * g + hi
                                hh, hp = h16 % 2, h16 // 2
                                pr_ = slice(64 * hh, 64 * hh + 64)
                                bS = bank()
                                mm(bS[:, 0:NQ], KT_ap_fn(pr_), qT_[pr_, hp, :], [KT_buf, qT_], [bS])
                                e = ebuf[eb_i[0] % 3]
                                eb_i[0] += 1
                                act(e[:], bS[:, 0:NQ], AF.Exp, [bS], [e], scale=0.125)
                                if msk_ap is not None:
                                    tt(e[:], e[:], msk_ap, ALU.mult, [e, msk_buf], [e])
                                accb = banks[4 + hi]
                                if zero:
                                    for z_ in range(4):
                                        PE(lambda h, accb=accb, z_=z_: h.matmul(accb[:, z_ * 128:(z_ + 1) * 128], lhsT=zer_bf[:], rhs=ones_bf[:, 0:128], start=(z_ == 0), stop=True,
                                                                                skip_group_check=True),
                                           reads=[zer_bf, ones_bf], writes=[accb])
                                PE(lambda h, accb=accb, g=g, e=e, Vblk=Vblk, Vw=Vw: h.matmul(accb[:, g * 128:g * 128 + Vw], lhsT=e[:, 0:128], rhs=Vblk,
                                                                                           start=False, stop=True, skip_group_check=True),
                                   reads=[e, Vc, VSb[0], VSb[1]], writes=[accb])

                        def s_combine(g, hi, br, first, with_imp):
                            h16 = 4 * g + hi
                            accb = banks[4 + hi]
                            a3 = accb[:, 0:512].rearrange("p (s n) -> p s n", n=128)
                            ts(w4[:, 0:1], a3[:, g, 64:65], 1e-30, None, ALU.max, None, [accb], [w4])
                            V(lambda h: h.reciprocal(out=w4[:, 0:1], in_=w4[:, 0:1]), reads=[w4], writes=[w4])
                            tt(w4g[:, 0:1], w4[:, 0:1], gat[:, 0, 3 * h16 + br:3 * h16 + br + 1], ALU.mult, [w4, gat], [w4g])
                            dst = onsa4[:, g, hi * 64:(hi + 1) * 64]
                            if first:
                                ts(dst, a3[:, g, 0:64], w4g[:, 0:1], None, ALU.mult, None, [accb, w4g], [onsa4])
                            else:
                                stt(dst, a3[:, g, 0:64], w4g[:, 0:1], dst, ALU.mult, ALU.add, [accb, w4g, onsa4], [onsa4])
                            if with_imp:
                                stt(imp4[:, g, :], a3[:, g, 65:97], w4[:, 0:1], imp4[:, g, :], ALU.mult, ALU.add, [accb, w4, imp4], [imp4])

                        def s_kv(rb, g, kcol, vcol):
                            i_ = kv_i[0] % 2
                            kv_i[0] += 1
                            V(lambda h: h.tensor_copy(out=ksd[i_][:, :].rearrange("p (t d) -> p t d", d=64), in_=rb[:, kcol:kcol + 64].unsqueeze(1).to_broadcast([128, 2, 64])),
                              reads=[rb], writes=[ksd[i_]])
                            bt = bank()
                            PE(lambda h: h.transpose(out=bt[:, 0:128], in_=ksd[i_][:], identity=ident[:]), reads=[ksd[i_], ident], writes=[bt])
                            acopy(kTb[i_][:], bt[:, 0:128], [bt], [kTb[i_]])
                            vcopy(VSb[i_][:, 0:64], rb[:, vcol:vcol + 64], [rb], [VSb[i_]])
                            return kTb[i_], VSb[i_]

                        for g in range(4):
                            s_attend(g, lambda pr_, g=g: KcT[pr_, g, :], KcT, Vc[:, g, 0:97], 97, cmask[:], cmask, qnT, g == 0)
                        for g in range(4):
                            for hi in range(4):
                                s_combine(g, hi, 0, True, True)
                        for g in range(4):
                            tt(sc_[:], imp4[:, g, :], seltab[:, 0, 0:32], ALU.mult, [imp4, seltab], [sc_])
                            tt(sc_[:], sc_[:], seltab[:, 0, 32:64], ALU.add, [sc_, seltab], [sc_])
                            V(lambda h, G3=G3, sc_=sc_: h.tensor_tensor(out=G3[:], in0=sc_[:, :].unsqueeze(1).to_broadcast([128, 32, 32]),
                                                                        in1=sc_[:, :].unsqueeze(2).to_broadcast([128, 32, 32]), op=ALU.is_gt), reads=[sc_], writes=[G3])
                            V(lambda h, G3=G3, cnt_=cnt_: h.tensor_reduce(out=cnt_[:], in_=G3[:], axis=AX.X, op=ALU.add), reads=[G3], writes=[cnt_])
                            ts(cnt_[:], cnt_[:], 15.0, None, ALU.is_lt, None, [cnt_], [cnt_])
                            tt(sel4[:, g, :], cnt_[:], seltab[:, 0, 64:96], ALU.mult, [cnt_, seltab], [sel4])
                        for kb in range(nks):
                            rb = rowsb[kb % 2]
                            load_rows(rb[:], rb, kb, 512, 1024)
                            if kb < 16:
                                V(lambda h, kb=kb, selx4=selx4, sel4=sel4: h.tensor_copy(out=selx4[:, :, :].rearrange("p s (t d) -> p s t d", d=64),
                                                                                         in_=sel4[:, :, 2 * kb:2 * kb + 2].unsqueeze(3).to_broadcast([128, 4, 2, 64])),
                                  reads=[sel4], writes=[selx4])
                                bm = bank()
                                for g in range(4):
                                    PE(lambda h, bm=bm, g=g, selx4=selx4: h.transpose(out=bm[:, g * 128:(g + 1) * 128], in_=selx4[:, g, :], identity=ident[:]),
                                       reads=[selx4, ident], writes=[bm])
                                mk_ = mska[kb % 2]
                                acopy(mk_[:], bm[:, 0:512], [bm], [mk_])
                            for g in range(4):
                                KT_, VS_ = s_kv(rb, g, g * 64, 256 + g * 64)
                                if kb < 16:
                                    s_attend(g, lambda pr_, KT_=KT_: KT_[pr_, :], KT_, VS_[:, 0:65], 65, mk_[:, g * 128:(g + 1) * 128], mk_, qrT, kb == 0 and g == 0)
                                else:
                                    s_attend(g, lambda pr_, KT_=KT_: KT_[pr_, :], KT_, VS_[:, 0:65], 65, row0m[:], row0m, qrT, False)
                        for g in range(4):
                            for hi in range(4):
                                s_combine(g, hi, 1, False, False)
                        for i_ in range(4):
                            rb = rowsb[i_ % 2]
                            DS(lambda h, rb=rb, i_=i_, bseq=bseq: h.dma_start(out=rb[:], in_=OUT['win_s'][bseq, i_ * 128:(i_ + 1) * 128, :]), reads=[OUT['win_s']], writes=[rb])
                            for g in range(4):
                                KT_, VS_ = s_kv(rb, g, g * 64, 256 + g * 64)
                                s_attend(g, lambda pr_, KT_=KT_: KT_[pr_, :], KT_, VS_[:, 0:65], 65, None, None, qrT, i_ == 0 and g == 0)
                        for g in range(4):
                            for hi in range(4):
                                s_combine(g, hi, 2, False, False)
                        stt(onsa_tot[:, :, :], onsa4[:, :, :], ident[:, bseq:bseq + 1], onsa_tot[:, :, :], ALU.mult, ALU.add, [onsa4, ident, onsa_tot], [onsa_tot])
                    for g in (range(4) if prm else []):
                        V(lambda h, impg=impg: h.memset(impg[:], 0.0), writes=[impg])
                        attend_block(g, lambda pr_, g=g: KcT[pr_, g, :], KcT, Vc[:, g, 0:97], 97, cmask, qnT, True)
                        for hi in range(4):
                            combine(g, hi, 0, True, True)
                        for qs in range(nqs):
                            tt(sc_[:], impg[:, qs, :], seltab[:, qs, 0:32], ALU.mult, [impg, seltab], [sc_])
                            tt(sc_[:], sc_[:], seltab[:, qs, 32:64], ALU.add, [sc_, seltab], [sc_])
                            V(lambda h, G3=G3, sc_=sc_: h.tensor_tensor(out=G3[:], in0=sc_[:, :].unsqueeze(1).to_broadcast([128, 32, 32]),
                                                                        in1=sc_[:, :].unsqueeze(2).to_broadcast([128, 32, 32]), op=ALU.is_gt), reads=[sc_], writes=[G3])
                            V(lambda h, G3=G3, cnt_=cnt_: h.tensor_reduce(out=cnt_[:], in_=G3[:], axis=AX.X, op=ALU.add), reads=[G3], writes=[cnt_])
                            ts(cnt_[:], cnt_[:], 16.0 if prm else 15.0, None, ALU.is_lt, None, [cnt_], [cnt_])
                            tt(sel_all[:, qs, :], cnt_[:], seltab[:, qs, 64:96], ALU.mult, [cnt_, seltab], [sel_all])
                        for kb in range(nks):
                            def ld(kt, vt, kb=kb, g=g):
                                load_rows(kt[:], kt, kb, 512 + g * 64, 576 + g * 64)
                                load_rows(vt[:], vt, kb, 768 + g * 64, 832 + g * 64)
                            KT_, VS_ = load_kv_block(ld)
                            if kb < 16:
                                V(lambda h, kb=kb, selx=selx, sel_all=sel_all: h.tensor_copy(out=selx[:, :, :].rearrange("p s (t d) -> p s t d", d=64),
                                                                                             in_=sel_all[:, :, 2 * kb:2 * kb + 2].unsqueeze(3).to_broadcast([128, nqs, 2, 64])),
                                  reads=[sel_all], writes=[selx])
                                bm = bank()
                                for qs in range(nqs):
                                    PE(lambda h, bm=bm, qs=qs, selx=selx: h.transpose(out=bm[:, qs * 128:(qs + 1) * 128], in_=selx[:, qs, :], identity=ident[:]),
                                       reads=[selx, ident], writes=[bm])
                                mk_ = mskb[kb % 2]
                                if prm and kb >= 4 * jt:
                                    i_ = kb - 4 * jt
                                    wm_ = wmk[kb % 2]
                                    G(lambda h, wm_=wm_, i_=i_: h.affine_select(out=wm_[:], in_=ones_bf[:], pattern=[[1, NQ]], compare_op=ALU.is_ge, fill=0.0,
                                                                                base=-128 * i_, channel_multiplier=-1), reads=[ones_bf], writes=[wm_])
                                    tt(mk_[:], bm[:, 0:NQ], wm_[:], ALU.mult, [bm, wm_], [mk_])
                                else:
                                    acopy(mk_[:], bm[:, 0:NQ], [bm], [mk_])
                            else:
                                mk_ = row0m
                            attend_block(g, lambda pr_, KT_=KT_: KT_[pr_, :], KT_, VS_[:, 0:65], 65, mk_, qrT, kb == 0)
                        for hi in range(4):
                            combine(g, hi, 1, False, False)
                        if prm:
                            wlist = [i_ for i_ in range(8) if 4 * jt - 4 + i_ >= 0]
                        else:
                            wlist = [0, 1, 2, 3]
                        for i_ in wlist:
                            if prm:
                                kbw = 4 * jt - 4 + i_

                                def ldw(kt, vt, kbw=kbw, g=g):
                                    DS(lambda h: h.dma_start(out=kt[:], in_=WIN[kbw * 128:(kbw + 1) * 128, g * 64:g * 64 + 64]), reads=[WIN], writes=[kt])
                                    DS(lambda h: h.dma_start(out=vt[:], in_=WIN[kbw * 128:(kbw + 1) * 128, 256 + g * 64:320 + g * 64]), reads=[WIN], writes=[vt])
                                wm_ = wmk[i_ % 2]
                                if i_ < 4:
                                    G(lambda h, wm_=wm_, i_=i_: h.affine_select(out=wm_[:], in_=ones_bf[:], pattern=[[-1, NQ]], compare_op=ALU.is_ge, fill=0.0,
                                                                                base=128 * i_ - 1, channel_multiplier=1), reads=[ones_bf], writes=[wm_])
                                else:
                                    G(lambda h, wm_=wm_, i_=i_: h.affine_select(out=wm_[:], in_=ones_bf[:], pattern=[[1, NQ]], compare_op=ALU.is_ge, fill=0.0,
                                                                                base=512 - 128 * i_, channel_multiplier=-1), reads=[ones_bf], writes=[wm_])
                            else:
                                def ldw(kt, vt, i_=i_, g=g, bseq=bseq):
                                    DS(lambda h: h.dma_start(out=kt[:], in_=OUT['win_s'][bseq, i_ * 128:(i_ + 1) * 128, g * 64:g * 64 + 64]), reads=[OUT['win_s']], writes=[kt])
                                    DS(lambda h: h.dma_start(out=vt[:], in_=OUT['win_s'][bseq, i_ * 128:(i_ + 1) * 128, 256 + g * 64:320 + g * 64]), reads=[OUT['win_s']], writes=[vt])
                                wm_ = None
                            KT_, VS_ = load_kv_block(ldw)
                            attend_block(g, lambda pr_, KT_=KT_: KT_[pr_, :], KT_, VS_[:, 0:65], 65, wm_, qrT, i_ == wlist[0])
                        for hi in range(4):
                            combine(g, hi, 2, False, False)
                        if prm:
                            for qs in range(nqs):
                                bt = bank()
                                for c2 in range(2):
                                    PE(lambda h, bt=bt, c2=c2, qs=qs, onsa=onsa: h.transpose(out=bt[:, c2 * 128:(c2 + 1) * 128], in_=onsa[:, qs, c2 * 128:(c2 + 1) * 128], identity=ident[:]),
                                       reads=[onsa, ident], writes=[bt])
                                vcopy(oT[:, 2 * g:2 * g + 2, qs * 128:(qs + 1) * 128], bt[:, 0:256].rearrange("p (c n) -> p c n", n=128), [bt], [oT])
                        else:
                            stt(onsa_tot[:, g, :], onsa[:, 0, :], ident[:, bseq:bseq + 1], onsa_tot[:, g, :], ALU.mult, ALU.add, [onsa, ident, onsa_tot], [onsa_tot])
                    fw.barrier()
                    es_a.close()
                bank_pool[:] = saved_pool
                if not prm:
                    for g in range(4):
                        bt = bank()
                        for c2 in range(2):
                            PE(lambda h, bt=bt, c2=c2, g=g: h.transpose(out=bt[:, c2 * 128:c2 * 128 + NSMP], in_=onsa_tot[0:NSMP, g, c2 * 128:(c2 + 1) * 128], identity=ident[0:NSMP, 0:NSMP]),
                               reads=[onsa_tot, ident], writes=[bt])
                        vcopy(oT[:, 2 * g:2 * g + 2, 0:NSMP], bt[:, 0:256].rearrange("p (c n) -> p c n", n=128)[:, :, 0:NSMP], [bt], [oT])
                    fw.barrier()
                es_n.close()
            if prm and last:
                for h8 in range(8):
                    hp, hh = h8 // 2, h8 % 2
                    DS(lambda h, h8=h8, hp=hp, hh=hh: h.dma_start(out=OUT['ret_p'][h8], in_=S2[64 * hh:64 * hh + 64, hp, :]), reads=[S2], writes=[OUT['ret_p']])
            fw.barrier()
            Wo = IN['cd_w_out'][0]
            for db in range(KC):
                bk = proj_fm(Wo, db * 128, 128, N, src=oT)
                tt(xT[:, db, :N], xT[:, db, :N], bk[:, :N], ALU.add, [xT, bk], [xT])
            fw.barrier()

    tiles = [("p", j) for j in range(NTILE)] + [("s", 0)]
    if stage in (0, 1, 2):
        tiles = [("p", 0), ("s", 0)]
    for kind, j in tiles:
        if kind == "p":
            N = TT
            load_xT(IN['xp'][j * TT:(j + 1) * TT, :], N)
        else:
            N = NSMP
            load_xT(IN['xs'][:, :], N)
        last = (kind == "s") or (j == NTILE - 1) or stage in (0, 1, 2)
        ffn(N, 0, 1)
        if stage >= 1:
            even_mixer(kind, N, last)
        if stage >= 2:
            import os as _os
            if _os.environ.get("DBG_FFNMID", "1") == "1":
                ffn(N, 0, 2)
                ffn(N, 1, 1)
            if _os.environ.get("DBG_ODD_" + kind.upper(), "1") == "1":
                odd_mixer(kind, N, j, last)
        if stage >= 3:
            ffn(N, 1, 2)
        if kind == "p":
            store_xT(OUT['yp'][j * TT:(j + 1) * TT, :], OUT['yp'], N)
        else:
            store_xT(OUT['ys'][:, :], OUT['ys'], N)
    counts = fw.finish()
    return nc, counts


OUT_ORDER = ['yp', 'ys', 'lru_h_p', 'lru_h_s', 'lru_conv_p', 'lru_conv_s', 'shift_p', 'shift_s', 'wkv_p', 'wkv_s',
             'kv_p', 'kv_s', 'win_p', 'win_s', 'ret_p', 'ret_s']


def make_in_maps(inp, cores):
    maps = []
    consts = make_consts()
    f = lambda a: np.ascontiguousarray(np.asarray(a, dtype=np.float32))
    for c in cores:
        b = c % 4
        s0 = c * NSMP
        m = {
            'xp': f(inp['x_prompt'][b]),
            'xs': f(inp['x_sample'][s0:s0 + NSMP, 0]),
            's_lru_h': f(inp['state_lru_h'][0, s0:s0 + NSMP]),
            's_lru_conv': f(inp['state_lru_conv'][0, s0:s0 + NSMP]).reshape(NSMP * 3, 1024),
            's_shift': f(inp['state_rwkv_shift'][0, s0:s0 + NSMP]),
            's_wkv': f(inp['state_rwkv_wkv'][0, s0:s0 + NSMP]),
            'cache_kv': f(inp['cache_nsa_kv'][0]).reshape(NPOOL * 256, 512),
            'cache_win': f(inp['cache_nsa_win'][0, s0:s0 + NSMP]).reshape(NSMP, 512, 512),
            's_ret': f(inp['state_ret'][0, s0:s0 + NSMP]),
            'page_table': np.ascontiguousarray(np.asarray(inp['page_table'][s0:s0 + NSMP], dtype=np.int32)),
        }
        for k in WEIGHT_NAMES:
            m[k] = f(inp[k])
        m.update(consts)
        maps.append(m)
    return maps


_CACHE = {}


def kernel(**inputs):
    if 'nc' not in _CACHE:
        _CACHE['nc'] = build()[0]
    nc = _CACHE['nc']
    cores = list(range(8))
    maps = make_in_maps(inputs, cores)
    res = run_bass_kernel_spmd(nc, maps, core_ids=cores)
    R = res.results
    B = 4
    o = {}
    o['yp'] = np.stack([R[b]['yp'] for b in range(B)])
    o['ys'] = np.concatenate([R[c]['ys'] for c in cores])[:, None, :]
    o['lru_h_p'] = np.stack([R[b]['lru_h_p'][0] for b in range(B)])[None]
    o['lru_h_s'] = np.concatenate([R[c]['lru_h_s'] for c in cores])[None]
    o['lru_conv_p'] = np.stack([R[b]['lru_conv_p'] for b in range(B)])[None]
    o['lru_conv_s'] = np.concatenate([R[c]['lru_conv_s'].reshape(NSMP, 3, 1024) for c in cores])[None]
    o['shift_p'] = np.stack([R[b]['shift_p'][0] for b in range(B)])[None]
    o['shift_s'] = np.concatenate([R[c]['shift_s'] for c in cores])[None]
    o['wkv_p'] = np.stack([R[b]['wkv_p'] for b in range(B)])[None]
    o['wkv_s'] = np.concatenate([R[c]['wkv_s'] for c in cores])[None]
    o['kv_p'] = np.stack([R[b]['kv_p'].reshape(T, 4, 4, 64) for b in range(B)])[None]
    o['kv_s'] = np.concatenate([R[c]['kv_s'].reshape(NSMP, 1, 4, 4, 64) for c in cores])[None]
    o['win_p'] = np.stack([R[b]['win_p'].reshape(512, 2, 4, 64) for b in range(B)])[None]
    o['win_s'] = np.concatenate([R[c]['win_s'].reshape(NSMP, 512, 2, 4, 64) for c in cores])[None]
    o['ret_p'] = np.stack([R[b]['ret_p'] for b in range(B)])[None]
    o['ret_s'] = np.concatenate([R[c]['ret_s'] for c in cores])[None]
    return tuple(np.ascontiguousarray(o[k], dtype=np.float32) for k in OUT_ORDER)
```

```python
import math
import numpy as np
from contextlib import ExitStack
import concourse.bass as bass
import concourse.mybir as mybir
from concourse.bass_utils import run_bass_kernel_spmd

F32 = mybir.dt.float32
BF16 = mybir.dt.bfloat16
I32 = mybir.dt.int32
AF = mybir.ActivationFunctionType
ALU = mybir.AluOpType
AX = mybir.AxisListType

SAME_ENGINE_SYNC = True
SEM_ROTATE = 30000
N_DMA_SLOTS = 16


class Buf:
    __slots__ = ("name", "last_w", "readers", "t")

    def __init__(self, name, t=None):
        self.name = name
        self.last_w = None
        self.readers = []
        self.t = t

    def __getitem__(self, k):
        return self.t[k]


class Eng:
    def __init__(self, name):
        self.name = name
        self.ops = []
        self.known = {}
        self.cnt = 0
        self.sem_id = None
        self.slots = []
        self.slot_next = 0
        self.pending = False


def _compact(readers):
    m = {}
    for s, v in readers:
        m[s] = max(m.get(s, 0), v)
    return list(m.items())


class FW:
    def __init__(self, nc):
        self.nc = nc
        self.es = ExitStack()
        self.engs = {n: Eng(n) for n in ("sync", "scalar", "gpsimd", "vector", "tensor")}
        self.sems = {}
        self.nsem = 0
        self.nbuf = 0
        for e in self.engs.values():
            self._new_eng_sem(e)
        for qn in ("sync", "scalar", "gpsimd"):
            q = self.engs[qn]
            for i in range(N_DMA_SLOTS):
                q.slots.append([self._alloc_sem(f"d_{qn}_{i}"), 0])

    def _alloc_sem(self, name):
        h = self.es.enter_context(self.nc.semaphore(f"{name}_{self.nsem}"))
        sid = self.nsem
        self.nsem += 1
        self.sems[sid] = h
        return sid

    def _new_eng_sem(self, e):
        e.sem_id = self._alloc_sem(f"e_{e.name}")
        e.cnt = 0

    def sb(self, name, shape, dtype=F32, es=None):
        t = (es or self.es).enter_context(self.nc.sbuf_tensor(f"{name}_{self.nbuf}", list(shape), dtype))
        self.nbuf += 1
        return Buf(name, t)

    def ps(self, name, shape, dtype=F32):
        t = self.es.enter_context(self.nc.psum_tensor(f"{name}_{self.nbuf}", list(shape), dtype))
        self.nbuf += 1
        return Buf(name, t)

    def dram(self, name, shape, dtype=F32, kind="Internal"):
        t = self.nc.dram_tensor(name, list(shape), dtype, kind=kind)
        return Buf(name, t.ap())

    def _waits(self, e, reads, writes):
        need = {}
        for b in reads:
            if b.last_w is not None:
                s, v = b.last_w
                need[s] = max(need.get(s, 0), v)
        for b in writes:
            if b.last_w is not None:
                s, v = b.last_w
                need[s] = max(need.get(s, 0), v)
            for (s, v) in b.readers:
                need[s] = max(need.get(s, 0), v)
        for s, v in need.items():
            if s == e.sem_id and (not SAME_ENGINE_SYNC or v > e.cnt):
                continue
            if e.known.get(s, 0) >= v:
                continue
            e.known[s] = v
            e.ops.append(("w", s, v))

    def _record(self, ev, reads, writes):
        for b in writes:
            b.last_w = ev
            b.readers = []
        for b in reads:
            if b not in writes:
                b.readers.append(ev)
                if len(b.readers) > 48:
                    b.readers = _compact(b.readers)

    def op(self, eng, fn, reads=(), writes=(), inc=True):
        e = self.engs[eng]
        self._waits(e, reads, writes)
        if inc:
            if e.cnt >= SEM_ROTATE and not e.pending:
                self._new_eng_sem(e)
            e.cnt += 1
            e.pending = False
            e.ops.append(("o", fn, e.sem_id, 1))
            self._record((e.sem_id, e.cnt), reads, writes)
        else:
            e.pending = True
            e.ops.append(("o", fn, None, 0))
            self._record((e.sem_id, e.cnt + 1), reads, writes)

    def dma(self, q, fn, reads=(), writes=()):
        e = self.engs[q]
        self._waits(e, reads, writes)
        slot = e.slots[e.slot_next % N_DMA_SLOTS]
        e.slot_next += 1
        sid, val = slot
        if val > 0 and e.known.get(sid, 0) < val:
            e.known[sid] = val
            e.ops.append(("w", sid, val))
        slot[1] = val + 16
        e.ops.append(("o", fn, sid, 16))
        self._record((sid, val + 16), reads, writes)

    def barrier(self):
        evs = []
        for e in self.engs.values():
            if e.cnt > 0:
                evs.append((e.sem_id, e.cnt))
            for sid, val in e.slots:
                if val > 0:
                    evs.append((sid, val))
        for e in self.engs.values():
            assert not e.pending
            for s, v in evs:
                if s == e.sem_id:
                    continue
                if e.known.get(s, 0) < v:
                    e.known[s] = v
                    e.ops.append(("w", s, v))

    def finish(self):
        self.barrier()
        nc, sems, engs = self.nc, self.sems, self.engs

        def replay(name):
            def f(h):
                for o in engs[name].ops:
                    if o[0] == "w":
                        h.wait_ge(sems[o[1]], o[2])
                    else:
                        ins = o[1](h)
                        if o[2] is not None:
                            ins.then_inc(sems[o[2]], o[3])
            return f

        with nc.allow_non_contiguous_dma(reason="small strided param loads"), nc.Block() as block:
            block.sync(replay("sync"))
            block.scalar(replay("scalar"))
            block.gpsimd(replay("gpsimd"))
            block.vector(replay("vector"))
            block.tensor(replay("tensor"))
        counts = {n: len(x.ops) for n, x in engs.items()}
        self.es.close()
        return counts


D = 2048
DFF = 5504
T = 2048
TT = 512
NTILE = T // TT
NSMP = 16
KC = 16
FC = 43
LRU_W = 1024
RW = 1024
SHIFT_W = 3360
AB_COLS = 5408
CD_COLS = 5680
NPOOL = 2560

WEIGHT_NAMES = [
    'norm_ffn1', 'ffn1_w_in', 'ffn1_w_out', 'norm_mix', 'norm_ffn2', 'ffn2_w_in', 'ffn2_w_out',
    'ab_w_in', 'lru_conv_w', 'lru_conv_b', 'lru_wa', 'lru_ba', 'lru_wx', 'lru_bx', 'lru_lambda',
    'rwkv_mu', 'rwkv_w0', 'rwkv_w2', 'rwkv_a0', 'rwkv_a2', 'rwkv_g2', 'rwkv_k_k', 'rwkv_k_a', 'rwkv_r_k',
    'rwkv_ln_g', 'rwkv_ln_b', 'ab_w_out',
    'cd_w_in', 'nsa_q_norm', 'nsa_k_norm', 'cmp_k_w1', 'cmp_k_b1', 'cmp_k_w2', 'cmp_k_b2',
    'cmp_v_w1', 'cmp_v_b1', 'cmp_v_w2', 'cmp_v_b2', 'ret_gn_g', 'ret_gn_b', 'cd_w_out']

WEIGHT_SHAPES = {
    'norm_ffn1': (2, 2048), 'ffn1_w_in': (2, 2048, 11008), 'ffn1_w_out': (2, 5504, 2048), 'norm_mix': (2, 2048),
    'norm_ffn2': (2, 2048), 'ffn2_w_in': (2, 2048, 11008), 'ffn2_w_out': (2, 5504, 2048),
    'ab_w_in': (1, 2048, 5408), 'lru_conv_w': (1, 4, 1024), 'lru_conv_b': (1, 1024), 'lru_wa': (1, 16, 64, 64),
    'lru_ba': (1, 1024), 'lru_wx': (1, 16, 64, 64), 'lru_bx': (1, 1024), 'lru_lambda': (1, 1024),
    'rwkv_mu': (1, 3360), 'rwkv_w0': (1, 1024), 'rwkv_w2': (1, 64, 1024), 'rwkv_a0': (1, 1024),
    'rwkv_a2': (1, 64, 1024), 'rwkv_g2': (1, 160, 1024), 'rwkv_k_k': (1, 1024), 'rwkv_k_a': (1, 1024),
    'rwkv_r_k': (1, 16, 64), 'rwkv_ln_g': (1, 1024), 'rwkv_ln_b': (1, 1024), 'ab_w_out': (1, 2048, 2048),
    'cd_w_in': (1, 2048, 5680), 'nsa_q_norm': (1, 64), 'nsa_k_norm': (1, 3, 64),
    'cmp_k_w1': (1, 2, 1024, 256), 'cmp_k_b1': (1, 256), 'cmp_k_w2': (1, 256, 64), 'cmp_k_b2': (1, 64),
    'cmp_v_w1': (1, 2, 1024, 256), 'cmp_v_b1': (1, 256), 'cmp_v_w2': (1, 256, 64), 'cmp_v_b2': (1, 64),
    'ret_gn_g': (1, 1024), 'ret_gn_b': (1, 1024), 'cd_w_out': (1, 2048, 2048)}

STATE_SHAPES = {
    'xp': (T, D), 'xs': (NSMP, D), 's_lru_h': (NSMP, 1024), 's_lru_conv': (NSMP * 3, 1024),
    's_shift': (NSMP, SHIFT_W), 's_wkv': (NSMP, 16, 64, 64), 'cache_kv': (NPOOL * 256, 512),
    'cache_win': (NSMP, 512, 512), 's_ret': (NSMP, 8, 64, 128)}

CONST_SHAPES = {'rope_nsa': (T + 8, 16), 'rope_ret': (T + 8, 64), 'ret_dmaskT': (8, 128, 128), 'ret_dec': (128, 16),
                'sel_tab': (17, 128, 96), 'cmp_ov': (128, 33)}


def make_consts():
    f32 = np.float32
    pos = np.arange(T + 8, dtype=f32)

    def tab(half, theta):
        inv = np.exp(-np.log(f32(theta)) * np.arange(half, dtype=f32) / f32(half)).astype(f32)
        ang = (pos[:, None] * inv[None, :]).astype(f32)
        return np.concatenate([np.cos(ang), np.sin(ang)], axis=1).astype(f32)
    lg = np.log1p(-np.exp2(-5.0 - np.arange(8, dtype=f32))).astype(f32)
    i = np.arange(128, dtype=f32)
    diff = i[:, None] - i[None, :]
    dm = np.where(diff >= 0, np.exp(np.where(diff >= 0, diff, 0.0)[None] * lg[:, None, None]), 0.0).astype(f32)
    dec = np.zeros((128, 16), f32)
    dec[:, 0:8] = np.exp((i[:, None] + 1.0) * lg[None, :])
    dec[:, 8:16] = np.exp((128 - 1.0 - i)[:, None] * lg[None, :])
    sel = np.zeros((17, 128, 96), f32)
    sel[16, :, 0:32] = 1.0
    sel[16, :, [0, 31]] = 0.0
    sel[16, :, [32, 63]] = 1e4
    sel[16, :, 64:96] = 1.0
    for t_ in range(16):
        for p_ in range(128):
            qb = (128 * t_ + p_) // 64
            for j_ in range(32):
                valid = j_ <= qb
                forced = valid and (j_ == 0 or j_ == qb or j_ == qb - 1)
                sel[t_, p_, j_] = 1.0 if (valid and not forced) else 0.0
                sel[t_, p_, 32 + j_] = 1e4 if forced else (0.0 if valid else -1.0)
                sel[t_, p_, 64 + j_] = 1.0 if valid else 0.0
    ov = np.zeros((128, 33), f32)
    ov[:, 0] = 1.0
    for c_ in range(127):
        for j_ in range(32):
            o_ = min(16 * c_ + 32, 64 * j_ + 64) - max(16 * c_, 64 * j_)
            ov[c_, 1 + j_] = max(o_, 0) / 32.0
    return {'sel_tab': sel, 'cmp_ov': ov, 'rope_nsa': tab(8, 500000.0), 'rope_ret': tab(32, 10000.0),
            'ret_dmaskT': np.ascontiguousarray(dm.transpose(0, 2, 1)), 'ret_dec': dec}


OUT_SHAPES = {
    'yp': (T, D), 'ys': (NSMP, D), 'lru_h_p': (1, 1024), 'lru_h_s': (NSMP, 1024),
    'lru_conv_p': (3, 1024), 'lru_conv_s': (NSMP * 3, 1024), 'shift_p': (1, SHIFT_W), 'shift_s': (NSMP, SHIFT_W),
    'wkv_p': (16, 64, 64), 'wkv_s': (NSMP, 16, 64, 64), 'kv_p': (T, 1024), 'kv_s': (NSMP, 1024),
    'win_p': (512, 512), 'win_s': (NSMP, 512, 512), 'ret_p': (8, 64, 128), 'ret_s': (NSMP, 8, 64, 128)}


def build(stage=99):
    nc = bass.Bass("TRN2", target_bir_lowering=False)
    fw = FW(nc)
    IN = {}
    for k, s in STATE_SHAPES.items():
        IN[k] = fw.dram(k, s, F32, kind="ExternalInput")
    IN['page_table'] = fw.dram('page_table', (NSMP, 16), I32, kind="ExternalInput")
    for k_, s_ in CONST_SHAPES.items():
        IN[k_] = fw.dram(k_, s_, F32, kind="ExternalInput")
    for k in WEIGHT_NAMES:
        IN[k] = fw.dram(k, WEIGHT_SHAPES[k], F32, kind="ExternalInput")
    OUT = {k: fw.dram(k, s, F32, kind="ExternalOutput") for k, s in OUT_SHAPES.items()}
    WIN = fw.dram('scr_win', (T, 512))
    NSA_ON = True
    NSA_SAMPLE = True
    POOL2D = IN['cache_kv']
    SCR = {'q': fw.dram('scr_q', (NSMP, 512)), 'k': fw.dram('scr_k', (NSMP, 512)), 'v': fw.dram('scr_v', (NSMP, 1024))}

    def V(fn, reads=(), writes=(), inc=True):
        fw.op("vector", fn, reads, writes, inc)

    def A(fn, reads=(), writes=(), inc=True):
        fw.op("scalar", fn, reads, writes, inc)

    def PE(fn, reads=(), writes=(), inc=True):
        fw.op("tensor", fn, reads, writes, inc)

    def G(fn, reads=(), writes=(), inc=True):
        fw.op("gpsimd", fn, reads, writes, inc)

    def DS(fn, reads=(), writes=()):
        fw.dma("sync", fn, reads, writes)

    def DG(fn, reads=(), writes=()):
        fw.dma("gpsimd", fn, reads, writes)

    xT = fw.sb("xT", [128, KC, TT], F32)
    hT = fw.sb("hT", [128, KC, TT], BF16)
    banks = [fw.ps(f"bank{i}", [128, 512], F32) for i in range(8)]
    bank_i = [0]
    bank_pool = list(banks)

    def bank():
        b = bank_pool[bank_i[0] % len(bank_pool)]
        bank_i[0] += 1
        return b

    wsm = [fw.sb(f"wsm{i}", [128, KC, 128], BF16) for i in range(3)]
    wsm_i = [0]

    def proj_fm(w_ap, col0, ncols, N, src=None):
        src = src or hT
        wt = wsm[wsm_i[0] % 3]
        wsm_i[0] += 1
        DG(lambda h: h.dma_start(out=wt[:, :, 0:ncols],
                                 in_=w_ap[:, col0:col0 + ncols].rearrange("(kc p) n -> p kc n", p=128)),
           reads=[], writes=[wt])
        bk = bank()
        for kc in range(KC):
            PE(lambda h, kc=kc: h.matmul(bk[0:ncols, :N], lhsT=wt[:, kc, 0:ncols], rhs=src[:, kc, :N],
                                         start=(kc == 0), stop=(kc == KC - 1)),
               reads=[wt, src], writes=[bk], inc=(kc == KC - 1))
        return bk

    ident = fw.sb("ident", [128, 128], F32)
    ones_f = fw.sb("ones_f", [128, 128], F32)
    eps6 = fw.sb("eps6", [128, 1], F32)
    gains = fw.sb("gains", [128, 6, KC], F32)
    rstd = fw.sb("rstd", [128, TT], F32)
    sqb = [fw.sb(f"sqb{i}", [128, TT], F32) for i in range(2)]

    G(lambda h: h.memset(ident[:], 1.0), writes=[ident])
    G(lambda h: h.affine_select(out=ident[:], in_=ident[:], pattern=[[-1, 128]], compare_op=ALU.is_equal,
                                fill=0.0, base=0, channel_multiplier=1), reads=[ident], writes=[ident])
    V(lambda h: h.memset(ones_f[:], 1.0), writes=[ones_f])
    V(lambda h: h.memset(eps6[:], 1e-6), writes=[eps6])
    with nc.allow_non_contiguous_dma(reason="small param vectors"):
        for li in range(2):
            for ni, nm in enumerate(("norm_ffn1", "norm_mix", "norm_ffn2")):
                DS(lambda h, li=li, ni=ni, nm=nm: h.dma_start(
                    out=gains[:, li * 3 + ni, :], in_=IN[nm][li, :].rearrange("(kc p) -> p kc", p=128)),
                    reads=[IN[nm]], writes=[gains])

    def load_xT(src_rows, N):
        nsub = (N + 127) // 128
        es_ = ExitStack()
        tok = [fw.sb(f"tok{i}", [128, D], F32, es=es_) for i in range(2)]
        for s in range(nsub):
            r = min(128, N - s * 128)
            tb = tok[s % 2]
            DS(lambda h, tb=tb, s=s, r=r: h.dma_start(out=tb[0:r, :], in_=src_rows[s * 128:s * 128 + r, :]),
               reads=[IN['xp'], IN['xs']], writes=[tb])
            for kc4 in range(4):
                bk = bank()
                for q in range(4):
                    kc = kc4 * 4 + q
                    PE(lambda h, bk=bk, tb=tb, kc=kc, q=q, r=r: h.transpose(
                        out=bk[:, q * 128:q * 128 + r], in_=tb[0:r, kc * 128:(kc + 1) * 128], identity=ident[0:r, 0:r]),
                        reads=[tb, ident], writes=[bk])
                V(lambda h, bk=bk, kc4=kc4, s=s, r=r: h.tensor_copy(
                    out=xT[:, kc4 * 4:kc4 * 4 + 4, s * 128:s * 128 + r],
                    in_=bk[:, :].rearrange("p (q n) -> p q n", n=128)[:, :, 0:r]), reads=[bk], writes=[xT])
        fw.barrier()
        es_.close()

    def store_xT(dst_rows, dst_buf, N):
        nsub = (N + 127) // 128
        es_ = ExitStack()
        tok = [fw.sb(f"tok{i}", [128, D], F32, es=es_) for i in range(2)]
        for s in range(nsub):
            r = min(128, N - s * 128)
            tb = tok[s % 2]
            for kc4 in range(4):
                bk = bank()
                for q in range(4):
                    kc = kc4 * 4 + q
                    PE(lambda h, bk=bk, kc=kc, q=q, r=r, s=s: h.transpose(
                        out=bk[0:r, q * 128:(q + 1) * 128], in_=xT[:, kc, s * 128:s * 128 + r], identity=ident[:]),
                        reads=[xT, ident], writes=[bk])
                A(lambda h, bk=bk, kc4=kc4, r=r, tb=tb: h.copy(out=tb[0:r, kc4 * 512:(kc4 + 1) * 512], in_=bk[0:r, :]),
                  reads=[bk], writes=[tb])
            DS(lambda h, tb=tb, s=s, r=r: h.dma_start(out=dst_rows[s * 128:s * 128 + r, :], in_=tb[0:r, :]),
               reads=[tb], writes=[dst_buf])
        fw.barrier()
        es_.close()

    def rmsnorm(N, gi):
        bk = bank()
        for kc in range(KC):
            sq = sqb[kc % 2]
            A(lambda h, sq=sq, kc=kc: h.activation(out=sq[:, :N], in_=xT[:, kc, :N], func=AF.Square),
              reads=[xT], writes=[sq])
            PE(lambda h, sq=sq, kc=kc, bk=bk: h.matmul(bk[:, :N], lhsT=ones_f[:], rhs=sq[:, :N],
                                                     start=(kc == 0), stop=(kc == KC - 1)),
               reads=[sq, ones_f], writes=[bk])
        A(lambda h, bk=bk: h.activation(out=rstd[:, :N], in_=bk[:, :N], func=AF.Sqrt, scale=1.0 / D, bias=eps6[:, 0:1]),
          reads=[bk, eps6], writes=[rstd])
        V(lambda h: h.reciprocal(out=rstd[:, :N], in_=rstd[:, :N]), reads=[rstd], writes=[rstd])
        for kc in range(KC):
            V(lambda h, kc=kc: h.scalar_tensor_tensor(out=hT[:, kc, :N], in0=xT[:, kc, :N],
                                                      scalar=gains[:, gi, kc:kc + 1], in1=rstd[:, :N],
                                                      op0=ALU.mult, op1=ALU.mult),
              reads=[xT, gains, rstd], writes=[hT])

    def ffn(N, layer, which):
        gi = layer * 3 + (0 if which == 1 else 2)
        w_in = IN[f'ffn{which}_w_in']
        w_out = IN[f'ffn{which}_w_out']
        rmsnorm(N, gi)
        with ExitStack() as es:
            actT = fw.sb("actT", [128, FC, N], BF16, es=es)
            sg = [fw.sb(f"sg{i}", [128, N], F32, es=es) for i in range(2)]
            wo = [fw.sb(f"wo{i}", [128, FC, 128], BF16, es=es) for i in range(3)]
            wbufs = []
            for i in range(4):
                t = fw.sb(f"wbuf{i}", [128, KC, 256], BF16, es=es)
                wbufs.append({"t": t, "a": Buf("wa"), "b": Buf("wb")})
            wb_i = [0]

            def wbuf():
                w = wbufs[wb_i[0] % 4]
                wb_i[0] += 1
                return w
            for fb in range(FC):
                f0 = fb * 128
                w = wbuf()
                wt = w["t"]
                DG(lambda h, wt=wt, f0=f0: h.dma_start(
                    out=wt[:, :, 0:128], in_=w_in[layer, :, f0:f0 + 128].rearrange("(kc p) n -> p kc n", p=128)),
                    writes=[w["a"]])
                DG(lambda h, wt=wt, f0=f0: h.dma_start(
                    out=wt[:, :, 128:256],
                    in_=w_in[layer, :, DFF + f0:DFF + f0 + 128].rearrange("(kc p) n -> p kc n", p=128)),
                    writes=[w["b"]])
                bg, bu = bank(), bank()
                for kc in range(KC):
                    PE(lambda h, bg=bg, wt=wt, kc=kc: h.matmul(
                        bg[:, :N], lhsT=wt[:, kc, 0:128], rhs=hT[:, kc, :N],
                        start=(kc == 0), stop=(kc == KC - 1)),
                        reads=[w["a"], hT], writes=[bg], inc=(kc == KC - 1))
                for kc in range(KC):
                    PE(lambda h, bu=bu, wt=wt, kc=kc: h.matmul(
                        bu[:, :N], lhsT=wt[:, kc, 128:256], rhs=hT[:, kc, :N],
                        start=(kc == 0), stop=(kc == KC - 1)),
                        reads=[w["b"], hT], writes=[bu], inc=(kc == KC - 1))
                s_ = sg[fb % 2]
                A(lambda h, s_=s_, bg=bg: h.activation(out=s_[:, :N], in_=bg[:, :N], func=AF.Silu),
                  reads=[bg], writes=[s_])
                V(lambda h, s_=s_, bu=bu, fb=fb: h.tensor_tensor(out=actT[:, fb, :N], in0=s_[:, :N], in1=bu[:, :N],
                                                                 op=ALU.mult),
                  reads=[s_, bu], writes=[actT])
            for db in range(KC):
                w2 = wo[db % 3]
                DG(lambda h, w2=w2, db=db: h.dma_start(
                    out=w2[:], in_=w_out[layer, :, db * 128:(db + 1) * 128].rearrange("(fc p) n -> p fc n", p=128)),
                    reads=[w_out], writes=[w2])
                bk = bank()
                for fc in range(FC):
                    PE(lambda h, bk=bk, w2=w2, fc=fc: h.matmul(bk[:, :N], lhsT=w2[:, fc, :], rhs=actT[:, fc, :N],
                                                             start=(fc == 0), stop=(fc == FC - 1)),
                       reads=[w2, actT], writes=[bk], inc=(fc == FC - 1))
                V(lambda h, bk=bk, db=db: h.scalar_tensor_tensor(out=xT[:, db, :N], in0=bk[:, :N], scalar=0.5,
                                                                 in1=xT[:, db, :N], op0=ALU.mult, op1=ALU.add),
                  reads=[bk, xT], writes=[xT])
            fw.barrier()

    CB, BA, BX_, NSP, W0, A0, KK_, KA, LNG, LNB, RK, CW0 = 0, 1, 2, 3, 4, 5, 6, 7, 8, 9, 10, 11
    cvec = fw.sb("cvec", [128, 15, 8], F32)
    muT = fw.sb("muT", [128, 27], F32)
    blk1 = fw.sb("blk1", [128, 128], F32)
    M_ar = fw.sb("M_ar", [128, 256], F32)
    M_sl = fw.sb("M_sl", [128, 128], F32)
    eps_gn = fw.sb("eps_gn", [128, 1], F32)
    ones_tt = fw.sb("ones_tt", [128, TT], F32)
    convtail = fw.sb("convtail", [128, 8, 3], F32)
    hprev = fw.sb("hprev", [128, 8], F32)
    lastcol = fw.sb("lastcol", [128, 27], F32)
    Pst = [[fw.sb(f"Pst{j}_{k}", [128, 128], F32) for k in range(2)] for j in range(8)]
    pcur = [0] * 8

    def vload(idx, ap1d):
        DS(lambda h: h.dma_start(out=cvec[:, idx, :], in_=ap1d.rearrange("(c p) -> p c", p=128)), writes=[cvec])

    vload(CB, IN['lru_conv_b'][0, :])
    vload(BA, IN['lru_ba'][0, :])
    vload(BX_, IN['lru_bx'][0, :])
    vload(NSP, IN['lru_lambda'][0, :])
    vload(W0, IN['rwkv_w0'][0, :])
    vload(A0, IN['rwkv_a0'][0, :])
    vload(KK_, IN['rwkv_k_k'][0, :])
    vload(KA, IN['rwkv_k_a'][0, :])
    vload(LNG, IN['rwkv_ln_g'][0, :])
    vload(LNB, IN['rwkv_ln_b'][0, :])
    vload(RK, IN['rwkv_r_k'][0].rearrange("h d -> (h d)"))
    for j_ in range(4):
        vload(CW0 + j_, IN['lru_conv_w'][0, j_, :])
    A(lambda h: h.activation(out=cvec[:, NSP, :], in_=cvec[:, NSP, :], func=AF.Exp, scale=-1.0), reads=[cvec], writes=[cvec])
    A(lambda h: h.activation(out=cvec[:, NSP, :], in_=cvec[:, NSP, :], func=AF.Ln, bias=ones_f[:, 0:1]), reads=[cvec, ones_f], writes=[cvec])
    V(lambda h: h.tensor_scalar(out=cvec[:, NSP, :], in0=cvec[:, NSP, :], scalar1=-8.0, scalar2=None, op0=ALU.mult), reads=[cvec], writes=[cvec])
    V(lambda h: h.memset(muT[:], 0.0), writes=[muT])
    DS(lambda h: h.dma_start(out=muT[:, 0:26], in_=IN['rwkv_mu'][0, 0:3328].rearrange("(c p) -> p c", p=128)), writes=[muT])
    DS(lambda h: h.dma_start(out=muT[0:32, 26:27], in_=IN['rwkv_mu'][0, 3328:3360].rearrange("(c p) -> p c", p=32)), writes=[muT])
    V(lambda h: h.memset(blk1[:], 0.0), writes=[blk1])
    V(lambda h: h.memset(blk1[0:64, 0:64], 1.0), writes=[blk1])
    V(lambda h: h.memset(blk1[64:128, 64:128], 1.0), writes=[blk1])
    G(lambda h: h.affine_select(out=M_ar[:, 0:128], in_=blk1[:], pattern=[[1, 128]], compare_op=ALU.is_gt, fill=0.0,
                                base=0, channel_multiplier=-1), reads=[blk1], writes=[M_ar])
    G(lambda h: h.affine_select(out=M_ar[:, 128:256], in_=blk1[:], pattern=[[1, 128]], compare_op=ALU.is_ge, fill=0.0,
                                base=0, channel_multiplier=-1), reads=[blk1], writes=[M_ar])
    G(lambda h: h.affine_select(out=M_sl[:], in_=blk1[:], pattern=[[-1, 128]], compare_op=ALU.is_gt, fill=0.0,
                                base=0, channel_multiplier=1), reads=[blk1], writes=[M_sl])
    V(lambda h: h.memset(eps_gn[:], 64e-5), writes=[eps_gn])
    V(lambda h: h.memset(ones_tt[:], 1.0), writes=[ones_tt])
    V(lambda h: h.memset(convtail[:], 0.0), writes=[convtail])
    V(lambda h: h.memset(hprev[:], 0.0), writes=[hprev])
    V(lambda h: h.memset(lastcol[:], 0.0), writes=[lastcol])
    for j_ in range(8):
        V(lambda h, j_=j_: h.memset(Pst[j_][0][:], 0.0), writes=[Pst[j_][0]])

    def tt(out, in0, in1, op, reads, writes, eng=None):
        (eng or V)(lambda h: h.tensor_tensor(out=out, in0=in0, in1=in1, op=op), reads=reads, writes=writes)

    def stt(out, in0, scalar, in1, op0, op1, reads, writes):
        V(lambda h: h.scalar_tensor_tensor(out=out, in0=in0, scalar=scalar, in1=in1, op0=op0, op1=op1), reads=reads, writes=writes)

    def ts(out, in0, s1, s2, op0, op1, reads, writes):
        if s2 is None:
            V(lambda h: h.tensor_scalar(out=out, in0=in0, scalar1=s1, scalar2=None, op0=op0), reads=reads, writes=writes)
        else:
            V(lambda h: h.tensor_scalar(out=out, in0=in0, scalar1=s1, scalar2=s2, op0=op0, op1=op1), reads=reads, writes=writes)

    def act(out, in_, func, reads, writes, bias=None, scale=None):
        kw = {}
        if bias is not None:
            kw["bias"] = bias
        if scale is not None:
            kw["scale"] = scale
        A(lambda h: h.activation(out=out, in_=in_, func=func, **kw), reads=reads, writes=writes)

    def mm(out, lhsT, rhs, reads, writes, start=True, stop=True, inc=True):
        PE(lambda h: h.matmul(out, lhsT=lhsT, rhs=rhs, start=start, stop=stop), reads=reads, writes=writes, inc=inc)

    def vcopy(out, in_, reads, writes):
        V(lambda h: h.tensor_copy(out=out, in_=in_), reads=reads, writes=writes)

    def acopy(out, in_, reads, writes):
        A(lambda h: h.copy(out=out, in_=in_), reads=reads, writes=writes)

    def even_mixer(kind, N, last):
        W = IN['ab_w_in'][0]
        prm = (kind == "p")
        n = 64 if prm else 1
        NU = N // n
        rmsnorm(N, 1)
        with ExitStack() as es:
            yTin = fw.sb("yTin", [128, KC, N], BF16, es=es)
            w2a2 = fw.sb("w2a2", [128, 1024], F32, es=es)
            g2a = fw.sb("g2a", [128, 1024], F32, es=es)
            g2b = fw.sb("g2b", [32, 1024], F32, es=es)
            WAbd = fw.sb("WAbd", [128, 8, 128], F32, es=es)
            WXbd = fw.sb("WXbd", [128, 8, 128], F32, es=es)
            DS(lambda h: h.dma_start(out=w2a2[0:64, :], in_=IN['rwkv_w2'][0]), writes=[w2a2])
            DS(lambda h: h.dma_start(out=w2a2[64:128, :], in_=IN['rwkv_a2'][0]), writes=[w2a2])
            DS(lambda h: h.dma_start(out=g2a[:], in_=IN['rwkv_g2'][0, 0:128, :]), writes=[g2a])
            DS(lambda h: h.dma_start(out=g2b[:], in_=IN['rwkv_g2'][0, 128:160, :]), writes=[g2b])
            V(lambda h: h.memset(WAbd[:], 0.0), writes=[WAbd])
            V(lambda h: h.memset(WXbd[:], 0.0), writes=[WXbd])
            for n_ in range(16):
                c_, hh_ = n_ // 2, n_ % 2
                sl_ = slice(64 * hh_, 64 * hh_ + 64)
                DS(lambda h, n_=n_, c_=c_, sl_=sl_: h.dma_start(out=WAbd[sl_, c_, sl_], in_=IN['lru_wa'][0, n_]), writes=[WAbd])
                DS(lambda h, n_=n_, c_=c_, sl_=sl_: h.dma_start(out=WXbd[sl_, c_, sl_], in_=IN['lru_wx'][0, n_]), writes=[WXbd])

            with ExitStack() as es2:
                xbpad = fw.sb("xbpad", [128, 8, N + 3], F32, es=es2)
                hs = fw.sb("hs", [128, 8, N], F32, es=es2)
                tl = [fw.sb(f"tl{i}", [128, N], F32, es=es2) for i in range(6)]
                if not prm:
                    convT = fw.sb("convT", [128, 8, 3, N], F32, es=es2)
                    h0T = fw.sb("h0T", [128, 8, N], F32, es=es2)
                    csout = fw.sb("csout", [128, 8, 3, N], F32, es=es2)
                    for c in range(8):
                        cs = slice(c * 128, (c + 1) * 128)
                        for j3 in range(3):
                            DS(lambda h, c=c, cs=cs, j3=j3: h.dma_start(out=convT[:, c, j3, :], in_=IN['s_lru_conv'][:, cs].rearrange("(b j) p -> p j b", j=3)[:, j3, :]), writes=[convT])
                        DS(lambda h, c=c, cs=cs: h.dma_start(out=h0T[:, c, :], in_=IN['s_lru_h'][:, cs].rearrange("b p -> p b")), writes=[h0T])
                for c in range(8):
                    bk = proj_fm(W, c * 128, 128, N)
                    acopy(xbpad[:, c, 3:3 + N], bk[:, :N], [bk], [xbpad])
                    if prm:
                        vcopy(xbpad[:, c, 0:3], convtail[:, c, :], [convtail], [xbpad])
                        srcs = [xbpad[:, c, j:j + N] for j in range(4)]
                        srd = [xbpad]
                    else:
                        srcs = [convT[:, c, 0, :], convT[:, c, 1, :], convT[:, c, 2, :], xbpad[:, c, 3:3 + N]]
                        srd = [xbpad, convT]
                    xc, gr, gi, a_, t1, g_ = tl
                    ts(xc[:, :N], srcs[0], cvec[:, CW0, c:c + 1], cvec[:, CB, c:c + 1], ALU.mult, ALU.add, srd + [cvec], [xc])
                    for j in range(1, 4):
                        stt(xc[:, :N], srcs[j], cvec[:, CW0 + j, c:c + 1], xc[:, :N], ALU.mult, ALU.add, srd + [cvec, xc], [xc])
                    b1, b2 = bank(), bank()
                    mm(b1[:, :N], WAbd[:, c, :], xc[:, :N], [WAbd, xc], [b1])
                    mm(b2[:, :N], WXbd[:, c, :], xc[:, :N], [WXbd, xc], [b2])
                    act(gr[:, :N], b1[:, :N], AF.Sigmoid, [b1, cvec], [gr], bias=cvec[:, BA, c:c + 1])
                    act(gi[:, :N], b2[:, :N], AF.Sigmoid, [b2, cvec], [gi], bias=cvec[:, BX_, c:c + 1])
                    act(a_[:, :N], gr[:, :N], AF.Exp, [gr, cvec], [a_], scale=cvec[:, NSP, c:c + 1])
                    tt(t1[:, :N], a_[:, :N], a_[:, :N], ALU.mult, [a_], [t1])
                    ts(t1[:, :N], t1[:, :N], -1.0, 1.0, ALU.mult, ALU.add, [t1], [t1])
                    act(t1[:, :N], t1[:, :N], AF.Sqrt, [t1], [t1])
                    tt(gi[:, :N], gi[:, :N], xc[:, :N], ALU.mult, [gi, xc], [gi])
                    tt(t1[:, :N], t1[:, :N], gi[:, :N], ALU.mult, [t1, gi], [t1])
                    if prm:
                        V(lambda h, c=c: h.tensor_tensor_scan(out=hs[:, c, :N], data0=a_[:, :N], data1=t1[:, :N],
                                                              initial=hprev[:, c:c + 1], op0=ALU.mult, op1=ALU.add),
                          reads=[a_, t1, hprev], writes=[hs])
                        vcopy(hprev[:, c:c + 1], hs[:, c, N - 1:N], [hs], [hprev])
                        vcopy(convtail[:, c, :], xbpad[:, c, N:N + 3], [xbpad], [convtail])
                    else:
                        tt(a_[:, :N], a_[:, :N], h0T[:, c, :], ALU.mult, [a_, h0T], [a_])
                        tt(hs[:, c, :N], a_[:, :N], t1[:, :N], ALU.add, [a_, t1], [hs])
                for c in range(8):
                    bk = proj_fm(W, 1024 + c * 128, 128, N)
                    g_ = tl[5]
                    act(g_[:, :N], bk[:, :N], AF.Gelu_apprx_tanh, [bk], [g_])
                    tt(yTin[:, c, :N], g_[:, :N], hs[:, c, :N], ALU.mult, [g_, hs], [yTin])
                if prm and last:
                    DS(lambda h: h.dma_start(out=OUT['lru_h_p'][0, :].rearrange("(c p) -> p c", p=128), in_=hprev[:, :]),
                       reads=[hprev], writes=[OUT['lru_h_p']])
                    for j3 in range(3):
                        DS(lambda h, j3=j3: h.dma_start(out=OUT['lru_conv_p'][j3, :].rearrange("(c p) -> p c", p=128), in_=convtail[:, :, j3]),
                           reads=[convtail], writes=[OUT['lru_conv_p']])
                if not prm:
                    for c in range(8):
                        cs = slice(c * 128, (c + 1) * 128)
                        DS(lambda h, c=c, cs=cs: h.dma_start(out=OUT['lru_h_s'][:, cs].rearrange("b p -> p b"), in_=hs[:, c, :N]),
                           reads=[hs], writes=[OUT['lru_h_s']])
                        vcopy(csout[:, c, 0:2, :], convT[:, c, 1:3, :], [convT], [csout])
                        vcopy(csout[:, c, 2, :], xbpad[:, c, 3:3 + N], [xbpad], [csout])
                        for j3 in range(3):
                            DS(lambda h, c=c, cs=cs, j3=j3: h.dma_start(out=OUT['lru_conv_s'][:, cs].rearrange("(b j) p -> p j b", j=3)[:, j3, :], in_=csout[:, c, j3, :]),
                               reads=[csout], writes=[OUT['lru_conv_s']])
                fw.barrier()
            with ExitStack() as es2:
                def mk(nm, w=N):
                    return fw.sb(nm, [128, w], F32, es=es2)
                rs24, rs25, rs26, th, sg25, sg26 = (mk(x) for x in ("rs24", "rs25", "rs26", "th", "sg25", "sg26"))
                pad = [mk("pad0", N + 1), mk("pad1", N + 1)]
                tmp = mk("tmp")
                r_, k_, v_, lw, aicl, gate, kk, kmod, bb, bonus, Wt, Winv, Wprev, yT, t2, t3 = (mk(x) for x in (
                    "r_", "k_", "v_", "lw", "aicl", "gate", "kk", "kmod", "bb", "bonus", "Wt", "Winv", "Wprev", "yT", "t2", "t3"))
                Lpad = mk("Lpad", N + 1)
                negL = mk("negL", N + 1)
                if not prm:
                    shiftT = fw.sb("shiftT", [128, 27, N], F32, es=es2)
                    rawS = fw.sb("rawS", [128, 27, N], F32, es=es2)
                    SXi = [fw.sb(f"SXi{i}", [128, 128], F32, es=es2) for i in range(2)]
                    SXo = [fw.sb(f"SXo{i}", [128, 128], F32, es=es2) for i in range(2)]
                    Ps = [fw.sb(f"Ps{i}", [128, 128], F32, es=es2) for i in range(4)]
                    for t_ in SXi:
                        V(lambda h, t_=t_: h.memset(t_[:], 0.0), writes=[t_])
                    for q in range(27):
                        rows = 128 if q < 26 else 32
                        DS(lambda h, q=q, rows=rows: h.dma_start(out=shiftT[0:rows, q, :], in_=IN['s_shift'][:, q * 128:q * 128 + rows].rearrange("b p -> p b")),
                           writes=[shiftT])
                SS = []
                for k in range(2):
                    S = {}
                    for nm, w in (("AR", 256), ("BX", 128), ("KX", 128), ("VXf", 128), ("NN1", 256), ("NN2", 256),
                                  ("A0", 128), ("A1", 128), ("N0", 128), ("N1", 128), ("G0", 128), ("G1", 128),
                                  ("T3", 384), ("R0", 128), ("U", 128)):
                        S[nm] = fw.sb(f"S{k}{nm}", [128, w], F32, es=es2)
                    for nm in ("AR", "BX", "KX", "VXf"):
                        V(lambda h, t_=S[nm]: h.memset(t_[:], 0.0), writes=[S[nm]])
                    SS.append(S)

                def shifted(q, dest, dbuf, rows=128):
                    bk = proj_fm(W, 2048 + q * 128, rows, N)
                    pd = pad[q % 2]
                    acopy(pd[0:rows, 1:N + 1], bk[0:rows, :N], [bk], [pd])
                    if prm:
                        vcopy(pd[0:rows, 0:1], lastcol[0:rows, q:q + 1], [lastcol], [pd])
                        prev = pd[0:rows, 0:N]
                        prd = [pd]
                    else:
                        prev = shiftT[0:rows, q, :]
                        prd = [pd, shiftT]
                        vcopy(rawS[0:rows, q, :], pd[0:rows, 1:N + 1], [pd], [rawS])
                    cur = pd[0:rows, 1:N + 1]
                    tt(tmp[0:rows, :N], prev, cur, ALU.subtract, prd, [tmp])
                    stt(dest, tmp[0:rows, :N], muT[0:rows, q:q + 1], cur, ALU.mult, ALU.add, [tmp, muT, pd], [dbuf])
                    if prm:
                        vcopy(lastcol[0:rows, q:q + 1], pd[0:rows, N:N + 1], [pd], [lastcol])

                shifted(24, rs24[:, :N], rs24)
                shifted(25, rs25[:, :N], rs25)
                shifted(26, rs26[0:32, :N], rs26, rows=32)
                act(th[0:64, :N], rs24[0:64, :N], AF.Tanh, [rs24], [th])
                act(sg25[:, :N], rs25[:, :N], AF.Sigmoid, [rs25], [sg25])
                act(sg26[0:32, :N], rs26[0:32, :N], AF.Sigmoid, [rs26], [sg26])
                ucount = 0
                for j in range(8):
                    jc = slice(j * 128, (j + 1) * 128)
                    shifted(j, r_[:, :N], r_)
                    shifted(8 + j, k_[:, :N], k_)
                    shifted(16 + j, v_[:, :N], v_)
                    bd = bank()
                    mm(bd[:, :N], w2a2[0:64, jc], th[0:64, :N], [w2a2, th], [bd])
                    act(t2[:, :N], bd[:, :N], AF.Sigmoid, [bd, cvec], [t2], bias=cvec[:, W0, j:j + 1])
                    ts(lw[:, :N], t2[:, :N], -math.exp(-0.5), None, ALU.mult, None, [t2], [lw])
                    ba_ = bank()
                    mm(ba_[:, :N], w2a2[64:128, jc], rs24[64:128, :N], [w2a2, rs24], [ba_])
                    act(aicl[:, :N], ba_[:, :N], AF.Sigmoid, [ba_, cvec], [aicl], bias=cvec[:, A0, j:j + 1])
                    bg = bank()
                    mm(bg[:, :N], g2a[:, jc], sg25[:, :N], [g2a, sg25], [bg], start=True, stop=False)
                    mm(bg[:, :N], g2b[0:32, jc], sg26[0:32, :N], [g2b, sg26], [bg], start=False, stop=True)
                    acopy(gate[:, :N], bg[:, :N], [bg], [gate])
                    ts(kk[:, :N], k_[:, :N], cvec[:, KK_, j:j + 1], None, ALU.mult, None, [k_, cvec], [kk])
                    act(t2[:, :N], kk[:, :N], AF.Square, [kk], [t2])
                    bs = bank()
                    mm(bs[:, :N], blk1[:], t2[:, :N], [blk1, t2], [bs])
                    act(t3[:, :N], bs[:, :N], AF.Sqrt, [bs], [t3])
                    ts(t3[:, :N], t3[:, :N], 1e-12, None, ALU.max, None, [t3], [t3])
                    V(lambda h: h.reciprocal(out=t3[:, :N], in_=t3[:, :N]), reads=[t3], writes=[t3])
                    tt(kk[:, :N], kk[:, :N], t3[:, :N], ALU.mult, [kk, t3], [kk])
                    ts(t2[:, :N], aicl[:, :N], -1.0, cvec[:, KA, j:j + 1], ALU.add, ALU.mult, [aicl, cvec], [t2])
                    stt(kmod[:, :N], t2[:, :N], 1.0, k_[:, :N], ALU.add, ALU.mult, [t2, k_], [kmod])
                    tt(bb[:, :N], kk[:, :N], aicl[:, :N], ALU.mult, [kk, aicl], [bb])
                    stt(t2[:, :N], r_[:, :N], cvec[:, RK, j:j + 1], kmod[:, :N], ALU.mult, ALU.mult, [r_, cvec, kmod], [t2])
                    bb2 = bank()
                    mm(bb2[:, :N], blk1[:], t2[:, :N], [blk1, t2], [bb2])
                    tt(bonus[:, :N], bb2[:, :N], v_[:, :N], ALU.mult, [bb2, v_], [bonus])
                    if prm:
                        V(lambda h: h.memset(Lpad[:, 0:1], 0.0), writes=[Lpad])
                        V(lambda h: h.tensor_tensor_scan(out=Lpad[:, 1:N + 1], data0=ones_tt[:, :N], data1=lw[:, :N], initial=0.0,
                                                         op0=ALU.mult, op1=ALU.add), reads=[ones_tt, lw], writes=[Lpad])
                        ts(negL[:, :], Lpad[:, :], -1.0, None, ALU.mult, None, [Lpad], [negL])
                        for ci in range(NU):
                            c0 = ci * 64
                            act(Wt[:, c0:c0 + 64], Lpad[:, c0 + 1:c0 + 65], AF.Exp, [Lpad, negL], [Wt], bias=negL[:, c0:c0 + 1])
                            act(Winv[:, c0:c0 + 64], Lpad[:, c0 + 1:c0 + 65], AF.Exp, [Lpad, negL], [Winv], bias=Lpad[:, c0:c0 + 1], scale=-1.0)
                            act(Wprev[:, c0:c0 + 64], Lpad[:, c0:c0 + 64], AF.Exp, [Lpad, negL], [Wprev], bias=negL[:, c0:c0 + 1])
                    else:
                        act(Wt[:, :N], lw[:, :N], AF.Exp, [lw], [Wt])
                        act(Winv[:, :N], lw[:, :N], AF.Exp, [lw], [Winv], scale=-1.0)
                        vcopy(Wprev[:, :N], ones_tt[:, :N], [ones_tt], [Wprev])
                    for u in range(NU):
                        S = SS[ucount % 2]
                        cols = slice(u * n, (u + 1) * n)
                        if prm:
                            P0 = Pst[j][pcur[j]]
                            P1 = Pst[j][1 - pcur[j]]
                            pcur[j] = 1 - pcur[j]
                        else:
                            sx = SXi[ucount % 2]
                            for hh in range(2):
                                sl = slice(64 * hh, 64 * hh + 64)
                                DS(lambda h, sx=sx, sl=sl, u=u, hh=hh, j=j: h.dma_start(out=sx[sl, sl], in_=IN['s_wkv'][u, 2 * j + hh]), writes=[sx])
                            bt0 = bank()
                            PE(lambda h, bt0=bt0, sx=sx: h.transpose(out=bt0[:, 0:128], in_=sx[:], identity=ident[:]), reads=[sx, ident], writes=[bt0])
                            P0 = Ps[(ucount % 2) * 2]
                            P1 = Ps[(ucount % 2) * 2 + 1]
                            acopy(P0[:], bt0[:, 0:128], [bt0], [P0])
                        ucount += 1
                        for hh in range(2):
                            ps_ = slice(64 * hh, 64 * hh + 64)
                            f0 = 64 * hh
                            stt(S["AR"][ps_, f0:f0 + n], kk[ps_, cols], -1.0, Wprev[ps_, cols], ALU.mult, ALU.mult, [kk, Wprev], [S["AR"]])
                            tt(S["AR"][ps_, 128 + f0:128 + f0 + n], r_[ps_, cols], Wt[ps_, cols], ALU.mult, [r_, Wt], [S["AR"]])
                            tt(S["BX"][ps_, f0:f0 + n], bb[ps_, cols], Winv[ps_, cols], ALU.mult, [bb, Winv], [S["BX"]])
                            tt(S["KX"][ps_, f0:f0 + n], kmod[ps_, cols], Winv[ps_, cols], ALU.mult, [kmod, Winv], [S["KX"]])
                            acopy(S["VXf"][ps_, f0:f0 + n], v_[ps_, cols], [v_], [S["VXf"]])
                        b1, b2 = bank(), bank()
                        mm(b1[:, 0:256], S["BX"][:], S["AR"][:], [S["BX"], S["AR"]], [b1])
                        tt(S["NN1"][:], b1[:, 0:256], M_ar[:], ALU.mult, [b1, M_ar], [S["NN1"]])
                        mm(b2[:, 0:256], S["KX"][:], S["AR"][:], [S["KX"], S["AR"]], [b2])
                        tt(S["NN2"][:], b2[:, 0:256], M_ar[:], ALU.mult, [b2, M_ar], [S["NN2"]])
                        if n > 1:
                            b3 = bank()
                            mm(b3[:, 0:128], S["AR"][:, 0:128], S["BX"][:], [S["AR"], S["BX"]], [b3])
                            tt(S["A0"][:], b3[:, 0:128], M_sl[:], ALU.mult, [b3, M_sl], [S["A0"]])
                            tt(S["G0"][:], S["NN1"][:, 0:128], ident[:], ALU.add, [S["NN1"], ident], [S["G0"]])
                            Ncur, Nb = S["NN1"][:, 0:128], S["NN1"]
                            Ab = [S["A0"], S["A1"]]
                            Nbufs = [S["N0"], S["N1"]]
                            Gb = [S["G0"], S["G1"]]
                            for i in range(5):
                                Acur = Ab[i % 2]
                                Anew = Ab[(i + 1) % 2]
                                bA = bank()
                                mm(bA[:, 0:128], Ncur, Acur[:], [Nb, Acur], [bA])
                                if i < 4:
                                    bN = bank()
                                    mm(bN[:, 0:128], Acur[:], Ncur, [Acur, Nb], [bN])
                                acopy(Anew[:], bA[:, 0:128], [bA], [Anew])
                                if i < 4:
                                    Nn = Nbufs[i % 2]
                                    acopy(Nn[:], bN[:, 0:128], [bN], [Nn])
                                bG = bank()
                                mm(bG[:, 0:128], Anew[:], Gb[i % 2][:], [Anew, Gb[i % 2]], [bG])
                                tt(Gb[(i + 1) % 2][:], bG[:, 0:128], Gb[i % 2][:], ALU.add, [bG, Gb[i % 2]], [Gb[(i + 1) % 2]])
                                if i < 4:
                                    Ncur, Nb = Nn[:], Nn
                            Gf = Gb[1]
                        bt = bank()
                        for ti, nm in enumerate(("BX", "KX", "VXf")):
                            PE(lambda h, bt=bt, ti=ti, src=S[nm]: h.transpose(out=bt[:, ti * 128:(ti + 1) * 128], in_=src[:], identity=ident[:]),
                               reads=[S[nm], ident], writes=[bt])
                        acopy(S["T3"][:], bt[:, 0:384], [bt], [S["T3"]])
                        Bt, Kt, VX = S["T3"][:, 0:128], S["T3"][:, 128:256], S["T3"][:, 256:384]
                        bR = bank()
                        mm(bR[:, 0:128], S["AR"][:, 0:128], P0[:], [S["AR"], P0], [bR], start=True, stop=False)
                        mm(bR[:, 0:128], S["NN2"][:, 0:128], VX, [S["NN2"], S["T3"]], [bR], start=False, stop=True)
                        vcopy(S["R0"][:], bR[:, 0:128], [bR], [S["R0"]])
                        if n > 1:
                            bU = bank()
                            mm(bU[:, 0:128], Gf[:], S["R0"][:], [Gf, S["R0"]], [bU])
                            vcopy(S["U"][:], bU[:, 0:128], [bU], [S["U"]])
                            Ub = S["U"]
                        else:
                            Ub = S["R0"]
                        bY = bank()
                        mm(bY[:, 0:128], P0[:], S["AR"][:, 128:256], [P0, S["AR"]], [bY], start=True, stop=False)
                        mm(bY[:, 0:128], Ub[:], S["NN1"][:, 128:256], [Ub, S["NN1"]], [bY], start=False, stop=False)
                        mm(bY[:, 0:128], VX, S["NN2"][:, 128:256], [S["T3"], S["NN2"]], [bY], start=False, stop=True)
                        for hh in range(2):
                            ps_ = slice(64 * hh, 64 * hh + 64)
                            acopy(yT[ps_, cols], bY[ps_, 64 * hh:64 * hh + n], [bY], [yT])
                        bP = bank()
                        mm(bP[:, 0:128], ident[:], P0[:], [ident, P0], [bP], start=True, stop=False)
                        mm(bP[:, 0:128], Bt, Ub[:], [S["T3"], Ub], [bP], start=False, stop=False)
                        mm(bP[:, 0:128], Kt, VX, [S["T3"]], [bP], start=False, stop=True)
                        wc = (u + 1) * n - 1
                        ts(P1[:], bP[:, 0:128], Wt[:, wc:wc + 1], None, ALU.mult, None, [bP, Wt], [P1])
                        if not prm:
                            bt1 = bank()
                            PE(lambda h, bt1=bt1, P1=P1: h.transpose(out=bt1[:, 0:128], in_=P1[:], identity=ident[:]), reads=[P1, ident], writes=[bt1])
                            so = SXo[u % 2]
                            vcopy(so[:], bt1[:, 0:128], [bt1], [so])
                            for hh in range(2):
                                sl = slice(64 * hh, 64 * hh + 64)
                                DS(lambda h, so=so, sl=sl, u=u, hh=hh, j=j: h.dma_start(out=OUT['wkv_s'][u, 2 * j + hh], in_=so[sl, sl]),
                                   reads=[so], writes=[OUT['wkv_s']])
                    bm = bank()
                    mm(bm[:, :N], blk1[:], yT[:, :N], [blk1, yT], [bm])
                    stt(t2[:, :N], bm[:, :N], -1.0 / 64, yT[:, :N], ALU.mult, ALU.add, [bm, yT], [t2])
                    act(t3[:, :N], t2[:, :N], AF.Square, [t2], [t3])
                    bv = bank()
                    mm(bv[:, :N], blk1[:], t3[:, :N], [blk1, t3], [bv])
                    act(t3[:, :N], bv[:, :N], AF.Sqrt, [bv, eps_gn], [t3], bias=eps_gn[:, 0:1], scale=1.0 / 64)
                    V(lambda h: h.reciprocal(out=t3[:, :N], in_=t3[:, :N]), reads=[t3], writes=[t3])
                    tt(t2[:, :N], t2[:, :N], t3[:, :N], ALU.mult, [t2, t3], [t2])
                    ts(t2[:, :N], t2[:, :N], cvec[:, LNG, j:j + 1], cvec[:, LNB, j:j + 1], ALU.mult, ALU.add, [t2, cvec], [t2])
                    tt(t2[:, :N], t2[:, :N], bonus[:, :N], ALU.add, [t2, bonus], [t2])
                    tt(yTin[:, 8 + j, :N], t2[:, :N], gate[:, :N], ALU.mult, [t2, gate], [yTin])
                    if prm and last:
                        bt1 = bank()
                        Pf = Pst[j][pcur[j]]
                        PE(lambda h, bt1=bt1, Pf=Pf: h.transpose(out=bt1[:, 0:128], in_=Pf[:], identity=ident[:]), reads=[Pf, ident], writes=[bt1])
                        vcopy(t3[:, 0:128], bt1[:, 0:128], [bt1], [t3])
                        for hh in range(2):
                            sl = slice(64 * hh, 64 * hh + 64)
                            DS(lambda h, sl=sl, hh=hh, j=j: h.dma_start(out=OUT['wkv_p'][2 * j + hh], in_=t3[sl, sl]), reads=[t3], writes=[OUT['wkv_p']])
                if prm and last:
                    DS(lambda h: h.dma_start(out=OUT['shift_p'][0, 0:3328].rearrange("(c p) -> p c", p=128), in_=lastcol[:, 0:26]),
                       reads=[lastcol], writes=[OUT['shift_p']])
                    DS(lambda h: h.dma_start(out=OUT['shift_p'][0, 3328:3360].rearrange("(c p) -> p c", p=32), in_=lastcol[0:32, 26:27]),
                       reads=[lastcol], writes=[OUT['shift_p']])
                if not prm:
                    for q in range(27):
                        rows = 128 if q < 26 else 32
                        DS(lambda h, q=q, rows=rows: h.dma_start(out=OUT['shift_s'][:, q * 128:q * 128 + rows].rearrange("b p -> p b"), in_=rawS[0:rows, q, :]),
                           reads=[rawS], writes=[OUT['shift_s']])
                fw.barrier()
            Wo = IN['ab_w_out'][0]
            for db in range(KC):
                bk = proj_fm(Wo, db * 128, 128, N, src=yTin)
                tt(xT[:, db, :N], xT[:, db, :N], bk[:, :N], ALU.add, [xT, bk], [xT])
            fw.barrier()

    Gq = fw.sb("Gq", [128, 64], F32)
    Gk = fw.sb("Gk", [128, 3, 64], F32)
    rdec = fw.sb("rdec", [128, 16], F32)
    S2 = fw.sb("S2", [128, 4, 128], F32)
    eps5 = fw.sb("eps5", [128, 1], F32)
    DS(lambda h: h.dma_start(out=Gq[:], in_=IN['nsa_q_norm'][0:1, :].to_broadcast([128, 64])), writes=[Gq])
    for i_ in range(3):
        DS(lambda h, i_=i_: h.dma_start(out=Gk[:, i_, :], in_=IN['nsa_k_norm'][0, i_:i_ + 1, :].to_broadcast([128, 64])), writes=[Gk])
    DS(lambda h: h.dma_start(out=rdec[:], in_=IN['ret_dec'][:, :]), writes=[rdec])
    V(lambda h: h.memset(S2[:], 0.0), writes=[S2])
    V(lambda h: h.memset(eps5[:], 1e-5), writes=[eps5])
    _lg = np.log1p(-np.exp2(-5.0 - np.arange(8, dtype=np.float32))).astype(np.float32)
    GAMMA_C = [float(np.exp(np.float32(128.0) * _lg[h_])) for h_ in range(8)]
    GAMMA_1 = [float(np.exp(_lg[h_])) for h_ in range(8)]

    def odd_mixer(kind, N, jt, last):
        Wc = IN['cd_w_in'][0]
        prm = (kind == "p")
        pos0 = jt * TT if prm else T
        rmsnorm(N, 4)
        nsub = (N + 127) // 128
        allsubs = [(s, min(128, N - s * 128)) for s in range(nsub)]
        with ExitStack() as es:
            oT = fw.sb("oT", [128, KC, N], BF16, es=es)
            wt2 = [fw.sb(f"wt2{i}", [128, KC, 512], BF16, es=es) for i in range(2)]
            wt2_i = [0]
            V(lambda h: h.memset(oT[:, 0:8, :], 0.0), writes=[oT])
            tA = fw.sb("tA", [128, 1024], F32, es=es)
            tS = fw.sb("tS", [128, 16], F32, es=es)
            tR = [fw.sb(f"tR{i}", [128, 256], F32, es=es) for i in range(4)]
            ropN = [fw.sb(f"ropN{i}", [128, 16], F32, es=es) for i in range(2)]
            ropR = [fw.sb(f"ropR{i}", [128, 64], F32, es=es) for i in range(2)]

            def rms_heads(dst3, src3, r, H, gain_ap, sbuf_, dbuf_):
                act(tA[0:r, 0:H * 64].rearrange("p (h d) -> p h d", d=64), src3, AF.Square, [sbuf_], [tA])
                V(lambda h: h.tensor_reduce(out=tS[0:r, 0:H], in_=tA[0:r, 0:H * 64].rearrange("p (h d) -> p h d", d=64), axis=AX.X, op=ALU.add),
                  reads=[tA], writes=[tS])
                act(tS[0:r, 0:H], tS[0:r, 0:H], AF.Sqrt, [tS, eps6], [tS], bias=eps6[0:r, 0:1], scale=1.0 / 64)
                V(lambda h: h.reciprocal(out=tS[0:r, 0:H], in_=tS[0:r, 0:H]), reads=[tS], writes=[tS])
                V(lambda h: h.tensor_tensor(out=dst3, in0=src3, in1=tS[0:r, 0:H].unsqueeze(2).to_broadcast([r, H, 64]), op=ALU.mult),
                  reads=[tS, sbuf_], writes=[dbuf_])
                V(lambda h: h.tensor_tensor(out=dst3, in0=dst3, in1=gain_ap.unsqueeze(1).to_broadcast([r, H, 64]), op=ALU.mult),
                  reads=[Gq, Gk, dbuf_], writes=[dbuf_])

            def rope_ip(x3, r, H, half, cs, sn, ropb, xbuf):
                x1 = x3[:, :, 0:half]
                x2 = x3[:, :, half:2 * half]
                cb = cs.unsqueeze(1).to_broadcast([r, H, half])
                sb_ = sn.unsqueeze(1).to_broadcast([r, H, half])
                tv = [t_[0:r, 0:H * half].rearrange("p (h e) -> p h e", e=half) for t_ in tR]
                crd = [ropb, xbuf]
                V(lambda h: h.tensor_tensor(out=tv[0], in0=x1, in1=cb, op=ALU.mult), reads=crd, writes=[tR[0]])
                V(lambda h: h.tensor_tensor(out=tv[1], in0=x2, in1=sb_, op=ALU.mult), reads=crd, writes=[tR[1]])
                V(lambda h: h.tensor_tensor(out=tv[2], in0=x2, in1=cb, op=ALU.mult), reads=crd, writes=[tR[2]])
                V(lambda h: h.tensor_tensor(out=tv[3], in0=x1, in1=sb_, op=ALU.mult), reads=crd, writes=[tR[3]])
                V(lambda h: h.tensor_tensor(out=x1, in0=tv[0], in1=tv[1], op=ALU.subtract), reads=[tR[0], tR[1]], writes=[xbuf])
                V(lambda h: h.tensor_tensor(out=x2, in0=tv[2], in1=tv[3], op=ALU.add), reads=[tR[2], tR[3]], writes=[xbuf])

            def group(col0, ncols, subs, fn):
                wt = wt2[wt2_i[0] % 2]
                wt2_i[0] += 1
                DG(lambda h: h.dma_start(out=wt[:, :, 0:ncols], in_=Wc[:, col0:col0 + ncols].rearrange("(kc p) n -> p kc n", p=128)), writes=[wt])
                for si, (s, r) in enumerate(subs):
                    bk = bank()
                    for kc in range(KC):
                        PE(lambda h, kc=kc, s=s, r=r, bk=bk: h.matmul(bk[0:r, 0:ncols], lhsT=hT[:, kc, s * 128:s * 128 + r], rhs=wt[:, kc, 0:ncols],
                                                                      start=(kc == 0), stop=(kc == KC - 1)),
                           reads=[wt, hT], writes=[bk], inc=(kc == KC - 1))
                    fn(si, s, r, bk)

            halves = [allsubs[0:2], allsubs[2:4]] if nsub == 4 else [allsubs]
            for subs in halves:
                for si, (s, r) in enumerate(subs):
                    if prm:
                        DS(lambda h, si=si, s=s, r=r: h.dma_start(out=ropN[si][0:r, :], in_=IN['rope_nsa'][pos0 + s * 128:pos0 + s * 128 + r, :]), writes=[ropN[si]])
                        DS(lambda h, si=si, s=s, r=r: h.dma_start(out=ropR[si][0:r, :], in_=IN['rope_ret'][pos0 + s * 128:pos0 + s * 128 + r, :]), writes=[ropR[si]])
                    else:
                        DS(lambda h, si=si, r=r: h.dma_start(out=ropN[si][0:r, :], in_=IN['rope_nsa'][T:T + 1, :].to_broadcast([r, 16])), writes=[ropN[si]])
                        DS(lambda h, si=si, r=r: h.dma_start(out=ropR[si][0:r, :], in_=IN['rope_ret'][T:T + 1, :].to_broadcast([r, 64])), writes=[ropR[si]])

                es_kv = ExitStack()
                rowb = [fw.sb(f"rowb{i}", [128, 1024], F32, es=es_kv) for i in range(2)]
                winb = [fw.sb(f"winb{i}", [128, 512], F32, es=es_kv) for i in range(2)]

                def f_kv0(si, s, r, bk):
                    acopy(rowb[si][0:r, 0:512], bk[0:r, 0:512], [bk], [rowb[si]])
                group(1024, 512, subs, f_kv0)

                def f_kv1(si, s, r, bk):
                    d3 = rowb[si][0:r, 512:768].rearrange("p (h d) -> p h d", d=64)
                    rms_heads(d3, bk[0:r, 0:256].rearrange("p (h d) -> p h d", d=64), r, 4, Gk[0:r, 1, :], bk, rowb[si])
                    rope_ip(d3, r, 4, 8, ropN[si][0:r, 0:8], ropN[si][0:r, 8:16], ropN[si], rowb[si])
                    acopy(rowb[si][0:r, 768:1024], bk[0:r, 256:512], [bk], [rowb[si]])
                    dst = OUT['kv_p'][pos0 + s * 128:pos0 + s * 128 + r, :] if prm else OUT['kv_s'][0:r, :]
                    DS(lambda h, si=si, r=r, dst=dst: h.dma_start(out=dst, in_=rowb[si][0:r, :]), reads=[rowb[si]], writes=[OUT['kv_p'], OUT['kv_s']])
                group(1536, 512, subs, f_kv1)

                def f_kvw(si, s, r, bk):
                    d3 = winb[si][0:r, 0:256].rearrange("p (h d) -> p h d", d=64)
                    rms_heads(d3, bk[0:r, 0:256].rearrange("p (h d) -> p h d", d=64), r, 4, Gk[0:r, 2, :], bk, winb[si])
                    rope_ip(d3, r, 4, 8, ropN[si][0:r, 0:8], ropN[si][0:r, 8:16], ropN[si], winb[si])
                    acopy(winb[si][0:r, 256:512], bk[0:r, 256:512], [bk], [winb[si]])
                    if prm and last:
                        DS(lambda h, si=si, s=s, r=r: h.dma_start(out=OUT['win_p'][s * 128:s * 128 + r, :], in_=winb[si][0:r, :]), reads=[winb[si]], writes=[OUT['win_p']])
                    if prm:
                        DS(lambda h, si=si, s=s, r=r: h.dma_start(out=WIN[pos0 + s * 128:pos0 + s * 128 + r, :], in_=winb[si][0:r, :]), reads=[winb[si]], writes=[WIN])
                    if not prm:
                        DS(lambda h, si=si, r=r: h.dma_start(out=OUT['win_s'][:, 511, :], in_=winb[si][0:r, :]), reads=[winb[si]], writes=[OUT['win_s']])
                        DS(lambda h: h.dma_start(out=OUT['win_s'][:, 0:511, :], in_=IN['cache_win'][:, 1:512, :]), writes=[OUT['win_s']])
                group(2048, 512, subs, f_kvw)
                fw.barrier()
                es_kv.close()
                es_rt = ExitStack()
                gng = fw.sb("gng", [128, 1024], F32, es=es_rt)
                gnb = fw.sb("gnb", [128, 1024], F32, es=es_rt)
                dmk = fw.sb("dmk", [128, 8, 128], F32, es=es_rt)
                DS(lambda h, gng=gng: h.dma_start(out=gng[:], in_=IN['ret_gn_g'][0:1, :].to_broadcast([128, 1024])), writes=[gng])
                DS(lambda h, gnb=gnb: h.dma_start(out=gnb[:], in_=IN['ret_gn_b'][0:1, :].to_broadcast([128, 1024])), writes=[gnb])
                for h_ in range(8):
                    DS(lambda h, h_=h_, dmk=dmk: h.dma_start(out=dmk[:, h_, :], in_=IN['ret_dmaskT'][h_]), writes=[dmk])
                rqb = [fw.sb(f"rqb{i}", [128, 512], F32, es=es_rt) for i in range(2)]
                rkb = [fw.sb(f"rkb{i}", [128, 512], F32, es=es_rt) for i in range(2)]
                rvb = [fw.sb(f"rvb{i}", [128, 1024], F32, es=es_rt) for i in range(2)]
                ynb = [fw.sb(f"ynb{i}", [128, 8, 128], F32, es=es_rt) for i in range(2)]
                qkT = [fw.sb(f"qkT{i}", [128, 4, 128], F32, es=es_rt) for i in range(2)]
                smb = [fw.sb(f"smb{i}", [128, 128], F32, es=es_rt) for i in range(2)]
                Asb = [fw.sb(f"Asb{i}", [128, 128], F32, es=es_rt) for i in range(2)]
                ktl = fw.sb("ktl", [128, 512], F32, es=es_rt)

                def f_rq(si, s, r, bk):
                    acopy(rqb[si][0:r, :], bk[0:r, 0:512], [bk], [rqb[si]])
                    rope_ip(rqb[si][0:r, :].rearrange("p (h d) -> p h d", d=64), r, 8, 32, ropR[si][0:r, 0:32], ropR[si][0:r, 32:64], ropR[si], rqb[si])
                group(2608, 512, subs, f_rq)

                def f_rk(si, s, r, bk):
                    A(lambda h, si=si, r=r, bk=bk: h.mul(out=rkb[si][0:r, :], in_=bk[0:r, 0:512], mul=0.125), reads=[bk], writes=[rkb[si]])
                    rope_ip(rkb[si][0:r, :].rearrange("p (h d) -> p h d", d=64), r, 8, 32, ropR[si][0:r, 0:32], ropR[si][0:r, 32:64], ropR[si], rkb[si])
                group(3120, 512, subs, f_rk)

                def f_rv0(si, s, r, bk):
                    acopy(rvb[si][0:r, 0:512], bk[0:r, 0:512], [bk], [rvb[si]])
                group(3632, 512, subs, f_rv0)

                def f_rv1(si, s, r, bk):
                    acopy(rvb[si][0:r, 512:1024], bk[0:r, 0:512], [bk], [rvb[si]])
                group(4144, 512, subs, f_rv1)

                if prm:
                    for si, (s, r) in enumerate(subs):
                        for wh, srcb in ((0, rqb[si]), (1, rkb[si])):
                            bt = bank()
                            for c4 in range(4):
                                PE(lambda h, bt=bt, c4=c4, srcb=srcb: h.transpose(out=bt[:, c4 * 128:(c4 + 1) * 128], in_=srcb[:, c4 * 128:(c4 + 1) * 128], identity=ident[:]),
                                   reads=[srcb, ident], writes=[bt])
                            acopy(qkT[wh][:, :, :], bt[:, :].rearrange("p (c n) -> p c n", n=128), [bt], [qkT[wh]])
                        qT_, kT_ = qkT
                        for hh_ in range(8):
                            V(lambda h, si=si, hh_=hh_: h.tensor_scalar(out=ktl[:, hh_ * 64:(hh_ + 1) * 64], in0=rkb[si][:, hh_ * 64:(hh_ + 1) * 64],
                                                                        scalar1=rdec[:, 8 + hh_:9 + hh_], scalar2=None, op0=ALU.mult),
                              reads=[rkb[si], rdec], writes=[ktl])
                        for h8 in range(8):
                            hp, hh = h8 // 2, h8 % 2
                            pr_ = slice(64 * hh, 64 * hh + 64)
                            bS = bank()
                            mm(bS[:, 0:128], kT_[pr_, hp, :], qT_[pr_, hp, :], [qkT[0], qkT[1]], [bS])
                            sm = smb[h8 % 2]
                            tt(sm[:], bS[:, 0:128], dmk[:, h8, :], ALU.mult, [bS, dmk], [sm])
                            bA = bank()
                            mm(bA[:, 0:128], sm[:], rvb[si][:, h8 * 128:(h8 + 1) * 128], [sm, rvb[si]], [bA])
                            bB = bank()
                            mm(bB[:, 0:128], qT_[pr_, hp, :], S2[pr_, hp, :], [qkT[0], S2], [bB])
                            As = Asb[h8 % 2]
                            acopy(As[:], bA[:, 0:128], [bA], [As])
                            stt(ynb[si][:, h8, :], bB[:, 0:128], rdec[:, h8:h8 + 1], As[:], ALU.mult, ALU.add, [bB, rdec, As], [ynb[si]])
                            bK = bank()
                            mm(bK[:, 0:128], ktl[:, hp * 128:(hp + 1) * 128], rvb[si][:, h8 * 128:(h8 + 1) * 128], [ktl, rvb[si]], [bK])
                            stt(S2[pr_, hp, :], S2[pr_, hp, :], GAMMA_C[h8], bK[pr_, 0:128], ALU.mult, ALU.add, [S2, bK], [S2])
                if not prm:
                    r = N
                    for nm_, srcb_ in (("q", rqb[0]), ("k", rkb[0])):
                        DS(lambda h, nm_=nm_, srcb_=srcb_: h.dma_start(out=SCR[nm_][:, :], in_=srcb_[0:r, :]), reads=[srcb_], writes=[SCR[nm_]])
                    DS(lambda h: h.dma_start(out=SCR["v"][:, :], in_=rvb[0][0:r, :]), reads=[rvb[0]], writes=[SCR["v"]])
                    es_s = ExitStack()
                    S0p = fw.sb("S0p", [128, NSMP, 128], F32, es=es_s)
                    vB = fw.sb("vB", [128, NSMP, 128], F32, es=es_s)
                    kTp = fw.sb("kTp", [128, NSMP], F32, es=es_s)
                    qTp = fw.sb("qTp", [128, NSMP], F32, es=es_s)
                    qTm = fw.sb("qTm", [128, NSMP, NSMP], F32, es=es_s)
                    qkp = fw.sb("qkp", [128, 8], F32, es=es_s)
                    V(lambda h: h.memset(qTm[:], 0.0), writes=[qTm])
                    tt(tA[0:r, 0:512], rqb[0][0:r, :], rkb[0][0:r, :], ALU.mult, [rqb[0], rkb[0]], [tA])
                    V(lambda h: h.tensor_reduce(out=qkp[0:r, 0:8], in_=tA[0:r, 0:512].rearrange("p (h d) -> p h d", d=64), axis=AX.X, op=ALU.add),
                      reads=[tA], writes=[qkp])
                    for hp in range(4):
                        for hh in range(2):
                            h8 = 2 * hp + hh
                            pr_ = slice(64 * hh, 64 * hh + 64)
                            DS(lambda h, h8=h8, pr_=pr_: h.dma_start(out=S0p[pr_, :, :], in_=IN['s_ret'][:, h8].rearrange("b d e -> d b e")), writes=[S0p])
                            DS(lambda h, h8=h8, pr_=pr_: h.dma_start(out=vB[pr_, :, :], in_=SCR["v"][:, h8 * 128:(h8 + 1) * 128].unsqueeze(0).to_broadcast([64, NSMP, 128])),
                               reads=[SCR["v"]], writes=[vB])
                        DS(lambda h, hp=hp: h.dma_start(out=kTp[:, :], in_=SCR["k"][:, hp * 128:(hp + 1) * 128].rearrange("b p -> p b")), reads=[SCR["k"]], writes=[kTp])
                        DS(lambda h, hp=hp: h.dma_start(out=qTp[:, :], in_=SCR["q"][:, hp * 128:(hp + 1) * 128].rearrange("b p -> p b")), reads=[SCR["q"]], writes=[qTp])
                        vcopy(qTm[:, :, :].rearrange("p a b -> p (a b)")[:, 0:NSMP * NSMP:NSMP + 1], qTp[:, :], [qTp], [qTm])
                        for hh in range(2):
                            h8 = 2 * hp + hh
                            pr_ = slice(64 * hh, 64 * hh + 64)
                            bq = bank()
                            for b_ in range(NSMP):
                                mm(bq[0:NSMP, 0:128], qTm[pr_, b_, :], S0p[pr_, b_, :], [qTm, S0p], [bq], start=(b_ == 0), stop=(b_ == NSMP - 1), inc=(b_ == NSMP - 1))
                            ts(tA[0:r, 0:128], rvb[0][0:r, h8 * 128:(h8 + 1) * 128], qkp[0:r, h8:h8 + 1], None, ALU.mult, None, [rvb[0], qkp], [tA])
                            stt(ynb[0][0:r, h8, :], bq[0:r, 0:128], GAMMA_1[h8], tA[0:r, 0:128], ALU.mult, ALU.add, [bq, tA], [ynb[0]])
                            ts(S0p[pr_, :, :], S0p[pr_, :, :], GAMMA_1[h8], None, ALU.mult, None, [S0p], [S0p])
                            for b_ in range(NSMP):
                                stt(vB[pr_, b_, :], vB[pr_, b_, :], kTp[pr_, b_:b_ + 1], S0p[pr_, b_, :], ALU.mult, ALU.add, [vB, kTp, S0p], [vB])
                            DS(lambda h, h8=h8, pr_=pr_: h.dma_start(out=OUT['ret_s'][:, h8].rearrange("b d e -> d b e"), in_=vB[pr_, :, :]), reads=[vB], writes=[OUT['ret_s']])
                    fw.barrier()
                    es_s.close()
                for si, (s, r) in enumerate(subs):
                    y3 = ynb[si][0:r, :, :]
                    V(lambda h, y3=y3, r=r: h.tensor_reduce(out=tS[0:r, 0:8], in_=y3, axis=AX.X, op=ALU.add), reads=[ynb[si]], writes=[tS])
                    ts(tS[0:r, 0:8], tS[0:r, 0:8], -1.0 / 128, None, ALU.mult, None, [tS], [tS])
                    V(lambda h, y3=y3, r=r: h.tensor_tensor(out=y3, in0=y3, in1=tS[0:r, 0:8].unsqueeze(2).to_broadcast([r, 8, 128]), op=ALU.add),
                      reads=[tS, ynb[si]], writes=[ynb[si]])
                    act(tA[0:r, :].rearrange("p (h d) -> p h d", d=128), y3, AF.Square, [ynb[si]], [tA])
                    V(lambda h, r=r: h.tensor_reduce(out=tS[0:r, 0:8], in_=tA[0:r, :].rearrange("p (h d) -> p h d", d=128), axis=AX.X, op=ALU.add),
                      reads=[tA], writes=[tS])
                    act(tS[0:r, 0:8], tS[0:r, 0:8], AF.Sqrt, [tS, eps5], [tS], bias=eps5[0:r, 0:1], scale=1.0 / 128)
                    V(lambda h, r=r: h.reciprocal(out=tS[0:r, 0:8], in_=tS[0:r, 0:8]), reads=[tS], writes=[tS])
                    V(lambda h, y3=y3, r=r: h.tensor_tensor(out=y3, in0=y3, in1=tS[0:r, 0:8].unsqueeze(2).to_broadcast([r, 8, 128]), op=ALU.mult),
                      reads=[tS, ynb[si]], writes=[ynb[si]])
                    y2 = ynb[si][0:r, :, :].rearrange("p h d -> p (h d)")
                    tt(y2, y2, gng[0:r, :], ALU.mult, [ynb[si], gng], [ynb[si]])
                    tt(y2, y2, gnb[0:r, :], ALU.add, [ynb[si], gnb], [ynb[si]])

                def f_rg(half_):
                    def f(si, s, r, bk):
                        act(tA[0:r, 0:512], bk[0:r, 0:512], AF.Silu, [bk], [tA])
                        y2 = ynb[si][0:r, :, :].rearrange("p h d -> p (h d)")[:, half_ * 512:(half_ + 1) * 512]
                        tt(y2, y2, tA[0:r, 0:512], ALU.mult, [ynb[si], tA], [ynb[si]])
                    return f
                group(4656, 512, subs, f_rg(0))
                group(5168, 512, subs, f_rg(1))
                for si, (s, r) in enumerate(subs):
                    y2 = ynb[si][:, :, :].rearrange("p h d -> p (h d)")
                    for c4 in range(2):
                        bt = bank()
                        for q in range(4):
                            c = c4 * 4 + q
                            PE(lambda h, bt=bt, q=q, c=c, r=r, y2=y2: h.transpose(out=bt[:, q * 128:q * 128 + r], in_=y2[0:r, c * 128:(c + 1) * 128], identity=ident[0:r, 0:r]),
                               reads=[ynb[si], ident], writes=[bt])
                        V(lambda h, bt=bt, c4=c4, s=s, r=r: h.tensor_copy(out=oT[:, 8 + c4 * 4:12 + c4 * 4, s * 128:s * 128 + r],
                                                                         in_=bt[:, :].rearrange("p (q n) -> p q n", n=128)[:, :, 0:r]), reads=[bt], writes=[oT])
                fw.barrier()
                es_rt.close()
            if NSA_ON and (prm or NSA_SAMPLE):
                fw.barrier()
                NQ = N if prm else 128
                nqs = NQ // 128
                q0 = jt * TT if prm else T
                qcoef = 1 if prm else 0
                es_n = ExitStack()
                qnT = fw.sb("qnT", [128, 8, NQ], BF16, es=es_n)
                qrT = fw.sb("qrT", [128, 8, NQ], BF16, es=es_n)
                gat = fw.sb("gat", [128, nqs, 48], F32, es=es_n)
                KcT = fw.sb("KcT", [128, 4, 128], BF16, es=es_n)
                Vc = fw.sb("Vc", [128, 4, 97], BF16, es=es_n)
                ropq = fw.sb("ropq", [128, nqs, 16], F32, es=es_n)
                qtm = [fw.sb(f"qtm{i}", [128, 512], F32, es=es_n) for i in range(2)]
                seltab = fw.sb("seltab", [128, nqs, 96], F32, es=es_n)
                ovc = fw.sb("ovc", [128, 33], F32, es=es_n)
                ones_bf = fw.sb("ones_bf", [128, NQ], BF16, es=es_n)
                zer_bf = fw.sb("zer_bf", [128, 128], BF16, es=es_n)
                cmask = fw.sb("cmask", [128, NQ], BF16, es=es_n)
                w2kf = fw.sb("w2kf", [128, 2, 64], F32, es=es_n)
                w2vf = fw.sb("w2vf", [128, 2, 64], F32, es=es_n)
                w2kd = fw.sb("w2kd", [128, 2, 128], BF16, es=es_n)
                w2v = fw.sb("w2v", [128, 2, 64], BF16, es=es_n)
                cst = fw.sb("cst", [128, 8], F32, es=es_n)
                b2vb = fw.sb("b2vb", [128, 64], F32, es=es_n)
                if not prm:
                    V(lambda h: h.memset(qnT[:], 0.0), writes=[qnT])
                    V(lambda h: h.memset(qrT[:], 0.0), writes=[qrT])
                    V(lambda h: h.memset(gat[:], 0.0), writes=[gat])
                    onsa_tot = fw.sb("onsa_tot", [128, 4, 256], F32, es=es_n)
                    V(lambda h: h.memset(onsa_tot[:], 0.0), writes=[onsa_tot])
                    row0m = fw.sb("row0m", [128, NQ], BF16, es=es_n)
                    ptb_i = fw.sb("ptb_i", [128, NSMP * 16], I32, es=es_n)
                    ptb_f = fw.sb("ptb_f", [128, NSMP * 16], F32, es=es_n)
                    pidx = fw.sb("pidx", [128, 1], F32, es=es_n)
                    idx_i = fw.sb("idx_i", [128, NSMP * 16], I32, es=es_n)
                    newrow = fw.sb("newrow", [128, 1024], F32, es=es_n)
                    DS(lambda h: h.dma_start(out=ptb_i[:], in_=IN['page_table'][:, :].rearrange("b k -> (b k)").unsqueeze(0).to_broadcast([128, NSMP * 16])), writes=[ptb_i])
                    vcopy(ptb_f[:], ptb_i[:], [ptb_i], [ptb_f])
                    G(lambda h: h.iota(pidx[:], pattern=[[0, 1]], base=0, channel_multiplier=1, allow_small_or_imprecise_dtypes=True), writes=[pidx])
                    ts(ptb_f[:], ptb_f[:], 128.0, pidx[:, 0:1], ALU.mult, ALU.add, [ptb_f, pidx], [ptb_f])
                    ts(ptb_f[:], ptb_f[:], 2.0, None, ALU.mult, None, [ptb_f], [ptb_f])
                    vcopy(idx_i[:], ptb_f[:], [ptb_f], [idx_i])
                    idx_b = fw.sb("idx_b", [128, NSMP * 16], I32, es=es_n)
                    ts(ptb_f[:], ptb_f[:], 1.0, None, ALU.add, None, [ptb_f], [ptb_f])
                    vcopy(idx_b[:], ptb_f[:], [ptb_f], [idx_b])
                    stg = [fw.sb(f"stg{i}", [128, 512], F32, es=es_n) for i in range(2)]
                    stg_i = [0]
                for s_ in range(nqs):
                    if prm:
                        DS(lambda h, s_=s_: h.dma_start(out=ropq[:, s_, :], in_=IN['rope_nsa'][q0 + s_ * 128:q0 + (s_ + 1) * 128, :]), writes=[ropq])
                        DS(lambda h, s_=s_: h.dma_start(out=seltab[:, s_, :], in_=IN['sel_tab'][jt * 4 + s_]), writes=[seltab])
                    else:
                        DS(lambda h: h.dma_start(out=ropq[:, 0, :], in_=IN['rope_nsa'][T:T + 1, :].to_broadcast([128, 16])), writes=[ropq])
                        DS(lambda h: h.dma_start(out=seltab[:, 0, :], in_=IN['sel_tab'][16]), writes=[seltab])
                DS(lambda h: h.dma_start(out=ovc[:], in_=IN['cmp_ov'][:, :]), writes=[ovc])
                V(lambda h: h.memset(ones_bf[:], 1.0), writes=[ones_bf])
                V(lambda h: h.memset(zer_bf[:], 0.0), writes=[zer_bf])
                G(lambda h: h.affine_select(out=cmask[:], in_=ones_bf[:], pattern=[[qcoef, NQ]], compare_op=ALU.is_ge, fill=0.0,
                                            base=q0 - 31, channel_multiplier=-16), reads=[ones_bf], writes=[cmask])
                if not prm:
                    G(lambda h: h.affine_select(out=row0m[:], in_=ones_bf[:], pattern=[[0, NQ]], compare_op=ALU.is_ge, fill=0.0,
                                                base=0, channel_multiplier=-1), reads=[ones_bf], writes=[row0m])
                DS(lambda h: h.dma_start(out=w2kf[:], in_=IN['cmp_k_w2'][0].rearrange("(c p) d -> p c d", p=128)), writes=[w2kf])
                DS(lambda h: h.dma_start(out=w2vf[:], in_=IN['cmp_v_w2'][0].rearrange("(c p) d -> p c d", p=128)), writes=[w2vf])
                for hc_ in range(2):
                    V(lambda h, hc_=hc_: h.tensor_copy(out=w2kd[:, hc_, :].rearrange("p (t d) -> p t d", d=64),
                                                       in_=w2kf[:, hc_, :].unsqueeze(1).to_broadcast([128, 2, 64])), reads=[w2kf], writes=[w2kd])
                vcopy(w2v[:], w2vf[:], [w2vf], [w2v])
                DS(lambda h: h.dma_start(out=cst[:, 0:2], in_=IN['cmp_k_b1'][0].rearrange("(c p) -> p c", p=128)), writes=[cst])
                DS(lambda h: h.dma_start(out=cst[:, 2:4], in_=IN['cmp_v_b1'][0].rearrange("(c p) -> p c", p=128)), writes=[cst])
                for hf_ in range(2):
                    DS(lambda h, hf_=hf_: h.dma_start(out=cst[64 * hf_:64 * hf_ + 64, 4:5], in_=IN['cmp_k_b2'][0].rearrange("(p c) -> p c", c=1)), writes=[cst])
                    DS(lambda h, hf_=hf_: h.dma_start(out=cst[64 * hf_:64 * hf_ + 64, 5:6], in_=IN['nsa_k_norm'][0, 0].rearrange("(p c) -> p c", c=1)), writes=[cst])
                DS(lambda h: h.dma_start(out=b2vb[:], in_=IN['cmp_v_b2'][0:1, :].to_broadcast([128, 64])), writes=[b2vb])

                def f_q(gi):
                    def f(si, s, r, bk):
                        qt = qtm[si % 2]
                        q3 = qt[0:r, :].rearrange("p (h d) -> p h d", d=64)
                        rms_heads(q3, bk[0:r, 0:512].rearrange("p (h d) -> p h d", d=64), r, 8, Gq[0:r, :], bk, qt)
                        for dstT in (qnT, qrT):
                            bt = bank()
                            for c4 in range(4):
                                PE(lambda h, bt=bt, c4=c4, qt=qt, r=r: h.transpose(out=bt[:, c4 * 128:c4 * 128 + r], in_=qt[0:r, c4 * 128:(c4 + 1) * 128], identity=ident[0:r, 0:r]),
                                   reads=[qt, ident], writes=[bt])
                            acopy(dstT[:, gi * 4:gi * 4 + 4, s * 128:s * 128 + r], bt[:, :].rearrange("p (c n) -> p c n", n=128)[:, :, 0:r], [bt], [dstT])
                            if dstT is qnT:
                                rope_ip(q3, r, 8, 8, ropq[0:r, s, 0:8], ropq[0:r, s, 8:16], ropq, qt)
                    return f
                group(0, 512, allsubs, f_q(0))
                group(512, 512, allsubs, f_q(1))

                def f_gt(si, s, r, bk):
                    act(gat[0:r, s, :], bk[0:r, 0:48], AF.Sigmoid, [bk], [gat])
                group(2560, 48, allsubs, f_gt)

                saved_pool = list(bank_pool)
                for bseq in ([None] if prm else list(range(NSMP))):
                    bank_pool[:] = saved_pool
                    if prm:
                        nkc = 4 * (jt + 1)
                        nks = nkc
                    else:
                        nkc = 16
                        nks = 17
                        V(lambda h: h.memset(newrow[:], 0.0), writes=[newrow])
                        DS(lambda h, bseq=bseq: h.dma_start(out=newrow[0:1, :], in_=OUT['kv_s'][bseq:bseq + 1, :]), reads=[OUT['kv_s']], writes=[newrow])
                    nch = 8 * nkc
                    ncmp = nch - 1

                    def load_rows(dst_ap, dbuf, kb, c0, c1, bseq=bseq):
                        if prm:
                            DS(lambda h: h.dma_start(out=dst_ap, in_=OUT['kv_p'][kb * 128:(kb + 1) * 128, c0:c1]), reads=[OUT['kv_p']], writes=[dbuf])
                        elif kb < 16:
                            col = bseq * 16 + kb
                            half = 0 if c0 < 512 else 1
                            ixt = idx_i if half == 0 else idx_b
                            if c1 - c0 == 512:
                                fw.dma("gpsimd", lambda h: h.indirect_dma_start(out=dst_ap, out_offset=None, in_=POOL2D[:, :],
                                                                               in_offset=bass.IndirectOffsetOnAxis(ap=ixt[:, col:col + 1], axis=0)),
                                       reads=[ixt], writes=[dbuf])
                            else:
                                st_ = stg[stg_i[0] % 2]
                                stg_i[0] += 1
                                fw.dma("gpsimd", lambda h: h.indirect_dma_start(out=st_[:], out_offset=None, in_=POOL2D[:, :],
                                                                               in_offset=bass.IndirectOffsetOnAxis(ap=ixt[:, col:col + 1], axis=0)),
                                       reads=[ixt], writes=[st_])
                                vcopy(dst_ap, st_[:, c0 - 512 * half:c1 - 512 * half], [st_], [dbuf])
                        else:
                            vcopy(dst_ap, newrow[:, c0:c1], [newrow], [dbuf])

                    es_c = ExitStack()
                    kvcT = fw.sb("kvcT", [128, 4, nkc * 128], BF16, es=es_c)
                    geluT = fw.sb("geluT", [128, 2, 2, 4, 128], BF16, es=es_c)
                    tokr = [fw.sb(f"tokr{i}", [128, 512], F32, es=es_c) for i in range(2)]
                    xs_ = fw.sb("xs_", [128, 128], F32, es=es_c)
                    xq_ = fw.sb("xq_", [128, 128], F32, es=es_c)
                    V(lambda h, geluT=geluT: h.memset(geluT[:], 0.0), writes=[geluT])
                    for kb in range(nkc):
                        tr_ = tokr[kb % 2]
                        load_rows(tr_[:], tr_, kb, 0, 512)
                        bt = bank()
                        for c4 in range(4):
                            PE(lambda h, bt=bt, c4=c4, tr_=tr_: h.transpose(out=bt[:, c4 * 128:(c4 + 1) * 128], in_=tr_[:, c4 * 128:(c4 + 1) * 128], identity=ident[:]),
                               reads=[tr_, ident], writes=[bt])
                        vcopy(kvcT[:, 0:4, kb * 128:(kb + 1) * 128], bt[:, :].rearrange("p (c n) -> p c n", n=128), [bt], [kvcT])
                    for kvi, wname in enumerate(('cmp_k_w1', 'cmp_v_w1')):
                        for rr in range(2):
                            for hf_ in range(2):
                                DG(lambda h, rr=rr, hf_=hf_, wname=wname: h.dma_start(out=wt2[rr][64 * hf_:64 * hf_ + 64, :, 0:256],
                                                                                      in_=IN[wname][0, rr].rearrange("(l d) h -> d l h", d=64)), writes=[wt2[rr]])
                        for g in range(4):
                            gh = g % 2
                            pr_ = slice(64 * gh, 64 * gh + 64)
                            for hc in range(2):
                                b0 = bank()
                                for rr in range(2):
                                    for l in range(16):
                                        mm(b0[:, rr * 128:rr * 128 + nch], wt2[rr][pr_, l, hc * 128:(hc + 1) * 128],
                                           kvcT[pr_, kvi * 2 + g // 2, l:nch * 16:16], [wt2[rr], kvcT], [b0], start=(l == 0), stop=(l == 15), inc=(l == 15))
                                ts(xs_[:, 0:ncmp], b0[:, 0:ncmp], cst[:, kvi * 2 + hc:kvi * 2 + hc + 1], None, ALU.add, None, [b0, cst], [xs_])
                                tt(xs_[:, 0:ncmp], xs_[:, 0:ncmp], b0[:, 129:129 + ncmp], ALU.add, [xs_, b0], [xs_])
                                act(geluT[:, kvi, hc, g, 0:ncmp], xs_[:, 0:ncmp], AF.Gelu_apprx_tanh, [xs_], [geluT])
                    for g in range(4):
                        bk_ = bank()
                        for hc in range(2):
                            mm(bk_[:, 0:128], w2kd[:, hc, :], geluT[:, 0, hc, g, :], [w2kd, geluT], [bk_], start=(hc == 0), stop=(hc == 1))
                        ts(xs_[:], bk_[:, 0:128], cst[:, 4:5], None, ALU.add, None, [bk_, cst], [xs_])
                        act(xq_[:], xs_[:], AF.Square, [xs_], [xq_])
                        bss = bank()
                        mm(bss[:, 0:128], blk1[:], xq_[:], [blk1, xq_], [bss])
                        act(xq_[:], bss[:, 0:128], AF.Sqrt, [bss, eps6], [xq_], bias=eps6[:, 0:1], scale=1.0 / 64)
                        V(lambda h, xq_=xq_: h.reciprocal(out=xq_[:], in_=xq_[:]), reads=[xq_], writes=[xq_])
                        tt(xs_[:], xs_[:], xq_[:], ALU.mult, [xs_, xq_], [xs_])
                        ts(KcT[:, g, :], xs_[:], cst[:, 5:6], None, ALU.mult, None, [xs_, cst], [KcT])
                        bv_ = bank()
                        for hc in range(2):
                            mm(bv_[:, 0:64], geluT[:, 1, hc, g, :], w2v[:, hc, :], [geluT, w2v], [bv_], start=(hc == 0), stop=(hc == 1))
                        tt(Vc[:, g, 0:64], bv_[:, 0:64], b2vb[:], ALU.add, [bv_, b2vb], [Vc])
                        vcopy(Vc[:, g, 64:97], ovc[:], [ovc], [Vc])
                    fw.barrier()
                    es_c.close()

                    es_a = ExitStack()
                    ebuf = [fw.sb(f"ebuf{i}", [128, NQ], BF16, es=es_a) for i in range(3)]
                    mskb = [fw.sb(f"mskb{i}", [128, NQ], BF16, es=es_a) for i in range(2)]
                    wmk = [fw.sb(f"wmk{i}", [128, NQ], BF16, es=es_a) for i in range(2)]
                    G3 = fw.sb("G3", [128, 32, 32], F32, es=es_a)
                    sel_all = fw.sb("sel_all", [128, nqs, 32], F32, es=es_a)
                    selx = fw.sb("selx", [128, nqs, 128], F32, es=es_a)
                    impg = fw.sb("impg", [128, nqs, 32], F32, es=es_a)
                    onsa = fw.sb("onsa", [128, nqs, 256], F32, es=es_a)
                    sc_ = fw.sb("sc_", [128, 32], F32, es=es_a)
                    cnt_ = fw.sb("cnt_", [128, 32], F32, es=es_a)
                    w4 = fw.sb("w4", [128, 4], F32, es=es_a)
                    w4g = fw.sb("w4g", [128, 4], F32, es=es_a)
                    ktk = [fw.sb(f"ktk{i}", [128, 64], F32, es=es_a) for i in range(2)]
                    vtk = [fw.sb(f"vtk{i}", [128, 64], F32, es=es_a) for i in range(2)]
                    ksd = [fw.sb(f"ksd{i}", [128, 128], F32, es=es_a) for i in range(2)]
                    kTb = [fw.sb(f"kTb{i}", [128, 128], BF16, es=es_a) for i in range(2)]
                    VSb = [fw.sb(f"VSb{i}", [128, 65], BF16, es=es_a) for i in range(2)]
                    for t_ in VSb:
                        V(lambda h, t_=t_: h.memset(t_[:], 1.0), writes=[t_])
                    bank_pool[:] = banks[0:4]
                    eb_i = [0]
                    kv_i = [0]

                    def combine(g, hi, br, first, with_imp):
                        h16 = 4 * g + hi
                        accb = banks[4 + hi]
                        a3 = accb[:, 0:nqs * 128].rearrange("p (s n) -> p s n", n=128)
                        ts(w4[:, 0:nqs], a3[:, :, 64], 1e-30, None, ALU.max, None, [accb], [w4])
                        V(lambda h: h.reciprocal(out=w4[:, 0:nqs], in_=w4[:, 0:nqs]), reads=[w4], writes=[w4])
                        tt(w4g[:, 0:nqs], w4[:, 0:nqs], gat[:, :, 3 * h16 + br], ALU.mult, [w4, gat], [w4g])
                        for qs in range(nqs):
                            dst = onsa[:, qs, hi * 64:(hi + 1) * 64]
                            if first:
                                ts(dst, a3[:, qs, 0:64], w4g[:, qs:qs + 1], None, ALU.mult, None, [accb, w4g], [onsa])
                            else:
                                stt(dst, a3[:, qs, 0:64], w4g[:, qs:qs + 1], dst, ALU.mult, ALU.add, [accb, w4g, onsa], [onsa])
                            if with_imp:
                                stt(impg[:, qs, :], a3[:, qs, 65:97], w4[:, qs:qs + 1], impg[:, qs, :], ALU.mult, ALU.add, [accb, w4, impg], [impg])

                    def attend_block(g, KT_ap_fn, KT_buf, Vblk, Vw, msk, qT_, first):
                        for hi in range(4):
                            h16 = 4 * g + hi
                            hh, hp = h16 % 2, h16 // 2
                            pr_ = slice(64 * hh, 64 * hh + 64)
                            bS = bank()
                            mm(bS[:, 0:NQ], KT_ap_fn(pr_), qT_[pr_, hp, :], [KT_buf, qT_], [bS])
                            e = ebuf[eb_i[0] % 3]
                            eb_i[0] += 1
                            act(e[:], bS[:, 0:NQ], AF.Exp, [bS], [e], scale=0.125)
                            if msk is not None:
                                tt(e[:], e[:], msk[:], ALU.mult, [e, msk], [e])
                            accb = banks[4 + hi]
                            if first:
                                for z_ in range(nqs):
                                    PE(lambda h, accb=accb, z_=z_: h.matmul(accb[:, z_ * 128:(z_ + 1) * 128], lhsT=zer_bf[:], rhs=ones_bf[:, 0:128], start=(z_ == 0), stop=True,
                                                                            skip_group_check=True),
                                       reads=[zer_bf, ones_bf], writes=[accb])
                            for qs in range(nqs):
                                PE(lambda h, accb=accb, qs=qs, e=e: h.matmul(accb[:, qs * 128:qs * 128 + Vw], lhsT=e[:, qs * 128:(qs + 1) * 128], rhs=Vblk,
                                                                             start=False, stop=True, skip_group_check=True),
                                   reads=[e, Vc, VSb[0], VSb[1]], writes=[accb])

                    def load_kv_block(loader):
                        i_ = kv_i[0] % 2
                        kv_i[0] += 1
                        loader(ktk[i_], vtk[i_])
                        V(lambda h: h.tensor_copy(out=ksd[i_][:, :].rearrange("p (t d) -> p t d", d=64), in_=ktk[i_][:, :].unsqueeze(1).to_broadcast([128, 2, 64])),
                          reads=[ktk[i_]], writes=[ksd[i_]])
                        bt = bank()
                        PE(lambda h: h.transpose(out=bt[:, 0:128], in_=ksd[i_][:], identity=ident[:]), reads=[ksd[i_], ident], writes=[bt])
                        acopy(kTb[i_][:], bt[:, 0:128], [bt], [kTb[i_]])
                        vcopy(VSb[i_][:, 0:64], vtk[i_][:], [vtk[i_]], [VSb[i_]])
                        return kTb[i_], VSb[i_]

                    if not prm:
                        onsa4 = fw.sb("onsa4", [128, 4, 256], F32, es=es_a)
                        sel4 = fw.sb("sel4", [128, 4, 32], F32, es=es_a)
                        imp4 = fw.sb("imp4", [128, 4, 32], F32, es=es_a)
                        selx4 = fw.sb("selx4", [128, 4, 128], F32, es=es_a)
                        mska = [fw.sb(f"mska{i}", [128, 512], BF16, es=es_a) for i in range(2)]
                        rowsb = [fw.sb(f"rowsb{i}", [128, 512], F32, es=es_a) for i in range(2)]
                        V(lambda h, imp4=imp4: h.memset(imp4[:], 0.0), writes=[imp4])

                        def s_attend(g, KT_ap_fn, KT_buf, Vblk, Vw, msk_ap, msk_buf, qT_, zero):
                            for hi in range(4):
                                h16 = 4 * g + hi
                                hh, hp = h16 % 2, h16 // 2
                                pr_ = slice(64 * hh, 64 * hh + 64)
                                bS = bank()
                                mm(bS[:, 0:NQ], KT_ap_fn(pr_), qT_[pr_, hp, :], [KT_buf, qT_], [bS])
                                e = ebuf[eb_i[0] % 3]
                                eb_i[0] += 1
                                act(e[:], bS[:, 0:NQ], AF.Exp, [bS], [e], scale=0.125)
                                if msk_ap is not None:
                                    tt(e[:], e[:], msk_ap, ALU.mult, [e, msk_buf], [e])
                                accb = banks[4 + hi]
                                if zero:
                                    for z_ in range(4):
                                        PE(lambda h, accb=accb, z_=z_: h.matmul(accb[:, z_ * 128:(z_ + 1) * 128], lhsT=zer_bf[:], rhs=ones_bf[:, 0:128], start=(z_ == 0), stop=True,
                                                                                skip_group_check=True),
                                           reads=[zer_bf, ones_bf], writes=[accb])
                                PE(lambda h, accb=accb, g=g, e=e, Vblk=Vblk, Vw=Vw: h.matmul(accb[:, g * 128:g * 128 + Vw], lhsT=e[:, 0:128], rhs=Vblk,
                                                                                           start=False, stop=True, skip_group_check=True),
                                   reads=[e, Vc, VSb[0], VSb[1]], writes=[accb])

                        def s_combine(g, hi, br, first, with_imp):
                            h16 = 4 * g + hi
                            accb = banks[4 + hi]
                            a3 = accb[:, 0:512].rearrange("p (s n) -> p s n", n=128)
                            ts(w4[:, 0:1], a3[:, g, 64:65], 1e-30, None, ALU.max, None, [accb], [w4])
                            V(lambda h: h.reciprocal(out=w4[:, 0:1], in_=w4[:, 0:1]), reads=[w4], writes=[w4])
                            tt(w4g[:, 0:1], w4[:, 0:1], gat[:, 0, 3 * h16 + br:3 * h16 + br + 1], ALU.mult, [w4, gat], [w4g])
                            dst = onsa4[:, g, hi * 64:(hi + 1) * 64]
                            if first:
                                ts(dst, a3[:, g, 0:64], w4g[:, 0:1], None, ALU.mult, None, [accb, w4g], [onsa4])
                            else:
                                stt(dst, a3[:, g, 0:64], w4g[:, 0:1], dst, ALU.mult, ALU.add, [accb, w4g, onsa4], [onsa4])
                            if with_imp:
                                stt(imp4[:, g, :], a3[:, g, 65:97], w4[:, 0:1], imp4[:, g, :], ALU.mult, ALU.add, [accb, w4, imp4], [imp4])

                        def s_kv(rb, g, kcol, vcol):
                            i_ = kv_i[0] % 2
                            kv_i[0] += 1
                            V(lambda h: h.tensor_copy(out=ksd[i_][:, :].rearrange("p (t d) -> p t d", d=64), in_=rb[:, kcol:kcol + 64].unsqueeze(1).to_broadcast([128, 2, 64])),
                              reads=[rb], writes=[ksd[i_]])
                            bt = bank()
                            PE(lambda h: h.transpose(out=bt[:, 0:128], in_=ksd[i_][:], identity=ident[:]), reads=[ksd[i_], ident], writes=[bt])
                            acopy(kTb[i_][:], bt[:, 0:128], [bt], [kTb[i_]])
                            vcopy(VSb[i_][:, 0:64], rb[:, vcol:vcol + 64], [rb], [VSb[i_]])
                            return kTb[i_], VSb[i_]

                        for g in range(4):
                            s_attend(g, lambda pr_, g=g: KcT[pr_, g, :], KcT, Vc[:, g, 0:97], 97, cmask[:], cmask, qnT, g == 0)
                        for g in range(4):
                            for hi in range(4):
                                s_combine(g, hi, 0, True, True)
                        for g in range(4):
                            tt(sc_[:], imp4[:, g, :], seltab[:, 0, 0:32], ALU.mult, [imp4, seltab], [sc_])
                            tt(sc_[:], sc_[:], seltab[:, 0, 32:64], ALU.add, [sc_, seltab], [sc_])
                            V(lambda h, G3=G3, sc_=sc_: h.tensor_tensor(out=G3[:], in0=sc_[:, :].unsqueeze(1).to_broadcast([128, 32, 32]),
                                                                        in1=sc_[:, :].unsqueeze(2).to_broadcast([128, 32, 32]), op=ALU.is_gt), reads=[sc_], writes=[G3])
                            V(lambda h, G3=G3, cnt_=cnt_: h.tensor_reduce(out=cnt_[:], in_=G3[:], axis=AX.X, op=ALU.add), reads=[G3], writes=[cnt_])
                            ts(cnt_[:], cnt_[:], 15.0, None, ALU.is_lt, None, [cnt_], [cnt_])
                            tt(sel4[:, g, :], cnt_[:], seltab[:, 0, 64:96], ALU.mult, [cnt_, seltab], [sel4])
                        for kb in range(nks):
                            rb = rowsb[kb % 2]
                            load_rows(rb[:], rb, kb, 512, 1024)
                            if kb < 16:
                                V(lambda h, kb=kb, selx4=selx4, sel4=sel4: h.tensor_copy(out=selx4[:, :, :].rearrange("p s (t d) -> p s t d", d=64),
                                                                                         in_=sel4[:, :, 2 * kb:2 * kb + 2].unsqueeze(3).to_broadcast([128, 4, 2, 64])),
                                  reads=[sel4], writes=[selx4])
                                bm = bank()
                                for g in range(4):
                                    PE(lambda h, bm=bm, g=g, selx4=selx4: h.transpose(out=bm[:, g * 128:(g + 1) * 128], in_=selx4[:, g, :], identity=ident[:]),
                                       reads=[selx4, ident], writes=[bm])
                                mk_ = mska[kb % 2]
                                acopy(mk_[:], bm[:, 0:512], [bm], [mk_])
                            for g in range(4):
                                KT_, VS_ = s_kv(rb, g, g * 64, 256 + g * 64)
                                if kb < 16:
                                    s_attend(g, lambda pr_, KT_=KT_: KT_[pr_, :], KT_, VS_[:, 0:65], 65, mk_[:, g * 128:(g + 1) * 128], mk_, qrT, kb == 0 and g == 0)
                                else:
                                    s_attend(g, lambda pr_, KT_=KT_: KT_[pr_, :], KT_, VS_[:, 0:65], 65, row0m[:], row0m, qrT, False)
                        for g in range(4):
                            for hi in range(4):
                                s_combine(g, hi, 1, False, False)
                        for i_ in range(4):
                            rb = rowsb[i_ % 2]
                            DS(lambda h, rb=rb, i_=i_, bseq=bseq: h.dma_start(out=rb[:], in_=OUT['win_s'][bseq, i_ * 128:(i_ + 1) * 128, :]), reads=[OUT['win_s']], writes=[rb])
                            for g in range(4):
                                KT_, VS_ = s_kv(rb, g, g * 64, 256 + g * 64)
                                s_attend(g, lambda pr_, KT_=KT_: KT_[pr_, :], KT_, VS_[:, 0:65], 65, None, None, qrT, i_ == 0 and g == 0)
                        for g in range(4):
                            for hi in range(4):
                                s_combine(g, hi, 2, False, False)
                        stt(onsa_tot[:, :, :], onsa4[:, :, :], ident[:, bseq:bseq + 1], onsa_tot[:, :, :], ALU.mult, ALU.add, [onsa4, ident, onsa_tot], [onsa_tot])
                    for g in (range(4) if prm else []):
                        V(lambda h, impg=impg: h.memset(impg[:], 0.0), writes=[impg])
                        attend_block(g, lambda pr_, g=g: KcT[pr_, g, :], KcT, Vc[:, g, 0:97], 97, cmask, qnT, True)
                        for hi in range(4):
                            combine(g, hi, 0, True, True)
                        for qs in range(nqs):
                            tt(sc_[:], impg[:, qs, :], seltab[:, qs, 0:32], ALU.mult, [impg, seltab], [sc_])
                            tt(sc_[:], sc_[:], seltab[:, qs, 32:64], ALU.add, [sc_, seltab], [sc_])
                            V(lambda h, G3=G3, sc_=sc_: h.tensor_tensor(out=G3[:], in0=sc_[:, :].unsqueeze(1).to_broadcast([128, 32, 32]),
                                                                        in1=sc_[:, :].unsqueeze(2).to_broadcast([128, 32, 32]), op=ALU.is_gt), reads=[sc_], writes=[G3])
                            V(lambda h, G3=G3, cnt_=cnt_: h.tensor_reduce(out=cnt_[:], in_=G3[:], axis=AX.X, op=ALU.add), reads=[G3], writes=[cnt_])
                            ts(cnt_[:], cnt_[:], 16.0 if prm else 15.0, None, ALU.is_lt, None, [cnt_], [cnt_])
                            tt(sel_all[:, qs, :], cnt_[:], seltab[:, qs, 64:96], ALU.mult, [cnt_, seltab], [sel_all])
                        for kb in range(nks):
                            def ld(kt, vt, kb=kb, g=g):
                                load_rows(kt[:], kt, kb, 512 + g * 64, 576 + g * 64)
                                load_rows(vt[:], vt, kb, 768 + g * 64, 832 + g * 64)
                            KT_, VS_ = load_kv_block(ld)
                            if kb < 16:
                                V(lambda h, kb=kb, selx=selx, sel_all=sel_all: h.tensor_copy(out=selx[:, :, :].rearrange("p s (t d) -> p s t d", d=64),
                                                                                             in_=sel_all[:, :, 2 * kb:2 * kb + 2].unsqueeze(3).to_broadcast([128, nqs, 2, 64])),
                                  reads=[sel_all], writes=[selx])
                                bm = bank()
                                for qs in range(nqs):
                                    PE(lambda h, bm=bm, qs=qs, selx=selx: h.transpose(out=bm[:, qs * 128:(qs + 1) * 128], in_=selx[:, qs, :], identity=ident[:]),
                                       reads=[selx, ident], writes=[bm])
                                mk_ = mskb[kb % 2]
                                if prm and kb >= 4 * jt:
                                    i_ = kb - 4 * jt
                                    wm_ = wmk[kb % 2]
                                    G(lambda h, wm_=wm_, i_=i_: h.affine_select(out=wm_[:], in_=ones_bf[:], pattern=[[1, NQ]], compare_op=ALU.is_ge, fill=0.0,
                                                                                base=-128 * i_, channel_multiplier=-1), reads=[ones_bf], writes=[wm_])
                                    tt(mk_[:], bm[:, 0:NQ], wm_[:], ALU.mult, [bm, wm_], [mk_])
                                else:
                                    acopy(mk_[:], bm[:, 0:NQ], [bm], [mk_])
                            else:
                                mk_ = row0m
                            attend_block(g, lambda pr_, KT_=KT_: KT_[pr_, :], KT_, VS_[:, 0:65], 65, mk_, qrT, kb == 0)
                        for hi in range(4):
                            combine(g, hi, 1, False, False)
                        if prm:
                            wlist = [i_ for i_ in range(8) if 4 * jt - 4 + i_ >= 0]
                        else:
                            wlist = [0, 1, 2, 3]
                        for i_ in wlist:
                            if prm:
                                kbw = 4 * jt - 4 + i_

                                def ldw(kt, vt, kbw=kbw, g=g):
                                    DS(lambda h: h.dma_start(out=kt[:], in_=WIN[kbw * 128:(kbw + 1) * 128, g * 64:g * 64 + 64]), reads=[WIN], writes=[kt])
                                    DS(lambda h: h.dma_start(out=vt[:], in_=WIN[kbw * 128:(kbw + 1) * 128, 256 + g * 64:320 + g * 64]), reads=[WIN], writes=[vt])
                                wm_ = wmk[i_ % 2]
                                if i_ < 4:
                                    G(lambda h, wm_=wm_, i_=i_: h.affine_select(out=wm_[:], in_=ones_bf[:], pattern=[[-1, NQ]], compare_op=ALU.is_ge, fill=0.0,
                                                                                base=128 * i_ - 1, channel_multiplier=1), reads=[ones_bf], writes=[wm_])
                                else:
                                    G(lambda h, wm_=wm_, i_=i_: h.affine_select(out=wm_[:], in_=ones_bf[:], pattern=[[1, NQ]], compare_op=ALU.is_ge, fill=0.0,
                                                                                base=512 - 128 * i_, channel_multiplier=-1), reads=[ones_bf], writes=[wm_])
                            else:
                                def ldw(kt, vt, i_=i_, g=g, bseq=bseq):
                                    DS(lambda h: h.dma_start(out=kt[:], in_=OUT['win_s'][bseq, i_ * 128:(i_ + 1) * 128, g * 64:g * 64 + 64]), reads=[OUT['win_s']], writes=[kt])
                                    DS(lambda h: h.dma_start(out=vt[:], in_=OUT['win_s'][bseq, i_ * 128:(i_ + 1) * 128, 256 + g * 64:320 + g * 64]), reads=[OUT['win_s']], writes=[vt])
                                wm_ = None
                            KT_, VS_ = load_kv_block(ldw)
                            attend_block(g, lambda pr_, KT_=KT_: KT_[pr_, :], KT_, VS_[:, 0:65], 65, wm_, qrT, i_ == wlist[0])
                        for hi in range(4):
                            combine(g, hi, 2, False, False)
                        if prm:
                            for qs in range(nqs):
                                bt = bank()
                                for c2 in range(2):
                                    PE(lambda h, bt=bt, c2=c2, qs=qs, onsa=onsa: h.transpose(out=bt[:, c2 * 128:(c2 + 1) * 128], in_=onsa[:, qs, c2 * 128:(c2 + 1) * 128], identity=ident[:]),
                                       reads=[onsa, ident], writes=[bt])
                                vcopy(oT[:, 2 * g:2 * g + 2, qs * 128:(qs + 1) * 128], bt[:, 0:256].rearrange("p (c n) -> p c n", n=128), [bt], [oT])
                        else:
                            stt(onsa_tot[:, g, :], onsa[:, 0, :], ident[:, bseq:bseq + 1], onsa_tot[:, g, :], ALU.mult, ALU.add, [onsa, ident, onsa_tot], [onsa_tot])
                    fw.barrier()
                    es_a.close()
                bank_pool[:] = saved_pool
                if not prm:
                    for g in range(4):
                        bt = bank()
                        for c2 in range(2):
                            PE(lambda h, bt=bt, c2=c2, g=g: h.transpose(out=bt[:, c2 * 128:c2 * 128 + NSMP], in_=onsa_tot[0:NSMP, g, c2 * 128:(c2 + 1) * 128], identity=ident[0:NSMP, 0:NSMP]),
                               reads=[onsa_tot, ident], writes=[bt])
                        vcopy(oT[:, 2 * g:2 * g + 2, 0:NSMP], bt[:, 0:256].rearrange("p (c n) -> p c n", n=128)[:, :, 0:NSMP], [bt], [oT])
                    fw.barrier()
                es_n.close()
            if prm and last:
                for h8 in range(8):
                    hp, hh = h8 // 2, h8 % 2
                    DS(lambda h, h8=h8, hp=hp, hh=hh: h.dma_start(out=OUT['ret_p'][h8], in_=S2[64 * hh:64 * hh + 64, hp, :]), reads=[S2], writes=[OUT['ret_p']])
            fw.barrier()
            Wo = IN['cd_w_out'][0]
            for db in range(KC):
                bk = proj_fm(Wo, db * 128, 128, N, src=oT)
                tt(xT[:, db, :N], xT[:, db, :N], bk[:, :N], ALU.add, [xT, bk], [xT])
            fw.barrier()

    tiles = [("p", j) for j in range(NTILE)] + [("s", 0)]
    if stage in (0, 1, 2):
        tiles = [("p", 0), ("s", 0)]
    for kind, j in tiles:
        if kind == "p":
            N = TT
            load_xT(IN['xp'][j * TT:(j + 1) * TT, :], N)
        else:
            N = NSMP
            load_xT(IN['xs'][:, :], N)
        last = (kind == "s") or (j == NTILE - 1) or stage in (0, 1, 2)
        ffn(N, 0, 1)
        if stage >= 1:
            even_mixer(kind, N, last)
        if stage >= 2:
            import os as _os
            if _os.environ.get("DBG_FFNMID", "1") == "1":
                ffn(N, 0, 2)
                ffn(N, 1, 1)
            if _os.environ.get("DBG_ODD_" + kind.upper(), "1") == "1":
                odd_mixer(kind, N, j, last)
        if stage >= 3:
            ffn(N, 1, 2)
        if kind == "p":
            store_xT(OUT['yp'][j * TT:(j + 1) * TT, :], OUT['yp'], N)
        else:
            store_xT(OUT['ys'][:, :], OUT['ys'], N)
    counts = fw.finish()
    return nc, counts


OUT_ORDER = ['yp', 'ys', 'lru_h_p', 'lru_h_s', 'lru_conv_p', 'lru_conv_s', 'shift_p', 'shift_s', 'wkv_p', 'wkv_s',
             'kv_p', 'kv_s', 'win_p', 'win_s', 'ret_p', 'ret_s']


def make_in_maps(inp, cores):
    maps = []
    consts = make_consts()
    f = lambda a: np.ascontiguousarray(np.asarray(a, dtype=np.float32))
    for c in cores:
        b = c % 4
        s0 = c * NSMP
        m = {
            'xp': f(inp['x_prompt'][b]),
            'xs': f(inp['x_sample'][s0:s0 + NSMP, 0]),
            's_lru_h': f(inp['state_lru_h'][0, s0:s0 + NSMP]),
            's_lru_conv': f(inp['state_lru_conv'][0, s0:s0 + NSMP]).reshape(NSMP * 3, 1024),
            's_shift': f(inp['state_rwkv_shift'][0, s0:s0 + NSMP]),
            's_wkv': f(inp['state_rwkv_wkv'][0, s0:s0 + NSMP]),
            'cache_kv': f(inp['cache_nsa_kv'][0]).reshape(NPOOL * 256, 512),
            'cache_win': f(inp['cache_nsa_win'][0, s0:s0 + NSMP]).reshape(NSMP, 512, 512),
            's_ret': f(inp['state_ret'][0, s0:s0 + NSMP]),
            'page_table': np.ascontiguousarray(np.asarray(inp['page_table'][s0:s0 + NSMP], dtype=np.int32)),
        }
        for k in WEIGHT_NAMES:
            m[k] = f(inp[k])
        m.update(consts)
        maps.append(m)
    return maps


_CACHE = {}


def kernel(**inputs):
    if 'nc' not in _CACHE:
        _CACHE['nc'] = build()[0]
    nc = _CACHE['nc']
    cores = list(range(8))
    maps = make_in_maps(inputs, cores)
    res = run_bass_kernel_spmd(nc, maps, core_ids=cores)
    R = res.results
    B = 4
    o = {}
    o['yp'] = np.stack([R[b]['yp'] for b in range(B)])
    o['ys'] = np.concatenate([R[c]['ys'] for c in cores])[:, None, :]
    o['lru_h_p'] = np.stack([R[b]['lru_h_p'][0] for b in range(B)])[None]
    o['lru_h_s'] = np.concatenate([R[c]['lru_h_s'] for c in cores])[None]
    o['lru_conv_p'] = np.stack([R[b]['lru_conv_p'] for b in range(B)])[None]
    o['lru_conv_s'] = np.concatenate([R[c]['lru_conv_s'].reshape(NSMP, 3, 1024) for c in cores])[None]
    o['shift_p'] = np.stack([R[b]['shift_p'][0] for b in range(B)])[None]
    o['shift_s'] = np.concatenate([R[c]['shift_s'] for c in cores])[None]
    o['wkv_p'] = np.stack([R[b]['wkv_p'] for b in range(B)])[None]
    o['wkv_s'] = np.concatenate([R[c]['wkv_s'] for c in cores])[None]
    o['kv_p'] = np.stack([R[b]['kv_p'].reshape(T, 4, 4, 64) for b in range(B)])[None]
    o['kv_s'] = np.concatenate([R[c]['kv_s'].reshape(NSMP, 1, 4, 4, 64) for c in cores])[None]
    o['win_p'] = np.stack([R[b]['win_p'].reshape(512, 2, 4, 64) for b in range(B)])[None]
    o['win_s'] = np.concatenate([R[c]['win_s'].reshape(NSMP, 512, 2, 4, 64) for c in cores])[None]
    o['ret_p'] = np.stack([R[b]['ret_p'] for b in range(B)])[None]
    o['ret_s'] = np.concatenate([R[c]['ret_s'] for c in cores])[None]
    return tuple(np.ascontiguousarray(o[k], dtype=np.float32) for k in OUT_ORDER)
```
